# Optimizing a Trainium2 kernel written in Bass

```python
import math
import jax, jax.numpy as jnp
from jax import lax
import numpy as np

D_MODEL = 1024
BATCH = 8
SEQ = 4096
DEPTH = 1

N_META = 16
HEAD_DIM = 64
ATT_Q_HEADS = D_MODEL // HEAD_DIM
ATT_KV_HEADS = max(1, ATT_Q_HEADS // 8)
ATT_GROUP = ATT_Q_HEADS // ATT_KV_HEADS
WINDOW = 128
BLOCK = 128
N_BUCKETS = 32
MAX_EXACT = N_BUCKETS // 2
MAX_DISTANCE = 128
RWKV_HEADS = D_MODEL // HEAD_DIM
RWKV_WIDTH = RWKV_HEADS * HEAD_DIM
DECAY_LORA = 64
ICLR_LORA = 64
GATE_LORA = 128
D_FF = 4 * D_MODEL
LN_EPS = 1e-5
GN_EPS = 1e-5 * HEAD_DIM
ALPHA = (2.0 * DEPTH) ** 0.25
BETA = (8.0 * DEPTH) ** -0.25
ATT_Q_COLS = ATT_Q_HEADS * HEAD_DIM
ATT_KV_COLS = ATT_KV_HEADS * HEAD_DIM
RWKV_COLS = 3 * RWKV_WIDTH + DECAY_LORA + ICLR_LORA + GATE_LORA
GATE_COLS = 2 * D_MODEL
IN_COLS = ATT_Q_COLS + 2 * ATT_KV_COLS + RWKV_COLS + GATE_COLS

kernel_name = 'hybrid_swa_sink_rwkv7_gated_deepnorm'


def layer_norm(x, g, b):
    xf = x.astype(jnp.float32)
    mu = jnp.mean(xf, -1, keepdims=True)
    var = jnp.mean(jnp.square(xf - mu), -1, keepdims=True)
    y = (xf - mu) * lax.rsqrt(var + LN_EPS)
    return (y * g.astype(jnp.float32) + b.astype(jnp.float32)).astype(x.dtype)


def t5_bucket(dist):
    small = dist < MAX_EXACT
    d = jnp.maximum(dist, 1).astype(jnp.float32)
    large = MAX_EXACT + (jnp.log(d / MAX_EXACT) / math.log(MAX_DISTANCE / MAX_EXACT)
                         * (N_BUCKETS - MAX_EXACT)).astype(jnp.int32)
    large = jnp.minimum(large, N_BUCKETS - 1)
    return jnp.where(small, dist, large)


def swa_one(q, k, v, rel_bias, sinks):
    T = q.shape[0]
    pad = (-T) % BLOCK
    nb = (T + pad) // BLOCK
    qb = jnp.pad(q, ((pad, 0), (0, 0), (0, 0), (0, 0))).reshape(nb, BLOCK, ATT_KV_HEADS, ATT_GROUP, HEAD_DIM)

    def windows(z):
        zb = jnp.pad(z, ((pad + BLOCK, 0), (0, 0), (0, 0))).reshape(nb + 1, BLOCK, ATT_KV_HEADS, HEAD_DIM)
        return jnp.concatenate([zb[:-1], zb[1:]], axis=1)

    kw, vw = windows(k), windows(v)
    logits = jnp.einsum('nqkgd,nskd->kgnqs', qb, kw).astype(jnp.float32) * (HEAD_DIM ** -0.5)
    dist = jnp.arange(BLOCK)[:, None] + BLOCK - jnp.arange(2 * BLOCK)[None, :]
    in_window = (dist >= 0) & (dist < WINDOW)
    bias = rel_bias[t5_bucket(jnp.maximum(dist, 0))].astype(jnp.float32)
    bias = jnp.transpose(bias, (2, 0, 1)).reshape(ATT_KV_HEADS, ATT_GROUP, 1, BLOCK, 2 * BLOCK)
    key_idx = jnp.arange(nb)[:, None] * BLOCK + jnp.arange(2 * BLOCK)[None, :]
    key_valid = key_idx >= pad + BLOCK
    mask = in_window[None] & key_valid[:, None, :]
    logits = jnp.where(mask, logits + bias, -jnp.inf)
    sink = sinks.astype(jnp.float32).reshape(ATT_KV_HEADS, ATT_GROUP, 1, 1, 1)
    m = jnp.maximum(jnp.max(logits, -1, keepdims=True), sink)
    p = jnp.exp(logits - m)
    denom = jnp.sum(p, -1, keepdims=True) + jnp.exp(sink - m)
    probs = (p / denom).astype(v.dtype)
    out = jnp.einsum('kgnqs,nskd->nqkgd', probs, vw)
    return out.reshape(nb * BLOCK, ATT_Q_COLS)[pad:]


def token_shift(z, mu):
    prev = jnp.pad(z, ((0, 0), (1, 0), (0, 0)))[:, :-1]
    return z + (prev - z) * mu


def rwkv7_time_mix(z, mu, w0, w2, a0, a2, g2, k_k, k_a, r_k, lnx_g, lnx_b):
    f32 = jnp.float32
    B, T = z.shape[0], z.shape[1]
    H, N = RWKV_HEADS, HEAD_DIM
    z = token_shift(z, mu)
    W = RWKV_WIDTH
    r, k, v, wl, al, gl = jnp.split(z, [W, 2 * W, 3 * W, 3 * W + DECAY_LORA, 3 * W + DECAY_LORA + ICLR_LORA], axis=-1)
    w = -jax.nn.softplus(-(w0 + jnp.tanh(wl) @ w2)) - 0.5
    decay = jnp.exp(-jnp.exp(w.astype(f32)))
    a = jax.nn.sigmoid(a0 + al @ a2)
    g = jax.nn.sigmoid(gl) @ g2
    heads = lambda t: t.astype(f32).reshape(B, T, H, N)
    rh, kh, vh, wh, ah = heads(r), heads(k), heads(v), heads(decay), heads(a)
    kk = kh * k_k.astype(f32).reshape(H, N)
    kk = kk / jnp.maximum(jnp.sqrt(jnp.sum(kk * kk, -1, keepdims=True)), 1e-12)
    kh = kh * (1.0 + (ah - 1.0) * k_a.astype(f32).reshape(H, N))

    def step(S, inp):
        r_t, w_t, k_t, v_t, kk_t, a_t = inp
        sa = jnp.einsum('bhij,bhj->bhi', S, -kk_t)
        S = S * w_t[:, :, None, :] + sa[..., None] * (kk_t * a_t)[:, :, None, :] + v_t[..., None] * k_t[:, :, None, :]
        return S, jnp.einsum('bhij,bhj->bhi', S, r_t)

    tm = lambda t: jnp.swapaxes(t, 0, 1)
    S0 = jnp.zeros((B, H, N, N), f32)
    _, y = lax.scan(step, S0, (tm(rh), tm(wh), tm(kh), tm(vh), tm(kk), tm(ah)))
    y = tm(y)
    ym = jnp.mean(y, -1, keepdims=True)
    yv = jnp.mean(jnp.square(y - ym), -1, keepdims=True)
    y = ((y - ym) * lax.rsqrt(yv + GN_EPS)).reshape(B, T, W) * lnx_g.astype(f32) + lnx_b.astype(f32)
    bonus = jnp.sum(rh * kh * r_k.astype(f32), -1, keepdims=True) * vh
    y = (y + bonus.reshape(B, T, W)) * g.astype(f32)
    return y.astype(z.dtype)


def setup_inputs(seed: int = 0) -> dict:
    key = jax.random.key(seed)
    ks = jax.random.split(key, 26)
    f32 = jnp.float32
    nrm = lambda k, shape, s: s * jax.random.normal(k, shape, f32)
    L = DEPTH
    decay_base = jnp.tile(jnp.linspace(-6.0, -1.0, HEAD_DIM), RWKV_HEADS)
    return {
        'x': nrm(ks[0], (BATCH, SEQ, D_MODEL), 1.0),
        'meta_tokens': nrm(ks[1], (N_META, D_MODEL), 1.0),
        'ln0_g': 1.0 + nrm(ks[2], (D_MODEL,), 0.02),
        'ln0_b': nrm(ks[3], (D_MODEL,), 0.02),
        'rel_bias': nrm(ks[4], (N_BUCKETS, ATT_Q_HEADS), 0.5),
        'w_in': nrm(ks[5], (L, D_MODEL, IN_COLS), D_MODEL ** -0.5),
        'shift_mu': jax.random.uniform(ks[6], (L, RWKV_COLS), f32),
        'attn_sinks': nrm(ks[7], (L, ATT_Q_HEADS), 0.5),
        'decay_w0': decay_base + nrm(ks[8], (L, RWKV_WIDTH), 0.1),
        'decay_w2': nrm(ks[9], (L, DECAY_LORA, RWKV_WIDTH), 0.5 * DECAY_LORA ** -0.5),
        'iclr_a0': nrm(ks[10], (L, RWKV_WIDTH), 0.3),
        'iclr_a2': nrm(ks[11], (L, ICLR_LORA, RWKV_WIDTH), ICLR_LORA ** -0.5),
        'gate_w2': nrm(ks[12], (L, GATE_LORA, RWKV_WIDTH), GATE_LORA ** -0.5),
        'k_k': 0.85 + nrm(ks[13], (L, RWKV_WIDTH), 0.05),
        'k_a': 1.0 + nrm(ks[14], (L, RWKV_WIDTH), 0.05),
        'r_k': nrm(ks[15], (L, RWKV_HEADS, HEAD_DIM), 0.1),
        'lnx_g': 1.0 + nrm(ks[16], (L, RWKV_WIDTH), 0.02),
        'lnx_b': nrm(ks[17], (L, RWKV_WIDTH), 0.02),
        'w_out': nrm(ks[18], (L, D_MODEL, D_MODEL), BETA * D_MODEL ** -0.5),
        'ln1_g': 1.0 + nrm(ks[19], (L, D_MODEL), 0.02),
        'ln1_b': nrm(ks[20], (L, D_MODEL), 0.02),
        'w_ff1': nrm(ks[21], (L, D_MODEL, D_FF), D_MODEL ** -0.5),
        'w_ff2': nrm(ks[22], (L, D_FF, D_MODEL), BETA * D_FF ** -0.5),
        'ln2_g': 1.0 + nrm(ks[23], (L, D_MODEL), 0.02),
        'ln2_b': nrm(ks[24], (L, D_MODEL), 0.02),
    }


def reference(x, meta_tokens, ln0_g, ln0_b, rel_bias, w_in, shift_mu, attn_sinks, decay_w0, decay_w2,
              iclr_a0, iclr_a2, gate_w2, k_k, k_a, r_k, lnx_g, lnx_b, w_out, ln1_g, ln1_b,
              w_ff1, w_ff2, ln2_g, ln2_b):
    B = x.shape[0]
    meta = jnp.broadcast_to(meta_tokens[None].astype(x.dtype), (B, N_META, D_MODEL))
    h = layer_norm(jnp.concatenate([meta, x], axis=1), ln0_g, ln0_b)
    T = h.shape[1]
    c0 = ATT_Q_COLS
    c1 = c0 + ATT_KV_COLS
    c2 = c1 + ATT_KV_COLS
    c3 = c2 + RWKV_COLS
    c4 = c3 + D_MODEL
    for l in range(DEPTH):
        proj = h @ w_in[l]
        q, k, v, zr, gate_att, gate_rwkv = jnp.split(proj, [c0, c1, c2, c3, c4], axis=-1)
        q = q.reshape(B, T, ATT_KV_HEADS, ATT_GROUP, HEAD_DIM)
        k = k.reshape(B, T, ATT_KV_HEADS, HEAD_DIM)
        v = v.reshape(B, T, ATT_KV_HEADS, HEAD_DIM)
        sinks_l = attn_sinks[l]
        att = lax.map(lambda qkv: swa_one(qkv[0], qkv[1], qkv[2], rel_bias, sinks_l), (q, k, v))
        rw = rwkv7_time_mix(zr, shift_mu[l], decay_w0[l], decay_w2[l], iclr_a0[l], iclr_a2[l], gate_w2[l],
                            k_k[l], k_a[l], r_k[l], lnx_g[l], lnx_b[l])
        merged = jax.nn.sigmoid(gate_att) * att + jax.nn.sigmoid(gate_rwkv) * rw
        h = layer_norm(ALPHA * h + merged @ w_out[l], ln1_g[l], ln1_b[l])
        ff = jnp.square(jax.nn.relu(h @ w_ff1[l])) @ w_ff2[l]
        h = layer_norm(ALPHA * h + ff, ln2_g[l], ln2_b[l])
    return h[:, N_META:]
```

```python
import math
from contextlib import ExitStack

import numpy as np
import concourse.bass as bass
import concourse.mybir as mybir
from concourse.bass_utils import run_bass_kernel_spmd

F32 = mybir.dt.float32
BF16 = mybir.dt.bfloat16
AF = mybir.ActivationFunctionType
ALU = mybir.AluOpType

ENGS = ['pe', 'act', 'dve', 'pool', 'sp']
BLOCKNAME = {'pe': 'tensor', 'act': 'scalar', 'dve': 'vector', 'pool': 'gpsimd', 'sp': 'sync'}
EPOCH = 16000
NDMA = 24

NT = 33
D = 1024
NCOL = 6656
ALPHA = 2.0 ** 0.25
LN_EPS = 1e-5
GN_EPS = 1e-5 * 64
DK = 0.6065306597126334
NEG = -30000.0
NP = 107


class Res:
    __slots__ = ('name', 'lw', 'rd')

    def __init__(self, name):
        self.name = name
        self.lw = None
        self.rd = []


class Prog:
    def __init__(self, nc, stack):
        self.nc = nc
        self.stack = stack
        self.q = {e: [] for e in ENGS}
        self.cnt = {e: 0 for e in ENGS}
        self.waited = {e: {} for e in ENGS}
        self.dma_i = 0
        self.dsem = [stack.enter_context(nc.semaphore(f"dsem{i}")) for i in range(NDMA)]
        self.esem = {e: [] for e in ENGS}
        self.out_toks = []

    def _sem(self, e, epoch):
        while len(self.esem[e]) <= epoch:
            self.esem[e].append(self.stack.enter_context(self.nc.semaphore(f"es_{e}_{len(self.esem[e])}")))
        return self.esem[e][epoch]

    def _need(self, eng, waits, tok, war=False):
        if tok is None:
            return
        if tok[0] == 'dma':
            key = ('dma', tok[1]); val = tok[2]
        else:
            peng, idx = tok
            if peng == eng:
                if war or eng == 'pe':
                    return
                if idx <= self.cnt[eng] - 2:
                    return
            key = peng; val = idx
        if self.waited[eng].get(key, 0) >= val:
            return
        if waits.get(key, 0) < val:
            waits[key] = val

    def _deps(self, eng, reads, writes, waits):
        for r in reads:
            self._need(eng, waits, r.lw)
        for w in writes:
            self._need(eng, waits, w.lw)
            for t in w.rd:
                self._need(eng, waits, t, war=True)
        for k, v in waits.items():
            self.waited[eng][k] = v

    def _mark(self, tok, reads, writes):
        for r in reads:
            r.rd.append(tok)
        for w in writes:
            w.lw = tok
            w.rd = []

    maxops = None
    allsig = False
    nops = 0

    def op(self, eng, fn, reads=(), writes=(), sig=True):
        self.nops += 1
        if self.maxops is not None and self.nops > self.maxops:
            return None
        if self.allsig:
            sig = True
        waits = {}
        self._deps(eng, reads, writes, waits)
        idx = self.cnt[eng] + 1
        if sig:
            self.cnt[eng] = idx
        tok = (eng, idx)
        self.q[eng].append((fn, list(waits.items()), sig, None))
        self._mark(tok, reads, writes)
        return tok

    def dma(self, eng, out_ap, in_ap, reads=(), writes=(), is_output=False, **kw):
        i = self.dma_i
        self.dma_i += 1
        s = i % NDMA
        val = 16 * (i // NDMA + 1)
        waits = {}
        if val > 16:
            self._need(eng, waits, ('dma', s, val - 16))
        self._deps(eng, reads, writes, waits)
        tok = ('dma', s, val)
        fn = lambda e: e.dma_start(out=out_ap, in_=in_ap, **kw)
        self.q[eng].append((fn, list(waits.items()), False, s))
        self._mark(tok, reads, writes)
        if is_output:
            self.out_toks.append(tok)
        return tok

    def finish(self):
        waits = {}
        for t in self.out_toks:
            self._need('sp', waits, t)
        self.q['sp'].append((None, list(waits.items()), False, None))

    def barrier(self):
        snap = dict(self.cnt)
        for e in ENGS:
            waits = {}
            for o in ENGS:
                if o != e and snap[o] > 0:
                    self._need(e, waits, (o, snap[o]))
            for k, v in waits.items():
                self.waited[e][k] = v
            self.q[e].append((None, list(waits.items()), False, None))

    def emit(self):
        nc = self.nc
        for e in ENGS:
            n = self.cnt[e]
            if n:
                self._sem(e, (n - 1) // EPOCH)
        with nc.Block() as block:
            for e in ENGS:
                if self.q[e]:
                    self._emit_engine(block, e)

    def _emit_engine(self, block, e):
        q = self.q[e]
        prog = self

        def body(eng):
            cnt = 0
            for fn, waits, sig, dsem in q:
                for key, val in waits:
                    if isinstance(key, tuple):
                        eng.wait_ge(prog.dsem[key[1]], val)
                    else:
                        eng.wait_ge(prog._sem(key, (val - 1) // EPOCH), (val - 1) % EPOCH + 1)
                if fn is None:
                    continue
                ins = fn(eng)
                if dsem is not None:
                    ins.then_inc(prog.dsem[dsem], 16)
                elif sig:
                    cnt += 1
                    ins.then_inc(prog._sem(e, (cnt - 1) // EPOCH), 1)
        getattr(block, BLOCKNAME[e])(body)

    def mm(self, out, lhsT, rhs, start, stop, reads, writes, sig=True):
        return self.op('pe', lambda e: e.matmul(out, lhsT=lhsT, rhs=rhs, start=start, stop=stop), reads, writes, sig)

    def tr(self, out, in_, ident, reads, writes, sig=True):
        return self.op('pe', lambda e: e.transpose(out, in_, ident), reads, writes, sig)

    def act(self, out, in_, func, reads, writes, bias=None, scale=None):
        kw = {}
        if bias is not None:
            kw['bias'] = bias
        if scale is not None:
            kw['scale'] = scale
        return self.op('act', lambda e: e.activation(out=out, in_=in_, func=func, **kw), reads, writes)

    def tt(self, eng, out, in0, in1, op, reads, writes):
        return self.op(eng, lambda e: e.tensor_tensor(out=out, in0=in0, in1=in1, op=op), reads, writes)

    def ts(self, eng, out, in0, s1, s2, op0, op1, reads, writes):
        if op1 is None:
            return self.op(eng, lambda e: e.tensor_scalar(out=out, in0=in0, scalar1=s1, scalar2=None, op0=op0), reads, writes)
        return self.op(eng, lambda e: e.tensor_scalar(out=out, in0=in0, scalar1=s1, scalar2=s2, op0=op0, op1=op1), reads, writes)

    def stt(self, out, in0, scalar, in1, op0, op1, reads, writes):
        return self.op('dve', lambda e: e.scalar_tensor_tensor(out=out, in0=in0, scalar=scalar, in1=in1, op0=op0, op1=op1), reads, writes)

    def cp(self, eng, out, in_, reads, writes):
        if eng == 'act':
            return self.op('act', lambda e: e.copy(out=out, in_=in_), reads, writes)
        return self.op(eng, lambda e: e.tensor_copy(out=out, in_=in_), reads, writes)

    def memset(self, eng, ap, val, writes):
        return self.op(eng, lambda e: e.memset(ap, val), (), writes)


def build_nc(cfg=None):
    cfg = cfg or {}
    nt = cfg.get('nt', NT)
    nc = bass.Bass("TRN2", target_bir_lowering=False, dynamic_dma_scratch_size=4096)
    dt = nc.dram_tensor
    x_d = dt("x", [4096, D], F32, kind="ExternalInput").ap()
    meta_d = dt("meta", [16, D], F32, kind="ExternalInput").ap()
    wcat_d = dt("wcat", [D, NCOL], F32, kind="ExternalInput").ap()
    wout_d = dt("wout", [D, D], F32, kind="ExternalInput").ap()
    wlora_d = dt("wlora", [128, D], F32, kind="ExternalInput").ap()
    wg2_d = dt("wg2", [128, D], F32, kind="ExternalInput").ap()
    wff1_d = dt("wff1", [D, 4096], F32, kind="ExternalInput").ap()
    wff2_d = dt("wff2", [4096, D], F32, kind="ExternalInput").ap()
    lnbc_d = dt("lnbc", [128, 6, D], F32, kind="ExternalInput").ap()
    pp_d = dt("pp", [128, NP], F32, kind="ExternalInput").ap()
    bias_d = dt("biasT", [128, 8, 512], F32, kind="ExternalInput").ap()
    cst_d = dt("cst", [128, 6, 128], F32, kind="ExternalInput").ap()
    out_d = dt("out", [4096, D], F32, kind="ExternalOutput").ap()
    h1_d = dt("h1s", [4096, D], F32, kind="Internal").ap()

    with ExitStack() as st:
        P = Prog(nc, st)
        P.maxops = cfg.get('maxops')
        P.allsig = cfg.get('allsig', False)

        ARENA = 222000
        arena = st.enter_context(nc.sbuf_tensor("arena", [128, ARENA // 2], BF16))
        bump = [0]

        class _T:
            def __init__(self, ap):
                self.ap = ap

            def __getitem__(self, k):
                return self.ap[k]

        def SB(name, shape, dtype):
            n = 1
            for s_ in shape[1:]:
                n *= s_
            nbytes = n * (4 if dtype == F32 else 2)
            nbytes = (nbytes + 31) // 32 * 32
            o = bump[0]
            bump[0] += nbytes
            assert bump[0] <= ARENA, (name, bump[0])
            v = arena[:, o // 2:(o + nbytes) // 2]
            if dtype == F32:
                v = v.bitcast(F32)
            v = v[0:shape[0], 0:n]
            if len(shape) == 3:
                v = v.rearrange("p (a b) -> p a b", a=shape[1])
            elif len(shape) == 4:
                v = v.rearrange("p (a b c) -> p a b c", a=shape[1], b=shape[2])
            return _T(v)

        def PS(name):
            return st.enter_context(nc.psum_tensor(name, [128, 512], F32))

        def same2(t):
            return [t, t]
        wbig = SB("wbig", [128, 65536], BF16)
        Rwk = [Res(f"wk{k}") for k in range(8)]; Rwo = Res("wo"); Rf2 = [Res(f"f2_{k}") for k in range(16)]
        w_in = wbig[:, 0:8 * NCOL].rearrange("p (k c) -> p k c", k=8)
        w_out = wbig[:, 8 * NCOL:8 * NCOL + 8192].rearrange("p (k c) -> p k c", k=8)
        wlora = _T(wbig[:, 61440:62464])
        wg2 = _T(wbig[:, 62464:63488])
        w_ff1 = wbig[:, 0:32768].rearrange("p (k c) -> p k c", k=8)
        w_ff2 = wbig[:, 32768:65536].rearrange("p (k c) -> p k c", k=32)
        Rwl = Res("wl")
        lnbc = SB("lnA", [128, 2, D], F32); Rlnbc = Res("lnbc")
        pp = SB("pp_sb", [128, NP], F32); Rpp = Res("pp")
        cst = SB("cst_sb", [128, 6, 128], F32); Rcst = Res("cst")
        identb = SB("identb", [128, 128], BF16)
        onesb = SB("onesb", [128, 128], BF16)
        blkb = SB("blkb", [128, 128], BF16)
        sinkexp = SB("sinkexp", [128, 8], F32)
        epsb = SB("epsb", [128, 4], F32); Reps = Res("eps")
        ident_f = cst[:, 0, :]
        mask2 = cst[:, 1:3, :]
        mask_ls = cst[:, 3, :]
        ones_f = cst[:, 5, :]
        stt6 = SB("stt6", [128, 2, 6], F32); Rst = Res("st")
        mv = SB("mv", [128, 2], F32); Rmv = Res("mv")
        rstd = SB("rstd", [128, 1], F32); Rrstd = Res("rstd")
        xt = same2(SB("xt", [128, D], F32)); Rxt = same2(Res("xt"))
        pre = xt[0]; Rpre = Rxt[0]
        xnb = SB("xnb", [128, D], BF16); Rxnb = Res("xnb")
        hres = [SB(f"hres{j}", [128, D], F32) for j in range(2)]; Rhres = [Res(f"hres{j}") for j in range(2)]
        hT = same2(SB("hT", [128, 8, 128], BF16)); RhT = same2(Res("hT"))
        mark = bump[0]
        biasT = SB("bias_sb", [128, 8, 512], BF16); Rbias = Res("bias")
        kT = [SB(f"kT{j}", [128, 128], BF16) for j in range(3)]; RkT = [Res(f"kT{j}") for j in range(3)]
        vat = [SB(f"vat{j}", [128, 128], BF16) for j in range(3)]; Rvat = [Res(f"vat{j}") for j in range(3)]
        qT = same2(SB("qT", [128, 8, 2, 128], BF16)); RqT = same2([Res(f"qT_{i}") for i in range(8)])
        gat = same2(SB("ga", [128, 8, 128], BF16)); Rga = same2([Res(f"ga_{i}") for i in range(8)])
        grw = same2(SB("gr", [128, 8, 128], BF16)); Rgr = same2([Res(f"gr_{i}") for i in range(8)])
        rkv = SB("rkv", [128, 8, 3, 128], F32); Rrkv = [Res(f"rkv{i}") for i in range(8)]
        car = SB("car", [128, 2, 26], F32); Rcar = Res("car")
        zt = [SB(f"zt{j}", [128, 3, 129], F32) for j in range(2)]; Rzt = [Res(f"zt{j}") for j in range(2)]
        zd = [SB(f"zd{j}", [128, 3, 128], F32) for j in range(1)]; Rzd = [Res(f"zd{j}") for j in range(1)]
        zAG = SB("zAG", [128, 2, 128], F32); RzAG = Res("zAG")
        lor = [SB(f"lor{j}", [128, 3, 128], BF16) for j in range(2)]; Rlor = [Res(f"lor{j}") for j in range(2)]
        merged = SB("merged", [128, 8, 128], BF16); Rmg = [Res(f"mg{i}") for i in range(8)]

        def dbl(name, shape, dtype):
            return same2(SB(name, shape, dtype)), same2(Res(name))
        sig_, Rsig = dbl("sig", [128, 128], F32)
        aa_, Raa = dbl("aa", [128, 128], F32)
        gg_, Rgg = dbl("gg", [128, 128], F32)
        cum_, Rcum = dbl("cum", [128, 128], F32)
        epos_, Repos = dbl("epos", [128, 128], F32)
        eneg_, Reneg = dbl("eneg", [128, 128], F32)
        eprv_, Reprv = dbl("eprv", [128, 128], F32)
        eend_, Reend = dbl("eend", [128, 128], F32)
        nb_, Rnb = dbl("nb", [128, 2], F32)
        kk_, Rkk = dbl("kk", [128, 128], F32)
        kk2_, Rkk2 = dbl("kk2", [128, 128], BF16)
        rs_, Rrs = dbl("rs", [128, 128], F32)
        tmp_, Rtmp = rs_, Rrs
        ka_, Rka = dbl("ka", [128, 128], F32)
        kp_, Rkp = dbl("kp", [128, 128], F32)
        AR_, RAR = dbl("AR", [128, 2, 128], BF16)
        Bt_, RBt = dbl("Bt", [128, 128], BF16)
        Kt_, RKt = dbl("Kt", [128, 128], BF16)
        BKg_, RBKg = dbl("BKg", [128, 3, 128], BF16)
        TM3_, RTM3 = dbl("TM3", [128, 3, 128], BF16)
        rkr_, Rrkr = dbl("rkr", [128, 128], BF16)
        LT_ = [SB(f"LT{g}", [128, 2, 128], BF16) for g in range(2)]; RLT = [Res(f"LT{g}") for g in range(2)]
        KT2_ = [SB(f"KT2{g}", [128, 2, 128], BF16) for g in range(2)]; RKT2 = [Res(f"KT2{g}") for g in range(2)]
        PP_ = [[SB(f"PP{g}_{j}", [128, 2, 128], BF16) for j in range(2)] for g in range(2)]
        RPP = [[Res(f"PP{g}_{j}") for j in range(2)] for g in range(2)]
        MT_ = [[SB(f"MT{g}_{j}", [128, 128], BF16) for j in range(2)] for g in range(2)]
        RMT = [[Res(f"MT{g}_{j}") for j in range(2)] for g in range(2)]
        Wb_ = [SB(f"Wb{g}", [128, 64], BF16) for g in range(2)]; RWb = [Res(f"Wb{g}") for g in range(2)]
        Ub_ = [SB(f"Ub{g}", [128, 64], BF16) for g in range(2)]; RUb = [Res(f"Ub{g}") for g in range(2)]
        Hst = SB("Hst", [128, 8, 64], F32); Hb = SB("Hb", [128, 8, 2, 64], BF16)
        RH = [Res(f"H{i}") for i in range(8)]; RHb = [Res(f"Hb{i}") for i in range(8)]
        att = SB("att", [128, 128], F32); Ratt = Res("att")
        y2_, Ry2 = same2(att), same2(Ratt)
        gst = SB("gst", [128, 2, 6], F32); Rgst = Res("gst")
        gmv = SB("gmv", [128, 2, 2], F32); Rgmv = Res("gmv")
        grs = SB("grs", [128, 2], F32); Rgrs = Res("grs")
        ynb = SB("ynb", [128, 128], BF16); Rynb = Res("ynb")
        yy_, Ryy = dbl("yy", [128, 128], F32)
        PT = SB("PTs", [128, 512], BF16); RPT = Res("PT")
        rden = rs_[0]; Rrden = Rrs[0]
        p1_end = bump[0]
        bump[0] = mark
        lnB = SB("lnB", [128, 2, D], F32)
        h1t = same2(SB("h1t", [128, D], F32)); Rh1t = same2(Res("h1t"))
        uT = SB("uT", [128, 32, 128], BF16); RuT = [Res(f"uT{j}") for j in range(8)]
        urelu = [SB(f"urelu{j}", [128, 512], F32) for j in range(2)]; Rurelu = [Res(f"urelu{j}") for j in range(2)]
        print("SBUF bytes: shared", mark, "phase1 end", p1_end, "phase2 end", bump[0])

        pb = [PS(f"pb{j}") for j in range(8)]
        Rpb = [Res(f"pb{j}") for j in range(8)]
        pb_bf = [p.bitcast(BF16) if hasattr(p, 'bitcast') else None for p in pb]

        P.dma('sp', pp[:], pp_d, writes=[Rpp])
        P.dma('sp', cst[:], cst_d, writes=[Rcst])
        P.dma('sp', lnbc[:], lnbc_d[:, 0:2, :], writes=[Rlnbc])
        for i_ in range(8):
            P.dma('pool', biasT[:, i_, :], bias_d[:, i_, :], writes=[Rbias])
        P.dma('pool', wlora[:], wlora_d, writes=[Rwl])
        P.dma('pool', wg2[:], wg2_d, writes=[Rwl])
        wcat_v = wcat_d.rearrange("(k p) c -> p k c", p=128)
        for kc in range(8):
            for c0 in range(0, NCOL, 2048):
                c1 = min(NCOL, c0 + 2048)
                P.dma('pool', w_in[:, kc, c0:c1], wcat_v[:, kc, c0:c1], writes=[Rwk[kc]])
        wout_v = wout_d.rearrange("(k p) c -> p k c", p=128)
        for kc in range(8):
            P.dma('pool', w_out[:, kc, :], wout_v[:, kc, :], writes=[Rwo])
        Rk = Res("consts")
        P.cp('dve', identb[:], ident_f, [Rcst], [Rk])
        P.cp('dve', onesb[:], ones_f, [Rcst], [Rk])
        P.cp('dve', blkb[:], cst[:, 4, :], [Rcst], [Rk])
        P.act(sinkexp[:], pp[:, 98:106], AF.Exp, [Rpp], [Rk])
        P.ts('pool', lnbc[:, 0, :], lnbc[:, 0, :], ALPHA, None, ALU.mult, None, [Rlnbc], [Rlnbc])
        P.ts('pool', lnbc[:, 1, :], lnbc[:, 1, :], ALPHA, None, ALU.mult, None, [Rlnbc], [Rlnbc])
        P.memset('pool', car[:], 0.0, [Rcar])
        P.memset('pool', xt[0][:], 0.0, [Rxt[0]])
        P.memset('dve', Hst[:], 0.0, RH)
        P.memset('dve', Hb[:], 0.0, RHb)
        P.memset('pool', qT[0][:], 0.0, RqT[0])
        P.memset('pool', lor[0][:], 0.0, [Rlor[0]])
        P.memset('pool', lor[1][:], 0.0, [Rlor[1]])

        zt_i = [0]
        zd_i = [0]

        def layer_norm_stats(src, Rsrc, eps):
            P.op('dve', lambda e: e.bn_stats(out=stt6[:, 0, :], in_=src[:, 0:512]), [Rsrc], [Rst])
            P.op('dve', lambda e: e.bn_stats(out=stt6[:, 1, :], in_=src[:, 512:1024]), [Rsrc], [Rst])
            P.op('dve', lambda e: e.bn_aggr(out=mv[:], in_=stt6[:].rearrange("p a b -> p (a b)")), [Rst], [Rmv])
            P.act(rstd[:], mv[:, 1:2], AF.Sqrt, [Rmv], [Rrstd], bias=eps_ap(eps), scale=1.0)
            P.op('dve', lambda e: e.reciprocal(out=rstd[:], in_=rstd[:]), [Rrstd], [Rrstd])

        P.memset('pool', epsb[:, 0:1], LN_EPS, [Reps])
        P.memset('pool', epsb[:, 1:2], GN_EPS, [Reps])
        P.memset('pool', epsb[:, 2:3], 1e-16, [Reps])

        def eps_ap(eps):
            return epsb[:, 0:1] if eps == LN_EPS else epsb[:, 1:2]

        def token_shift(psrc, Rpsrc, c0, nb, j0, mu_ap, dst, Rdst, n):
            k = zt_i[0] % 2; zt_i[0] += 1
            kd = 0
            z = zt[k]
            pc, pn_ = (n - 1) % 2, n % 2
            P.cp('act', z[:, 0:nb, 1:129], psrc[:, c0:c0 + nb * 128].rearrange("p (a b) -> p a b", a=nb), [Rpsrc], [Rzt[k]])
            P.cp('pool', z[:, 0:nb, 0], car[:, pc, j0:j0 + nb], [Rcar], [Rzt[k]])
            P.cp('pool', car[:, pn_, j0:j0 + nb], z[:, 0:nb, 128], [Rzt[k]], [Rcar])
            d = zd[kd]
            P.tt('dve', d[:, 0:nb, :], z[:, 0:nb, 0:128], z[:, 0:nb, 1:129], ALU.subtract, [Rzt[k]], [Rzd[kd]])
            P.tt('dve', d[:, 0:nb, :], d[:, 0:nb, :], mu_ap, ALU.mult, [Rzd[kd], Rpp], [Rzd[kd]])
            P.tt('pool', dst, d[:, 0:nb, :], z[:, 0:nb, 1:129], ALU.add, [Rzd[kd], Rzt[k]], [Rdst])

        pj_i = [0]

        def proj_bank():
            b = 1 + (pj_i[0] % 2); pj_i[0] += 1
            return b

        def front(n):
            par = n % 2
            if n >= 1:
                P.dma('sp', xt[par][:], x_d[(n - 1) * 128:n * 128, :], writes=[Rxt[par]])
            else:
                P.dma('sp', xt[0][112:128, :], meta_d, writes=[Rxt[0]])
            layer_norm_stats(xt[par], Rxt[par], LN_EPS)
            P.ts('dve', hres[par][:], xt[par][:], mv[:, 0:1], rstd[:, 0:1], ALU.subtract, ALU.mult, [Rxt[par], Rmv, Rrstd], [Rhres[par]])
            P.cp('pool', xnb[:], hres[par][:], [Rhres[par]], [Rxnb])
            if n >= 1:
                P.tt('pool', hres[par][:], hres[par][:], lnbc[:, 0, :], ALU.mult, [Rhres[par], Rlnbc], [Rhres[par]])
                P.tt('pool', hres[par][:], hres[par][:], lnbc[:, 1, :], ALU.add, [Rhres[par], Rlnbc], [Rhres[par]])
            yield
            ptb = pb[0].bitcast(BF16)
            for kc in range(8):
                P.tr(ptb[:, kc * 128:(kc + 1) * 128], xnb[:, kc * 128:(kc + 1) * 128], identb[:], [Rxnb, Rk], [Rpb[0]], sig=(kc == 7))
            for kc in range(8):
                P.act(hT[par][:, kc, :], ptb[:, kc * 128:(kc + 1) * 128], AF.Identity, [Rpb[0], Rpp], [RhT[par]],
                      bias=pp[:, 8 + kc:9 + kc], scale=pp[:, kc:kc + 1])
            if n == 0:
                P.memset('pool', hT[0][:, :, 0:112], 0.0, [RhT[0]])
            yield

            def fm_group(cols, b):
                for s_, cb in enumerate(cols):
                    for kc in range(8):
                        P.mm(pb[b][:, s_ * 128:(s_ + 1) * 128], w_in[:, kc, cb:cb + 128], hT[par][:, kc, :],
                             kc == 0, kc == 7, [Rwk[kc], RhT[par]], [Rpb[b]], sig=(kc == 7 and s_ == len(cols) - 1))

            b = proj_bank()
            for s_, cb in enumerate((0, 256, 384)):
                for kc in range(8):
                    P.mm(pb[b][:, s_ * 128:(s_ + 1) * 128], w_in[:, kc, cb:cb + 128], hT[par][:, kc, :],
                         kc == 0, kc == 7, [Rwk[kc], RhT[par]], [Rpb[b]], sig=False)
            for kc in range(8):
                P.mm(pb[b][:, 384:512], hT[par][:, kc, :], w_in[:, kc, 128:256], kc == 0, kc == 7,
                     [Rwk[kc], RhT[par]], [Rpb[b]], sig=(kc == 7))
            P.cp('act', kT[n % 3][:], pb[b][:, 0:128], [Rpb[b]], [RkT[n % 3]])
            P.cp('act', vat[n % 3][:], pb[b][:, 384:512], [Rpb[b]], [Rvat[n % 3]])
            token_shift(pb[b], Rpb[b], 128, 2, 24, pp[:, 40:42].unsqueeze(2).to_broadcast([128, 2, 128]),
                        zAG[:], RzAG, n)
            P.act(lor[par][0:64, 0, :], zAG[0:64, 0, :], AF.Tanh, [RzAG], [Rlor[par]])
            P.cp('dve', lor[par][64:128, 1, :], zAG[64:128, 0, :], [RzAG], [Rlor[par]])
            P.act(lor[par][:, 2, :], zAG[:, 1, :], AF.Sigmoid, [RzAG], [Rlor[par]])
            yield
            for i in range(8):
                yield ('need', i)
                base = 512 + i * 768
                b = proj_bank()
                fm_group((base, base + 128, base + 256), b)
                P.act(qT[par][0:64, i, 0, :], pb[b][0:64, 0:128], AF.Identity, [Rpb[b]], [RqT[par][i]], scale=0.125)
                P.act(qT[par][64:128, i, 1, :], pb[b][64:128, 0:128], AF.Identity, [Rpb[b]], [RqT[par][i]], scale=0.125)
                P.act(gat[par][:, i, :], pb[b][:, 128:256], AF.Sigmoid, [Rpb[b]], [Rga[par][i]])
                P.act(grw[par][:, i, :], pb[b][:, 256:384], AF.Sigmoid, [Rpb[b]], [Rgr[par][i]])
                yield
                b = proj_bank()
                fm_group((base + 384, base + 512, base + 640), b)
                token_shift(pb[b], Rpb[b], 0, 3, i * 3,
                            pp[:, 16 + i * 3:19 + i * 3].unsqueeze(2).to_broadcast([128, 3, 128]),
                            rkv[:, i, :, :], Rrkv[i], n)
                yield

        def mix(n):
            par = n % 2
            ppar = (n - 1) % 2
            for i in range(8):
                q_ = i % 2
                r_s = rkv[:, i, 0, :]; k_s = rkv[:, i, 1, :]; v_s = rkv[:, i, 2, :]
                Rr = Rrkv[i]
                b = 3
                P.mm(pb[b][:, 0:128], wlora[:, i * 128:(i + 1) * 128], lor[par][:, 0, :], True, True, [Rwl, Rlor[par]], [Rpb[b]], sig=False)
                P.mm(pb[b][:, 128:256], wlora[:, i * 128:(i + 1) * 128], lor[par][:, 1, :], True, True, [Rwl, Rlor[par]], [Rpb[b]], sig=False)
                P.mm(pb[b][:, 256:384], wg2[:, i * 128:(i + 1) * 128], lor[par][:, 2, :], True, True, [Rwl, Rlor[par]], [Rpb[b]])
                P.act(sig_[q_][:], pb[b][:, 0:128], AF.Sigmoid, [Rpb[b], Rpp], [Rsig[q_]], bias=pp[:, 42 + i:43 + i], scale=1.0)
                P.act(aa_[q_][:], pb[b][:, 128:256], AF.Sigmoid, [Rpb[b], Rpp], [Raa[q_]], bias=pp[:, 50 + i:51 + i], scale=1.0)
                P.cp('act', gg_[q_][:], pb[b][:, 256:384], [Rpb[b]], [Rgg[q_]])
                P.op('dve', lambda e, o=cum_[q_], s=sig_[q_]: e.tensor_tensor_scan(out=o[:], data0=ones_f, data1=s[:], initial=0.0, op0=ALU.mult, op1=ALU.add),
                     [Rsig[q_], Rcst], [Rcum[q_]])
                P.tt('pool', tmp_[q_][:], cum_[q_][:], sig_[q_][:], ALU.subtract, [Rcum[q_], Rsig[q_]], [Rtmp[q_]])
                P.ts('dve', nb_[q_][:, 0:1], cum_[q_][:, 127:128], -DK, None, ALU.mult, None, [Rcum[q_]], [Rnb[q_]])
                P.act(epos_[q_][:], cum_[q_][:], AF.Exp, [Rcum[q_]], [Repos[q_]], scale=-DK)
                P.act(eneg_[q_][:], cum_[q_][:], AF.Exp, [Rcum[q_]], [Reneg[q_]], scale=DK)
                P.act(eprv_[q_][:], tmp_[q_][:], AF.Exp, [Rtmp[q_]], [Reprv[q_]], scale=-DK)
                P.act(eend_[q_][:], cum_[q_][:], AF.Exp, [Rcum[q_], Rnb[q_]], [Reend[q_]], bias=nb_[q_][:, 0:1], scale=DK)
                P.act(nb_[q_][:, 1:2], cum_[q_][:, 127:128], AF.Exp, [Rcum[q_]], [Rnb[q_]], scale=-DK)
                P.ts('dve', kk_[q_][:], k_s, pp[:, 58 + i:59 + i], None, ALU.mult, None, [Rr, Rpp], [Rkk[q_]])
                P.tt('pool', kk2_[q_][:], kk_[q_][:], kk_[q_][:], ALU.mult, [Rkk[q_]], [Rkk2[q_]])
                P.mm(pb[b][:, 384:512], blkb[:], kk2_[q_][:], True, True, [Rk, Rkk2[q_]], [Rpb[b]])
                P.act(rs_[q_][:], pb[b][:, 384:512], AF.Ln, [Rpb[b], Reps], [Rrs[q_]], bias=epsb[:, 2:3], scale=1.0)
                P.act(rs_[q_][:], rs_[q_][:], AF.Exp, [Rrs[q_]], [Rrs[q_]], scale=-0.5)
                P.tt('dve', kk_[q_][:], kk_[q_][:], rs_[q_][:], ALU.mult, [Rkk[q_], Rrs[q_]], [Rkk[q_]])
                P.tt('pool', ka_[q_][:], kk_[q_][:], aa_[q_][:], ALU.mult, [Rkk[q_], Raa[q_]], [Rka[q_]])
                P.ts('dve', kp_[q_][:], aa_[q_][:], -1.0, pp[:, 66 + i:67 + i], ALU.add, ALU.mult, [Raa[q_], Rpp], [Rkp[q_]])
                P.stt(kp_[q_][:], kp_[q_][:], 1.0, k_s, ALU.add, ALU.mult, [Rkp[q_], Rr], [Rkp[q_]])
                P.stt(AR_[q_][:, 0, :], kk_[q_][:], -1.0, eprv_[q_][:], ALU.mult, ALU.mult, [Rkk[q_], Reprv[q_]], [RAR[q_]])
                P.tt('pool', AR_[q_][:, 1, :], r_s, epos_[q_][:], ALU.mult, [Rr, Repos[q_]], [RAR[q_]])
                P.tt('dve', Bt_[q_][:], ka_[q_][:], eneg_[q_][:], ALU.mult, [Rka[q_], Reneg[q_]], [RBt[q_]])
                P.tt('pool', Kt_[q_][:], kp_[q_][:], eneg_[q_][:], ALU.mult, [Rkp[q_], Reneg[q_]], [RKt[q_]])
                P.tt('dve', BKg_[q_][:, 0, :], ka_[q_][:], eend_[q_][:], ALU.mult, [Rka[q_], Reend[q_]], [RBKg[q_]])
                P.tt('pool', BKg_[q_][:, 1, :], kp_[q_][:], eend_[q_][:], ALU.mult, [Rkp[q_], Reend[q_]], [RBKg[q_]])
                P.cp('pool', BKg_[q_][:, 2, :], v_s, [Rr], [RBKg[q_]])
                P.stt(rkr_[q_][:], r_s, pp[:, 74 + i:75 + i], kp_[q_][:], ALU.mult, ALU.mult, [Rr, Rpp, Rkp[q_]], [Rrkr[q_]])
                yield
                if cfg.get("mix_stop", 99) <= 1:
                    return
                ptb = pb[0].bitcast(BF16)
                for j in range(3):
                    P.tr(ptb[:, j * 128:(j + 1) * 128], BKg_[q_][:, j, :], identb[:], [RBKg[q_], Rk], [Rpb[0]], sig=(j == 2))
                P.cp('act', TM3_[q_][:].rearrange("p a b -> p (a b)"), ptb[:, 0:384], [Rpb[0]], [RTM3[q_]])
                for g in range(2):
                    gs = slice(g * 64, (g + 1) * 64)
                    ba = 4 + g
                    bd = 6 + g
                    ARv = AR_[q_][gs, :, :].rearrange("p a b -> p (a b)")
                    P.mm(pb[ba][:, 0:256], Bt_[q_][gs, :], ARv, True, True, [RBt[q_], RAR[q_]], [Rpb[ba]], sig=False)
                    P.mm(pb[ba][:, 256:512], Kt_[q_][gs, :], ARv, True, True, [RKt[q_], RAR[q_]], [Rpb[ba]])
                    P.mm(pb[bd][:, 384:512], AR_[q_][gs, 0, :], Bt_[q_][gs, :], True, True, [RBt[q_], RAR[q_]], [Rpb[bd]])
                    P.tt('dve', LT_[g][:].rearrange("p a b -> p (a b)"), pb[ba][:, 0:256], mask2.rearrange("p a b -> p (a b)"), ALU.mult,
                         [Rpb[ba], Rcst], [RLT[g]])
                    P.tt('dve', KT2_[g][:].rearrange("p a b -> p (a b)"), pb[ba][:, 256:512], mask2.rearrange("p a b -> p (a b)"), ALU.mult,
                         [Rpb[ba], Rcst], [RKT2[g]])
                    P.tt('dve', PP_[g][0][:, 0, :], pb[bd][:, 384:512], mask_ls, ALU.mult, [Rpb[bd], Rcst], [RPP[g][0]])
                    P.cp('pool', PP_[g][0][:, 1, :], LT_[g][:, 0, :], [RLT[g]], [RPP[g][0]])
                    P.tt('pool', MT_[g][0][:], LT_[g][:, 0, :], identb[:], ALU.add, [RLT[g], Rk], [RMT[g][0]])
                yield
                if cfg.get("mix_stop", 99) <= 2:
                    return
                for lev in range(1, 7):
                    src = (lev - 1) % 2
                    dst = lev % 2
                    for g in range(2):
                        bd = 6 + g
                        Pm = PP_[g][src][:, 0, :]; PTm = PP_[g][src][:, 1, :]
                        last = (lev == 6)
                        P.mm(pb[bd][:, 0:128], PTm, Pm, True, True, [RPP[g][src]], [Rpb[bd]], sig=last)
                        if not last:
                            P.mm(pb[bd][:, 128:256], Pm, PTm, True, True, [RPP[g][src]], [Rpb[bd]])
                            P.cp('act', PP_[g][dst][:].rearrange("p a b -> p (a b)"), pb[bd][:, 0:256], [Rpb[bd]], [RPP[g][dst]])
                        else:
                            P.cp('act', PP_[g][dst][:, 0, :], pb[bd][:, 0:128], [Rpb[bd]], [RPP[g][dst]])
                        P.mm(pb[bd][:, 256:384], PP_[g][dst][:, 0, :], MT_[g][src][:], True, True, [RPP[g][dst], RMT[g][src]], [Rpb[bd]])
                        P.tt('dve', MT_[g][dst][:], pb[bd][:, 256:384], MT_[g][src][:], ALU.add, [Rpb[bd], RMT[g][src]], [RMT[g][dst]])
                    if lev % 2 == 0:
                        yield
                if cfg.get("mix_stop", 99) <= 3:
                    return
                MTf = [MT_[g][0] for g in range(2)]
                RMTf = [RMT[g][0] for g in range(2)]
                bs_ = 5
                for g in range(2):
                    gs = slice(g * 64, (g + 1) * 64)
                    Vt_g = TM3_[q_][:, 2, gs]
                    P.mm(pb[bs_][:, g * 64:(g + 1) * 64], AR_[q_][:, 0, :], Hb[:, i, g, :], True, False, [RAR[q_], RHb[i]], [Rpb[bs_]], sig=False)
                    P.mm(pb[bs_][:, g * 64:(g + 1) * 64], KT2_[g][:, 0, :], Vt_g, False, True, [RKT2[g], RTM3[q_]], [Rpb[bs_]])
                    P.cp('act', Wb_[g][:], pb[bs_][:, g * 64:(g + 1) * 64], [Rpb[bs_]], [RWb[g]])
                for g in range(2):
                    P.mm(pb[bs_][:, 128 + g * 64:128 + (g + 1) * 64], MTf[g][:], Wb_[g][:], True, True, [RMTf[g], RWb[g]], [Rpb[bs_]])
                    P.cp('dve', Ub_[g][:], pb[bs_][:, 128 + g * 64:128 + (g + 1) * 64], [Rpb[bs_]], [RUb[g]])
                for g in range(2):
                    gs = slice(g * 64, (g + 1) * 64)
                    Vt_g = TM3_[q_][:, 2, gs]
                    oc = slice(256 + g * 64, 256 + (g + 1) * 64)
                    P.mm(pb[bs_][:, oc], AR_[q_][:, 1, :], Hb[:, i, g, :], True, False, [RAR[q_], RHb[i]], [Rpb[bs_]], sig=False)
                    P.mm(pb[bs_][:, oc], LT_[g][:, 1, :], Ub_[g][:], False, False, [RLT[g], RUb[g]], [Rpb[bs_]], sig=False)
                    P.mm(pb[bs_][:, oc], KT2_[g][:, 1, :], Vt_g, False, True, [RKT2[g], RTM3[q_]], [Rpb[bs_]], sig=False)
                    P.mm(pb[bs_][gs, 384:448], TM3_[q_][:, 0, gs], Ub_[g][:], True, False, [RTM3[q_], RUb[g]], [Rpb[bs_]], sig=False)
                    P.mm(pb[bs_][gs, 384:448], TM3_[q_][:, 1, gs], Vt_g, False, True, [RTM3[q_]], [Rpb[bs_]], sig=(g == 1))
                P.stt(Hst[:, i, :], Hst[:, i, :], nb_[q_][:, 1:2], pb[bs_][:, 384:448], ALU.mult, ALU.add, [RH[i], Rnb[q_], Rpb[bs_]], [RH[i]])
                P.cp('act', Hb[0:64, i, 0, :], Hst[0:64, i, :], [RH[i]], [RHb[i]])
                P.cp('act', Hb[64:128, i, 1, :], Hst[64:128, i, :], [RH[i]], [RHb[i]])
                for g in range(2):
                    oc = slice(256 + g * 64, 256 + (g + 1) * 64)
                    P.op('dve', lambda e, g=g, oc=oc: e.bn_stats(out=gst[:, g, :], in_=pb[bs_][:, oc]), [Rpb[bs_]], [Rgst])
                for g in range(2):
                    P.op('dve', lambda e, g=g: e.bn_aggr(out=gmv[:, g, :], in_=gst[:, g, :]), [Rgst], [Rgmv])
                P.act(grs[:], gmv[:, :, 1], AF.Sqrt, [Rgmv, Reps], [Rgrs], bias=epsb[:, 1:2], scale=1.0)
                P.op('dve', lambda e: e.reciprocal(out=grs[:], in_=grs[:]), [Rgrs], [Rgrs])
                for g in range(2):
                    oc = slice(256 + g * 64, 256 + (g + 1) * 64)
                    P.ts('dve', ynb[:, g * 64:(g + 1) * 64], pb[bs_][:, oc], gmv[:, g, 0:1], grs[:, g:g + 1], ALU.subtract, ALU.mult,
                         [Rpb[bs_], Rgmv, Rgrs], [Rynb])
                yield
                if cfg.get("mix_stop", 99) <= 4:
                    return
                bm = 4
                pmb = pb[bm].bitcast(BF16)
                P.tr(pmb[:, 768:896], ynb[:], identb[:], [Rynb, Rk], [Rpb[bm]])
                P.mm(pb[bm][:, 256:384], blkb[:], rkr_[q_][:], True, True, [Rk, Rrkr[q_]], [Rpb[bm]])
                P.ts('dve', yy_[q_][:], pmb[:, 768:896], pp[:, 82 + i:83 + i], pp[:, 90 + i:91 + i], ALU.mult, ALU.add, [Rpb[bm], Rpp], [Ryy[q_]])
                P.tt('dve', y2_[q_][:], pb[bm][:, 256:384], v_s, ALU.mult, [Rpb[bm], Rr], [Ry2[q_]])
                if cfg.get('dbgbs') and i == 7:
                    P.cp('dve', kk_[0][:], pb[bm][:, 256:384], [Rpb[bm]], [Rkk[0]])
                P.tt('pool', yy_[q_][:], yy_[q_][:], y2_[q_][:], ALU.add, [Ryy[q_], Ry2[q_]], [Ryy[q_]])
                P.tt('pool', yy_[q_][:], yy_[q_][:], gg_[q_][:], ALU.mult, [Ryy[q_], Rgg[q_]], [Ryy[q_]])
                if n == 0:
                    yield ('done', i)
                    continue
                P.tt('pool', yy_[q_][:], yy_[q_][:], grw[par][:, i, :], ALU.mult, [Ryy[q_], Rgr[par][i]], [Ryy[q_]])
                bl = 3
                for g in range(2):
                    gs = slice(g * 64, (g + 1) * 64)
                    for sh in range(2):
                        slot = (n - 1) % 3 if sh == 0 else n % 3
                        c0 = (g * 2 + sh) * 128
                        P.mm(pb[bl][:, c0:c0 + 128], kT[slot][:, :], qT[par][:, i, g, :], True, True, [RkT[slot], RqT[par][i]], [Rpb[bl]],
                             sig=(g == 1 and sh == 1))
                P.tt('dve', pb[bl][:], pb[bl][:], biasT[:, i, :], ALU.add, [Rpb[bl], Rbias], [Rpb[bl]])
                if n == 1:
                    lbv = pb[bl][:].rearrange("p (g s q) -> p g s q", g=2, s=2)
                    PTv = PT[:].rearrange("p (g s q) -> p g s q", g=2, s=2)
                    P.act(PTv[:, :, 0, :], lbv[:, :, 0, :], AF.Exp, [Rpb[bl], Rpp], [RPT], bias=pp[:, 106:107], scale=1.0)
                    P.act(PTv[:, :, 1, :], lbv[:, :, 1, :], AF.Exp, [Rpb[bl]], [RPT])
                else:
                    P.act(PT[:], pb[bl][:], AF.Exp, [Rpb[bl]], [RPT])
                for g in range(2):
                    gs = slice(g * 64, (g + 1) * 64)
                    for sh in range(2):
                        slot = (n - 1) % 3 if sh == 0 else n % 3
                        c0 = (g * 2 + sh) * 128
                        P.mm(pb[bm][gs, 0:128], vat[slot][:, gs], PT[:, c0:c0 + 128], sh == 0, sh == 1, [Rvat[slot], RPT], [Rpb[bm]], sig=False)
                    for sh in range(2):
                        c0 = (g * 2 + sh) * 128
                        P.mm(pb[bm][gs, 128:256], onesb[:, 0:64], PT[:, c0:c0 + 128], sh == 0, sh == 1, [Rk, RPT], [Rpb[bm]],
                             sig=(g == 1 and sh == 1))
                P.ts('dve', rden[:], pb[bm][:, 128:256], sinkexp[:, i:i + 1], None, ALU.add, None, [Rpb[bm], Rk], [Rrden])
                P.op('dve', lambda e: e.reciprocal(out=rden[:], in_=rden[:]), [Rrden], [Rrden])
                P.tt('dve', att[:], pb[bm][:, 0:128], rden[:], ALU.mult, [Rpb[bm], Rrden], [Ratt])
                P.tt('pool', att[:], att[:], gat[par][:, i, :], ALU.mult, [Ratt, Rga[par][i]], [Ratt])
                P.tt('pool', merged[:, i, :], att[:], yy_[q_][:], ALU.add, [Ratt, Ryy[q_]], [Rmg[i]])
                yield ('done', i)
                if cfg.get("mix_stop", 99) <= 6:
                    return
            if n == 0:
                return
            for hh in range(2):
                b = 1 + hh
                for i in range(8):
                    P.mm(pb[b][:], merged[:, i, :], w_out[:, i, hh * 512:(hh + 1) * 512], i == 0, i == 7, [Rmg[i], Rwo], [Rpb[b]], sig=(i == 7))
                P.stt(pre[:, hh * 512:(hh + 1) * 512], xn_dummy(hres[par], hh), 1.0, pb[b][:], ALU.mult, ALU.add, [Rhres[par], Rpb[b]], [Rpre])
            yield
            layer_norm_stats(pre, Rpre, LN_EPS)
            P.ts('dve', pre[:], pre[:], mv[:, 0:1], rstd[:, 0:1], ALU.subtract, ALU.mult, [Rpre, Rmv, Rrstd], [Rpre])
            P.dma('sp', h1_d[(n - 1) * 128:n * 128, :], pre[:], reads=[Rpre], writes=[Rscr])
            yield

        def xn_dummy(t, hh):
            return t[:, hh * 512:(hh + 1) * 512]

        Rscr = Res("scratch")

        def ffn(n):
            par = n % 2
            P.dma('sp', h1t[par][:], h1_d[(n - 1) * 128:n * 128, :], reads=[Rscr], writes=[Rh1t[par]])
            P.tt('pool', h1t[par][:], h1t[par][:], lnbc[:, 0, :], ALU.mult, [Rh1t[par], Rlnbc], [Rh1t[par]])
            P.tt('pool', h1t[par][:], h1t[par][:], lnbc[:, 1, :], ALU.add, [Rh1t[par], Rlnbc], [Rh1t[par]])
            P.cp('pool', xnb[:], h1t[par][:], [Rh1t[par]], [Rxnb])
            ptb = pb[0].bitcast(BF16)
            for kc in range(8):
                P.tr(ptb[:, kc * 128:(kc + 1) * 128], xnb[:, kc * 128:(kc + 1) * 128], identb[:], [Rxnb, Rk], [Rpb[0]], sig=(kc == 7))
            P.cp('act', hT[par][:].rearrange("p a b -> p (a b)"), ptb[:, 0:1024], [Rpb[0]], [RhT[par]])
            yield
            for cg in range(8):
                b = 3 + (cg % 4)
                for s_ in range(4):
                    c = cg * 4 + s_
                    for kc in range(8):
                        P.mm(pb[b][:, s_ * 128:(s_ + 1) * 128], w_ff1[:, kc, c * 128:(c + 1) * 128], hT[par][:, kc, :], kc == 0, kc == 7,
                             [Rwk[kc], RhT[par]], [Rpb[b]], sig=(kc == 7 and s_ == 3))
                u = cg % 2
                P.act(urelu[u][:], pb[b][:], AF.Relu, [Rpb[b]], [Rurelu[u]])
                P.tt('pool' if cg % 2 else 'dve', uT[:, cg * 4:(cg + 1) * 4, :].rearrange("p a b -> p (a b)"), urelu[u][:], urelu[u][:], ALU.mult,
                     [Rurelu[u]], [RuT[cg]])
                yield
            for hh in range(2):
                b = 1 + hh
                for c in range(32):
                    P.mm(pb[b][:], uT[:, c, :], w_ff2[:, c, hh * 512:(hh + 1) * 512], c == 0, c == 31, [RuT[c // 4], Rf2[c // 2]], [Rpb[b]], sig=(c == 31))
                P.stt(pre[:, hh * 512:(hh + 1) * 512], h1t[par][:, hh * 512:(hh + 1) * 512], ALPHA, pb[b][:], ALU.mult, ALU.add,
                      [Rh1t[par], Rpb[b]], [Rpre])
                yield
            layer_norm_stats(pre, Rpre, LN_EPS)
            P.ts('dve', pre[:], pre[:], mv[:, 0:1], rstd[:, 0:1], ALU.subtract, ALU.mult, [Rpre, Rmv, Rrstd], [Rpre])
            P.tt('pool', hres[par][:], pre[:], lnB[:, 0, :], ALU.mult, [Rpre, RlnB], [Rhres[par]])
            P.tt('pool', hres[par][:], hres[par][:], lnB[:, 1, :], ALU.add, [Rhres[par], RlnB], [Rhres[par]])
            P.dma('sp', out_d[(n - 1) * 128:n * 128, :], hres[par][:], reads=[Rhres[par]], is_output=True)
            yield

        def run2(mx, fr):
            mix_done = -1 if mx is not None else 99
            fr_need = -1
            while mx is not None or fr is not None:
                progressed = False
                if mx is not None:
                    try:
                        r = next(mx)
                        if isinstance(r, tuple) and r[0] == 'done':
                            mix_done = r[1]
                    except StopIteration:
                        mx = None
                        mix_done = 99
                    progressed = True
                if fr is not None and fr_need <= mix_done:
                    try:
                        r = next(fr)
                        if isinstance(r, tuple) and r[0] == 'need':
                            fr_need = r[1]
                    except StopIteration:
                        fr = None
                    progressed = True
                assert progressed

        def run(*gens):
            gens = [g for g in gens if g is not None]
            while gens:
                for g in list(gens):
                    try:
                        next(g)
                    except StopIteration:
                        gens.remove(g)

        run(front(0))
        for n in range(nt):
            if cfg.get('seq'):
                run(mix(n))
                if n + 1 < nt:
                    run(front(n + 1))
                continue
            run2(mix(n) if cfg.get('mix', True) else None, front(n + 1) if n + 1 < nt else None)
        dumps = {'hT': (hT[0], [128, 8, 128], BF16, RhT[0]), 'qT': (qT[0], [128, 8, 2, 128], BF16, RqT[0][7]),
                 'rkv': (rkv, [128, 8, 3, 128], F32, Rrkv[7]), 'kT0': (kT[0], [128, 128], BF16, RkT[0]), 'kT1': (kT[1], [128, 128], BF16, RkT[1]),
                 'vat0': (vat[0], [128, 128], BF16, Rvat[0]), 'lor0': (lor[0], [128, 3, 128], BF16, Rlor[0]),
                 'Hst': (Hst, [128, 8, 64], F32, RH[7]), 'Hb': (Hb, [128, 8, 2, 64], BF16, RHb[7]), 'merged': (merged, [128, 8, 128], BF16, Rmg[7]),
                 'pre': (pre, [128, D], F32, Rpre), 'hres1': (hres[1], [128, D], F32, Rhres[1]), 'hres0': (hres[0], [128, D], F32, Rhres[0]),
                 'yy': (yy_[0], [128, 128], F32, Ryy[0]), 'MT0': (MT_[0][0], [128, 128], BF16, RMT[0][0]), 'LT0': (LT_[0], [128, 2, 128], BF16, RLT[0]),
                 'sig': (sig_[0], [128, 128], F32, Rsig[0]), 'kk': (kk_[0], [128, 128], F32, Rkk[0]), 'att': (att, [128, 128], F32, Ratt),
                 'AR': (AR_[0], [128, 2, 128], BF16, RAR[0]), 'TM3': (TM3_[0], [128, 3, 128], BF16, RTM3[0]), 'PT': (PT, [128, 512], BF16, RPT),
                 'ynb': (ynb, [128, 128], BF16, Rynb), 'gg': (gg_[0], [128, 128], F32, Rgg[0]), 'gmv': (gmv, [128, 2, 2], F32, Rgmv), 'grs': (grs, [128, 2], F32, Rgrs), 'rkr': (rkr_[0], [128, 128], BF16, Rrkr[0]), 'kp': (kp_[0], [128, 128], F32, Rkp[0]), 'ga': (gat[0], [128, 8, 128], BF16, Rga[0][7]), 'gr': (grw[0], [128, 8, 128], BF16, Rgr[0][7])}
        print("nops", P.nops)
        P.maxops = None
        for name in cfg.get('dump', []):
            t_, shp, dty, R_ = dumps[name]
            dd = nc.dram_tensor("dbg_" + name, shp, dty, kind="ExternalOutput").ap()
            P.dma('sp', dd, t_[:], reads=[R_], is_output=True)
        if not cfg.get('phase2', True):
            P.finish()
            P.emit()
            return nc
        P.barrier()
        RlnB = Res("lnB")
        P.dma('sp', lnbc[:], lnbc_d[:, 2:4, :], writes=[Rlnbc])
        P.dma('sp', lnB[:], lnbc_d[:, 4:6, :], writes=[RlnB])
        for k in range(8):
            Rwk[k] = Res(f"wk2_{k}")
        wff1_v = wff1_d.rearrange("(k p) c -> p k c", p=128)
        for kc in range(8):
            for c0 in range(0, 4096, 2048):
                P.dma('pool', w_ff1[:, kc, c0:c0 + 2048], wff1_v[:, kc, c0:c0 + 2048], writes=[Rwk[kc]])
        wff2_v = wff2_d.rearrange("(c p) n -> p c n", p=128)
        for c0 in range(0, 32, 2):
            P.dma('pool', w_ff2[:, c0:c0 + 2, :], wff2_v[:, c0:c0 + 2, :], writes=[Rf2[c0 // 2]])
        prev = None
        for n in range(1, nt):
            run(ffn(n))
        P.finish()
        P.emit()
    return nc


def _paired_perm():
    idx = np.empty(1024, np.int64)
    for i in range(8):
        for g in range(2):
            idx[i * 128 + g * 64:i * 128 + (g + 1) * 64] = g * 512 + i * 64 + np.arange(64)
    return idx


def _t5_bucket(dist):
    d = np.maximum(dist, 1).astype(np.float32)
    large = 16 + (np.log(d / np.float32(16)) / np.float32(math.log(128 / 16)) * np.float32(16)).astype(np.int32)
    large = np.minimum(large, 31)
    return np.where(dist < 16, dist, large)


def _prep_shared(inp):
    f32 = np.float32
    PPm = _paired_perm()
    w_in = np.asarray(inp['w_in'], f32)[0]
    cols = [np.arange(1024, 1152), np.arange(1152, 1280), np.arange(4352, 4480), np.arange(4480, 4608)]
    for i in range(8):
        pi = PPm[i * 128:(i + 1) * 128]
        for B in (0, 4608, 5632, 1280, 2304, 3328):
            cols.append(B + pi)
    cols = np.concatenate(cols)
    assert cols.shape[0] == NCOL
    wcat = np.ascontiguousarray(w_in[:, cols])
    wout = np.ascontiguousarray(np.asarray(inp['w_out'], f32)[0][PPm, :])
    wlora = np.ascontiguousarray(np.concatenate([np.asarray(inp['decay_w2'], f32)[0][:, PPm],
                                                 np.asarray(inp['iclr_a2'], f32)[0][:, PPm]], 0))
    wg2 = np.ascontiguousarray(np.asarray(inp['gate_w2'], f32)[0][:, PPm])
    rows = [inp['ln0_g'], inp['ln0_b'], inp['ln1_g'][0], inp['ln1_b'][0], inp['ln2_g'][0], inp['ln2_b'][0]]
    lnbc = np.ascontiguousarray(np.broadcast_to(np.stack([np.asarray(r, f32) for r in rows], 0)[None], (128, 6, 1024)))
    pp = np.zeros((128, NP), f32)
    pp[:, 0:8] = np.asarray(inp['ln0_g'], f32).reshape(8, 128).T
    pp[:, 8:16] = np.asarray(inp['ln0_b'], f32).reshape(8, 128).T
    mu = np.asarray(inp['shift_mu'], f32)[0]
    pair = lambda v: np.asarray(v, f32).reshape(-1)[PPm].reshape(8, 128).T
    for j in range(3):
        pp[:, 16 + j:40:3] = pair(mu[j * 1024:(j + 1) * 1024])
    pp[:, 40] = mu[3072:3200]
    pp[:, 41] = mu[3200:3328]
    pp[:, 42:50] = pair(inp['decay_w0'][0])
    pp[:, 50:58] = pair(inp['iclr_a0'][0])
    pp[:, 58:66] = pair(inp['k_k'][0])
    pp[:, 66:74] = pair(inp['k_a'][0])
    pp[:, 74:82] = pair(np.asarray(inp['r_k'])[0].reshape(-1))
    pp[:, 82:90] = pair(inp['lnx_g'][0])
    pp[:, 90:98] = pair(inp['lnx_b'][0])
    sinks = np.asarray(inp['attn_sinks'], f32)[0]
    for i in range(8):
        pp[0:64, 98 + i] = sinks[i]
        pp[64:128, 98 + i] = sinks[8 + i]
    pp[0:112, 106] = NEG
    rb = np.asarray(inp['rel_bias'], f32)
    s = np.arange(128)[:, None]; q = np.arange(128)[None, :]
    biasT = np.empty((128, 8, 2, 2, 128), f32)
    for sh in range(2):
        dist = q + 128 - (sh * 128 + s)
        inw = (dist >= 0) & (dist < 128)
        bk = _t5_bucket(np.maximum(dist, 0))
        for i in range(8):
            for g in range(2):
                biasT[:, i, g, sh, :] = np.where(inw, rb[bk, g * 8 + i], f32(NEG))
    biasT = np.ascontiguousarray(biasT.reshape(128, 8, 512))
    cst = np.zeros((128, 6, 128), f32)
    r_ = np.arange(128)[:, None]; c_ = np.arange(128)[None, :]
    cst[:, 0] = (r_ == c_); cst[:, 1] = (c_ > r_); cst[:, 2] = (c_ >= r_); cst[:, 3] = (r_ > c_)
    cst[:, 4] = ((r_ // 64) == (c_ // 64)); cst[:, 5] = 1.0
    return dict(meta=np.ascontiguousarray(np.asarray(inp['meta_tokens'], f32)), wcat=wcat, wout=wout, wlora=wlora, wg2=wg2,
                wff1=np.ascontiguousarray(np.asarray(inp['w_ff1'], f32)[0]), wff2=np.ascontiguousarray(np.asarray(inp['w_ff2'], f32)[0]),
                lnbc=lnbc, pp=pp, biasT=biasT, cst=cst)


_NC_CACHE = {}


def kernel(**inputs):
    x = np.asarray(inputs['x'], np.float32)
    shared = _prep_shared(inputs)
    if 'nc' not in _NC_CACHE:
        _NC_CACHE['nc'] = build_nc()
    nc = _NC_CACHE['nc']
    in_maps = [dict(shared, x=np.ascontiguousarray(x[b])) for b in range(8)]
    res = run_bass_kernel_spmd(nc, in_maps, core_ids=list(range(8)))
    return np.stack([np.asarray(res.results[b]["out"], np.float32) for b in range(8)], 0)
```

```python
import math
from contextlib import ExitStack

import numpy as np
import concourse.bass as bass
import concourse.mybir as mybir
from concourse.bass_utils import run_bass_kernel_spmd

F32 = mybir.dt.float32
BF16 = mybir.dt.bfloat16
AF = mybir.ActivationFunctionType
ALU = mybir.AluOpType

ENGS = ['pe', 'act', 'dve', 'pool', 'sp']
BLOCKNAME = {'pe': 'tensor', 'act': 'scalar', 'dve': 'vector', 'pool': 'gpsimd', 'sp': 'sync'}
EPOCH = 16000
NDMA = 24

NT = 33
D = 1024
NCOL = 6656
ALPHA = 2.0 ** 0.25
LN_EPS = 1e-5
GN_EPS = 1e-5 * 64
DK = 0.6065306597126334
NEG = -30000.0
NP = 107


class Res:
    __slots__ = ('name', 'lw', 'rd')

    def __init__(self, name):
        self.name = name
        self.lw = None
        self.rd = []


class Prog:
    def __init__(self, nc, stack):
        self.nc = nc
        self.stack = stack
        self.q = {e: [] for e in ENGS}
        self.cnt = {e: 0 for e in ENGS}
        self.waited = {e: {} for e in ENGS}
        self.dma_i = 0
        self.dsem = [stack.enter_context(nc.semaphore(f"dsem{i}")) for i in range(NDMA)]
        self.esem = {e: [] for e in ENGS}
        self.out_toks = []

    def _sem(self, e, epoch):
        while len(self.esem[e]) <= epoch:
            self.esem[e].append(self.stack.enter_context(self.nc.semaphore(f"es_{e}_{len(self.esem[e])}")))
        return self.esem[e][epoch]

    def _need(self, eng, waits, tok, war=False):
        if tok is None:
            return
        if tok[0] == 'dma':
            key = ('dma', tok[1]); val = tok[2]
        else:
            peng, idx = tok
            if peng == eng:
                if war or eng == 'pe':
                    return
                if idx <= self.cnt[eng] - 2:
                    return
            key = peng; val = idx
        if self.waited[eng].get(key, 0) >= val:
            return
        if waits.get(key, 0) < val:
            waits[key] = val

    def _deps(self, eng, reads, writes, waits):
        for r in reads:
            self._need(eng, waits, r.lw)
        for w in writes:
            self._need(eng, waits, w.lw)
            for t in w.rd:
                self._need(eng, waits, t, war=True)
        for k, v in waits.items():
            self.waited[eng][k] = v

    def _mark(self, tok, reads, writes):
        for r in reads:
            r.rd.append(tok)
        for w in writes:
            w.lw = tok
            w.rd = []

    maxops = None
    allsig = False
    nops = 0

    def op(self, eng, fn, reads=(), writes=(), sig=True):
        self.nops += 1
        if self.maxops is not None and self.nops > self.maxops:
            return None
        if self.allsig:
            sig = True
        waits = {}
        self._deps(eng, reads, writes, waits)
        idx = self.cnt[eng] + 1
        if sig:
            self.cnt[eng] = idx
        tok = (eng, idx)
        self.q[eng].append((fn, list(waits.items()), sig, None))
        self._mark(tok, reads, writes)
        return tok

    def dma(self, eng, out_ap, in_ap, reads=(), writes=(), is_output=False, **kw):
        i = self.dma_i
        self.dma_i += 1
        s = i % NDMA
        val = 16 * (i // NDMA + 1)
        waits = {}
        if val > 16:
            self._need(eng, waits, ('dma', s, val - 16))
        self._deps(eng, reads, writes, waits)
        tok = ('dma', s, val)
        fn = lambda e: e.dma_start(out=out_ap, in_=in_ap, **kw)
        self.q[eng].append((fn, list(waits.items()), False, s))
        self._mark(tok, reads, writes)
        if is_output:
            self.out_toks.append(tok)
        return tok

    def finish(self):
        waits = {}
        for t in self.out_toks:
            self._need('sp', waits, t)
        self.q['sp'].append((None, list(waits.items()), False, None))

    def barrier(self):
        snap = dict(self.cnt)
        for e in ENGS:
            waits = {}
            for o in ENGS:
                if o != e and snap[o] > 0:
                    self._need(e, waits, (o, snap[o]))
            for k, v in waits.items():
                self.waited[e][k] = v
            self.q[e].append((None, list(waits.items()), False, None))

    def emit(self):
        nc = self.nc
        for e in ENGS:
            n = self.cnt[e]
            if n:
                self._sem(e, (n - 1) // EPOCH)
        with nc.Block() as block:
            for e in ENGS:
                if self.q[e]:
                    self._emit_engine(block, e)

    def _emit_engine(self, block, e):
        q = self.q[e]
        prog = self

        def body(eng):
            cnt = 0
            for fn, waits, sig, dsem in q:
                for key, val in waits:
                    if isinstance(key, tuple):
                        eng.wait_ge(prog.dsem[key[1]], val)
                    else:
                        eng.wait_ge(prog._sem(key, (val - 1) // EPOCH), (val - 1) % EPOCH + 1)
                if fn is None:
                    continue
                ins = fn(eng)
                if dsem is not None:
                    ins.then_inc(prog.dsem[dsem], 16)
                elif sig:
                    cnt += 1
                    ins.then_inc(prog._sem(e, (cnt - 1) // EPOCH), 1)
        getattr(block, BLOCKNAME[e])(body)

    def mm(self, out, lhsT, rhs, start, stop, reads, writes, sig=True):
        return self.op('pe', lambda e: e.matmul(out, lhsT=lhsT, rhs=rhs, start=start, stop=stop), reads, writes, sig)

    def tr(self, out, in_, ident, reads, writes, sig=True):
        return self.op('pe', lambda e: e.transpose(out, in_, ident), reads, writes, sig)

    def act(self, out, in_, func, reads, writes, bias=None, scale=None):
        kw = {}
        if bias is not None:
            kw['bias'] = bias
        if scale is not None:
            kw['scale'] = scale
        return self.op('act', lambda e: e.activation(out=out, in_=in_, func=func, **kw), reads, writes)

    def tt(self, eng, out, in0, in1, op, reads, writes):
        return self.op(eng, lambda e: e.tensor_tensor(out=out, in0=in0, in1=in1, op=op), reads, writes)

    def ts(self, eng, out, in0, s1, s2, op0, op1, reads, writes):
        if op1 is None:
            return self.op(eng, lambda e: e.tensor_scalar(out=out, in0=in0, scalar1=s1, scalar2=None, op0=op0), reads, writes)
        return self.op(eng, lambda e: e.tensor_scalar(out=out, in0=in0, scalar1=s1, scalar2=s2, op0=op0, op1=op1), reads, writes)

    def stt(self, out, in0, scalar, in1, op0, op1, reads, writes):
        return self.op('dve', lambda e: e.scalar_tensor_tensor(out=out, in0=in0, scalar=scalar, in1=in1, op0=op0, op1=op1), reads, writes)

    def cp(self, eng, out, in_, reads, writes):
        if eng == 'act':
            return self.op('act', lambda e: e.copy(out=out, in_=in_), reads, writes)
        return self.op(eng, lambda e: e.tensor_copy(out=out, in_=in_), reads, writes)

    def memset(self, eng, ap, val, writes):
        return self.op(eng, lambda e: e.memset(ap, val), (), writes)


def build_nc(cfg=None):
    cfg = cfg or {}
    nt = cfg.get('nt', NT)
    nc = bass.Bass("TRN2", target_bir_lowering=False, dynamic_dma_scratch_size=4096)
    dt = nc.dram_tensor
    x_d = dt("x", [4096, D], F32, kind="ExternalInput").ap()
    meta_d = dt("meta", [16, D], F32, kind="ExternalInput").ap()
    wcat_d = dt("wcat", [D, NCOL], F32, kind="ExternalInput").ap()
    wout_d = dt("wout", [D, D], F32, kind="ExternalInput").ap()
    wlora_d = dt("wlora", [128, D], F32, kind="ExternalInput").ap()
    wg2_d = dt("wg2", [128, D], F32, kind="ExternalInput").ap()
    wff1_d = dt("wff1", [D, 4096], F32, kind="ExternalInput").ap()
    wff2_d = dt("wff2", [4096, D], F32, kind="ExternalInput").ap()
    lnbc_d = dt("lnbc", [128, 6, D], F32, kind="ExternalInput").ap()
    pp_d = dt("pp", [128, NP], F32, kind="ExternalInput").ap()
    bias_d = dt("biasT", [128, 8, 512], F32, kind="ExternalInput").ap()
    cst_d = dt("cst", [128, 6, 128], F32, kind="ExternalInput").ap()
    out_d = dt("out", [4096, D], F32, kind="ExternalOutput").ap()
    h1_d = dt("h1s", [4096, D], F32, kind="Internal").ap()

    with ExitStack() as st:
        P = Prog(nc, st)
        P.maxops = cfg.get('maxops')
        P.allsig = cfg.get('allsig', False)

        ARENA = 222000
        arena = st.enter_context(nc.sbuf_tensor("arena", [128, ARENA // 2], BF16))
        bump = [0]

        class _T:
            def __init__(self, ap):
                self.ap = ap

            def __getitem__(self, k):
                return self.ap[k]

        def SB(name, shape, dtype):
            n = 1
            for s_ in shape[1:]:
                n *= s_
            nbytes = n * (4 if dtype == F32 else 2)
            nbytes = (nbytes + 31) // 32 * 32
            o = bump[0]
            bump[0] += nbytes
            assert bump[0] <= ARENA, (name, bump[0])
            v = arena[:, o // 2:(o + nbytes) // 2]
            if dtype == F32:
                v = v.bitcast(F32)
            v = v[0:shape[0], 0:n]
            if len(shape) == 3:
                v = v.rearrange("p (a b) -> p a b", a=shape[1])
            elif len(shape) == 4:
                v = v.rearrange("p (a b c) -> p a b c", a=shape[1], b=shape[2])
            return _T(v)

        def PS(name):
            return st.enter_context(nc.psum_tensor(name, [128, 512], F32))

        def same2(t):
            return [t, t]
        wbig = SB("wbig", [128, 65536], BF16)
        Rwk = [Res(f"wk{k}") for k in range(8)]; Rwo = Res("wo"); Rf2 = [Res(f"f2_{k}") for k in range(16)]
        w_in = wbig[:, 0:8 * NCOL].rearrange("p (k c) -> p k c", k=8)
        w_out = wbig[:, 8 * NCOL:8 * NCOL + 8192].rearrange("p (k c) -> p k c", k=8)
        wlora = _T(wbig[:, 61440:62464])
        wg2 = _T(wbig[:, 62464:63488])
        w_ff1 = wbig[:, 0:32768].rearrange("p (k c) -> p k c", k=8)
        w_ff2 = wbig[:, 32768:65536].rearrange("p (k c) -> p k c", k=32)
        Rwl = Res("wl")
        lnbc = SB("lnA", [128, 2, D], F32); Rlnbc = Res("lnbc")
        pp = SB("pp_sb", [128, NP], F32); Rpp = Res("pp")
        cst = SB("cst_sb", [128, 6, 128], F32); Rcst = Res("cst")
        identb = SB("identb", [128, 128], BF16)
        onesb = SB("onesb", [128, 128], BF16)
        blkb = SB("blkb", [128, 128], BF16)
        sinkexp = SB("sinkexp", [128, 8], F32)
        epsb = SB("epsb", [128, 4], F32); Reps = Res("eps")
        ident_f = cst[:, 0, :]
        mask2 = cst[:, 1:3, :]
        mask_ls = cst[:, 3, :]
        ones_f = cst[:, 5, :]
        stt6 = SB("stt6", [128, 2, 6], F32); Rst = Res("st")
        mv = SB("mv", [128, 2], F32); Rmv = Res("mv")
        rstd = SB("rstd", [128, 1], F32); Rrstd = Res("rstd")
        xt = same2(SB("xt", [128, D], F32)); Rxt = same2(Res("xt"))
        pre = xt[0]; Rpre = Rxt[0]
        xnb = SB("xnb", [128, D], BF16); Rxnb = Res("xnb")
        hres = [SB(f"hres{j}", [128, D], F32) for j in range(2)]; Rhres = [Res(f"hres{j}") for j in range(2)]
        hT = same2(SB("hT", [128, 8, 128], BF16)); RhT = same2(Res("hT"))
        mark = bump[0]
        biasT = SB("bias_sb", [128, 8, 512], BF16); Rbias = Res("bias")
        kT = [SB(f"kT{j}", [128, 128], BF16) for j in range(3)]; RkT = [Res(f"kT{j}") for j in range(3)]
        vat = [SB(f"vat{j}", [128, 128], BF16) for j in range(3)]; Rvat = [Res(f"vat{j}") for j in range(3)]
        qT = same2(SB("qT", [128, 8, 2, 128], BF16)); RqT = same2([Res(f"qT_{i}") for i in range(8)])
        gat = same2(SB("ga", [128, 8, 128], BF16)); Rga = same2([Res(f"ga_{i}") for i in range(8)])
        grw = same2(SB("gr", [128, 8, 128], BF16)); Rgr = same2([Res(f"gr_{i}") for i in range(8)])
        rkv = SB("rkv", [128, 8, 3, 128], F32); Rrkv = [Res(f"rkv{i}") for i in range(8)]
        car = SB("car", [128, 2, 26], F32); Rcar = Res("car")
        zt = [SB(f"zt{j}", [128, 3, 129], F32) for j in range(2)]; Rzt = [Res(f"zt{j}") for j in range(2)]
        zd = [SB(f"zd{j}", [128, 3, 128], F32) for j in range(1)]; Rzd = [Res(f"zd{j}") for j in range(1)]
        zAG = SB("zAG", [128, 2, 128], F32); RzAG = Res("zAG")
        lor = [SB(f"lor{j}", [128, 3, 128], BF16) for j in range(2)]; Rlor = [Res(f"lor{j}") for j in range(2)]
        merged = SB("merged", [128, 8, 128], BF16); Rmg = [Res(f"mg{i}") for i in range(8)]

        def dbl(name, shape, dtype):
            return same2(SB(name, shape, dtype)), same2(Res(name))
        sig_, Rsig = dbl("sig", [128, 128], F32)
        aa_, Raa = dbl("aa", [128, 128], F32)
        gg_, Rgg = dbl("gg", [128, 128], F32)
        cum_, Rcum = dbl("cum", [128, 128], F32)
        epos_, Repos = dbl("epos", [128, 128], F32)
        eneg_, Reneg = dbl("eneg", [128, 128], F32)
        eprv_, Reprv = dbl("eprv", [128, 128], F32)
        eend_, Reend = dbl("eend", [128, 128], F32)
        nb_, Rnb = dbl("nb", [128, 2], F32)
        kk_, Rkk = dbl("kk", [128, 128], F32)
        kk2_, Rkk2 = dbl("kk2", [128, 128], BF16)
        rs_, Rrs = dbl("rs", [128, 128], F32)
        tmp_, Rtmp = rs_, Rrs
        ka_, Rka = dbl("ka", [128, 128], F32)
        kp_, Rkp = dbl("kp", [128, 128], F32)
        AR_, RAR = dbl("AR", [128, 2, 128], BF16)
        Bt_, RBt = dbl("Bt", [128, 128], BF16)
        Kt_, RKt = dbl("Kt", [128, 128], BF16)
        BKg_, RBKg = dbl("BKg", [128, 3, 128], BF16)
        TM3_, RTM3 = dbl("TM3", [128, 3, 128], BF16)
        rkr_, Rrkr = dbl("rkr", [128, 128], BF16)
        LT_ = [SB(f"LT{g}", [128, 2, 128], BF16) for g in range(2)]; RLT = [Res(f"LT{g}") for g in range(2)]
        KT2_ = [SB(f"KT2{g}", [128, 2, 128], BF16) for g in range(2)]; RKT2 = [Res(f"KT2{g}") for g in range(2)]
        PP_ = [[SB(f"PP{g}_{j}", [128, 2, 128], BF16) for j in range(2)] for g in range(2)]
        RPP = [[Res(f"PP{g}_{j}") for j in range(2)] for g in range(2)]
        MT_ = [[SB(f"MT{g}_{j}", [128, 128], BF16) for j in range(2)] for g in range(2)]
        RMT = [[Res(f"MT{g}_{j}") for j in range(2)] for g in range(2)]
        Wb_ = [SB(f"Wb{g}", [128, 64], BF16) for g in range(2)]; RWb = [Res(f"Wb{g}") for g in range(2)]
        Ub_ = [SB(f"Ub{g}", [128, 64], BF16) for g in range(2)]; RUb = [Res(f"Ub{g}") for g in range(2)]
        Hst = SB("Hst", [128, 8, 64], F32); Hb = SB("Hb", [128, 8, 2, 64], BF16)
        RH = [Res(f"H{i}") for i in range(8)]; RHb = [Res(f"Hb{i}") for i in range(8)]
        att = SB("att", [128, 128], F32); Ratt = Res("att")
        y2_, Ry2 = same2(rs_[0]), same2(Rrs[0])
        gst = SB("gst", [128, 2, 6], F32); Rgst = Res("gst")
        gmv = SB("gmv", [128, 2, 2], F32); Rgmv = Res("gmv")
        grs = SB("grs", [128, 2], F32); Rgrs = Res("grs")
        ynb = SB("ynb", [128, 128], BF16); Rynb = Res("ynb")
        yy_, Ryy = dbl("yy", [128, 128], F32)
        PT = SB("PTs", [128, 512], BF16); RPT = Res("PT")
        rden = rs_[0]; Rrden = Rrs[0]
        p1_end = bump[0]
        bump[0] = mark
        lnB = SB("lnB", [128, 2, D], F32)
        h1t = same2(SB("h1t", [128, D], F32)); Rh1t = same2(Res("h1t"))
        uT = SB("uT", [128, 32, 128], BF16); RuT = [Res(f"uT{j}") for j in range(8)]
        urelu = [SB(f"urelu{j}", [128, 512], F32) for j in range(2)]; Rurelu = [Res(f"urelu{j}") for j in range(2)]
        print("SBUF bytes: shared", mark, "phase1 end", p1_end, "phase2 end", bump[0])

        pb = [PS(f"pb{j}") for j in range(8)]
        Rpb = [Res(f"pb{j}") for j in range(8)]
        pb_bf = [p.bitcast(BF16) if hasattr(p, 'bitcast') else None for p in pb]

        P.dma('sp', pp[:], pp_d, writes=[Rpp])
        P.dma('sp', cst[:], cst_d, writes=[Rcst])
        P.dma('sp', lnbc[:], lnbc_d[:, 0:2, :], writes=[Rlnbc])
        for i_ in range(8):
            P.dma('pool', biasT[:, i_, :], bias_d[:, i_, :], writes=[Rbias])
        P.dma('pool', wlora[:], wlora_d, writes=[Rwl])
        P.dma('pool', wg2[:], wg2_d, writes=[Rwl])
        wcat_v = wcat_d.rearrange("(k p) c -> p k c", p=128)
        for kc in range(8):
            for c0 in range(0, NCOL, 2048):
                c1 = min(NCOL, c0 + 2048)
                P.dma('pool', w_in[:, kc, c0:c1], wcat_v[:, kc, c0:c1], writes=[Rwk[kc]])
        wout_v = wout_d.rearrange("(k p) c -> p k c", p=128)
        for kc in range(8):
            P.dma('pool', w_out[:, kc, :], wout_v[:, kc, :], writes=[Rwo])
        Rk = Res("consts")
        P.cp('dve', identb[:], ident_f, [Rcst], [Rk])
        P.cp('dve', onesb[:], ones_f, [Rcst], [Rk])
        P.cp('dve', blkb[:], cst[:, 4, :], [Rcst], [Rk])
        P.act(sinkexp[:], pp[:, 98:106], AF.Exp, [Rpp], [Rk])
        P.ts('pool', lnbc[:, 0, :], lnbc[:, 0, :], ALPHA, None, ALU.mult, None, [Rlnbc], [Rlnbc])
        P.ts('pool', lnbc[:, 1, :], lnbc[:, 1, :], ALPHA, None, ALU.mult, None, [Rlnbc], [Rlnbc])
        P.memset('pool', car[:], 0.0, [Rcar])
        P.memset('pool', xt[0][:], 0.0, [Rxt[0]])
        P.memset('dve', Hst[:], 0.0, RH)
        P.memset('dve', Hb[:], 0.0, RHb)
        P.memset('pool', qT[0][:], 0.0, RqT[0])
        P.memset('pool', lor[0][:], 0.0, [Rlor[0]])
        P.memset('pool', lor[1][:], 0.0, [Rlor[1]])

        zt_i = [0]
        zd_i = [0]

        def layer_norm_stats(src, Rsrc, eps):
            P.op('dve', lambda e: e.bn_stats(out=stt6[:, 0, :], in_=src[:, 0:512]), [Rsrc], [Rst])
            P.op('dve', lambda e: e.bn_stats(out=stt6[:, 1, :], in_=src[:, 512:1024]), [Rsrc], [Rst])
            P.op('dve', lambda e: e.bn_aggr(out=mv[:], in_=stt6[:].rearrange("p a b -> p (a b)")), [Rst], [Rmv])
            P.act(rstd[:], mv[:, 1:2], AF.Sqrt, [Rmv], [Rrstd], bias=eps_ap(eps), scale=1.0)
            P.op('dve', lambda e: e.reciprocal(out=rstd[:], in_=rstd[:]), [Rrstd], [Rrstd])

        P.memset('pool', epsb[:, 0:1], LN_EPS, [Reps])
        P.memset('pool', epsb[:, 1:2], GN_EPS, [Reps])
        P.memset('pool', epsb[:, 2:3], 1e-16, [Reps])

        def eps_ap(eps):
            return epsb[:, 0:1] if eps == LN_EPS else epsb[:, 1:2]

        def token_shift(psrc, Rpsrc, c0, nb, j0, mu_ap, dst, Rdst, n):
            k = zt_i[0] % 2; zt_i[0] += 1
            kd = 0
            z = zt[k]
            pc, pn_ = (n - 1) % 2, n % 2
            P.cp('act', z[:, 0:nb, 1:129], psrc[:, c0:c0 + nb * 128].rearrange("p (a b) -> p a b", a=nb), [Rpsrc], [Rzt[k]])
            P.cp('pool', z[:, 0:nb, 0], car[:, pc, j0:j0 + nb], [Rcar], [Rzt[k]])
            P.cp('pool', car[:, pn_, j0:j0 + nb], z[:, 0:nb, 128], [Rzt[k]], [Rcar])
            d = zd[kd]
            P.tt('dve', d[:, 0:nb, :], z[:, 0:nb, 0:128], z[:, 0:nb, 1:129], ALU.subtract, [Rzt[k]], [Rzd[kd]])
            P.tt('dve', d[:, 0:nb, :], d[:, 0:nb, :], mu_ap, ALU.mult, [Rzd[kd], Rpp], [Rzd[kd]])
            P.tt('pool', dst, d[:, 0:nb, :], z[:, 0:nb, 1:129], ALU.add, [Rzd[kd], Rzt[k]], [Rdst])

        pj_i = [0]

        def proj_bank():
            b = 1 + (pj_i[0] % 2); pj_i[0] += 1
            return b

        def front(n):
            par = n % 2
            if n >= 1:
                P.dma('sp', xt[par][:], x_d[(n - 1) * 128:n * 128, :], writes=[Rxt[par]])
            else:
                P.dma('sp', xt[0][112:128, :], meta_d, writes=[Rxt[0]])
            layer_norm_stats(xt[par], Rxt[par], LN_EPS)
            P.ts('dve', hres[par][:], xt[par][:], mv[:, 0:1], rstd[:, 0:1], ALU.subtract, ALU.mult, [Rxt[par], Rmv, Rrstd], [Rhres[par]])
            P.cp('pool', xnb[:], hres[par][:], [Rhres[par]], [Rxnb])
            if n >= 1:
                P.tt('pool', hres[par][:], hres[par][:], lnbc[:, 0, :], ALU.mult, [Rhres[par], Rlnbc], [Rhres[par]])
                P.tt('pool', hres[par][:], hres[par][:], lnbc[:, 1, :], ALU.add, [Rhres[par], Rlnbc], [Rhres[par]])
            yield
            ptb = pb[0].bitcast(BF16)
            for kc in range(8):
                P.tr(ptb[:, kc * 128:(kc + 1) * 128], xnb[:, kc * 128:(kc + 1) * 128], identb[:], [Rxnb, Rk], [Rpb[0]], sig=(kc == 7))
            for kc in range(8):
                P.act(hT[par][:, kc, :], ptb[:, kc * 128:(kc + 1) * 128], AF.Identity, [Rpb[0], Rpp], [RhT[par]],
                      bias=pp[:, 8 + kc:9 + kc], scale=pp[:, kc:kc + 1])
            if n == 0:
                P.memset('pool', hT[0][:, :, 0:112], 0.0, [RhT[0]])
            yield

            def fm_group(cols, b):
                for s_, cb in enumerate(cols):
                    for kc in range(8):
                        P.mm(pb[b][:, s_ * 128:(s_ + 1) * 128], w_in[:, kc, cb:cb + 128], hT[par][:, kc, :],
                             kc == 0, kc == 7, [Rwk[kc], RhT[par]], [Rpb[b]], sig=(kc == 7 and s_ == len(cols) - 1))

            b = proj_bank()
            for s_, cb in enumerate((0, 256, 384)):
                for kc in range(8):
                    P.mm(pb[b][:, s_ * 128:(s_ + 1) * 128], w_in[:, kc, cb:cb + 128], hT[par][:, kc, :],
                         kc == 0, kc == 7, [Rwk[kc], RhT[par]], [Rpb[b]], sig=False)
            for kc in range(8):
                P.mm(pb[b][:, 384:512], hT[par][:, kc, :], w_in[:, kc, 128:256], kc == 0, kc == 7,
                     [Rwk[kc], RhT[par]], [Rpb[b]], sig=(kc == 7))
            P.cp('act', kT[n % 3][:], pb[b][:, 0:128], [Rpb[b]], [RkT[n % 3]])
            P.cp('act', vat[n % 3][:], pb[b][:, 384:512], [Rpb[b]], [Rvat[n % 3]])
            token_shift(pb[b], Rpb[b], 128, 2, 24, pp[:, 40:42].unsqueeze(2).to_broadcast([128, 2, 128]),
                        zAG[:], RzAG, n)
            P.act(lor[par][0:64, 0, :], zAG[0:64, 0, :], AF.Tanh, [RzAG], [Rlor[par]])
            P.cp('dve', lor[par][64:128, 1, :], zAG[64:128, 0, :], [RzAG], [Rlor[par]])
            P.act(lor[par][:, 2, :], zAG[:, 1, :], AF.Sigmoid, [RzAG], [Rlor[par]])
            yield
            for i in range(8):
                yield ('need', i)
                base = 512 + i * 768
                b = proj_bank()
                fm_group((base, base + 128, base + 256), b)
                P.act(qT[par][0:64, i, 0, :], pb[b][0:64, 0:128], AF.Identity, [Rpb[b]], [RqT[par][i]], scale=0.125)
                P.act(qT[par][64:128, i, 1, :], pb[b][64:128, 0:128], AF.Identity, [Rpb[b]], [RqT[par][i]], scale=0.125)
                P.act(gat[par][:, i, :], pb[b][:, 128:256], AF.Sigmoid, [Rpb[b]], [Rga[par][i]])
                P.act(grw[par][:, i, :], pb[b][:, 256:384], AF.Sigmoid, [Rpb[b]], [Rgr[par][i]])
                yield
                b = proj_bank()
                fm_group((base + 384, base + 512, base + 640), b)
                token_shift(pb[b], Rpb[b], 0, 3, i * 3,
                            pp[:, 16 + i * 3:19 + i * 3].unsqueeze(2).to_broadcast([128, 3, 128]),
                            rkv[:, i, :, :], Rrkv[i], n)
                yield

        def mix(n):
            par = n % 2
            ppar = (n - 1) % 2
            for i in range(8):
                q_ = i % 2
                r_s = rkv[:, i, 0, :]; k_s = rkv[:, i, 1, :]; v_s = rkv[:, i, 2, :]
                Rr = Rrkv[i]
                b = 3
                P.mm(pb[b][:, 0:128], wlora[:, i * 128:(i + 1) * 128], lor[par][:, 0, :], True, True, [Rwl, Rlor[par]], [Rpb[b]], sig=False)
                P.mm(pb[b][:, 128:256], wlora[:, i * 128:(i + 1) * 128], lor[par][:, 1, :], True, True, [Rwl, Rlor[par]], [Rpb[b]], sig=False)
                P.mm(pb[b][:, 256:384], wg2[:, i * 128:(i + 1) * 128], lor[par][:, 2, :], True, True, [Rwl, Rlor[par]], [Rpb[b]])
                P.act(sig_[q_][:], pb[b][:, 0:128], AF.Sigmoid, [Rpb[b], Rpp], [Rsig[q_]], bias=pp[:, 42 + i:43 + i], scale=1.0)
                P.act(aa_[q_][:], pb[b][:, 128:256], AF.Sigmoid, [Rpb[b], Rpp], [Raa[q_]], bias=pp[:, 50 + i:51 + i], scale=1.0)
                P.cp('act', gg_[q_][:], pb[b][:, 256:384], [Rpb[b]], [Rgg[q_]])
                P.op('dve', lambda e, o=cum_[q_], s=sig_[q_]: e.tensor_tensor_scan(out=o[:], data0=ones_f, data1=s[:], initial=0.0, op0=ALU.mult, op1=ALU.add),
                     [Rsig[q_], Rcst], [Rcum[q_]])
                P.tt('pool', tmp_[q_][:], cum_[q_][:], sig_[q_][:], ALU.subtract, [Rcum[q_], Rsig[q_]], [Rtmp[q_]])
                P.ts('dve', nb_[q_][:, 0:1], cum_[q_][:, 127:128], -DK, None, ALU.mult, None, [Rcum[q_]], [Rnb[q_]])
                P.act(epos_[q_][:], cum_[q_][:], AF.Exp, [Rcum[q_]], [Repos[q_]], scale=-DK)
                P.act(eneg_[q_][:], cum_[q_][:], AF.Exp, [Rcum[q_]], [Reneg[q_]], scale=DK)
                P.act(eprv_[q_][:], tmp_[q_][:], AF.Exp, [Rtmp[q_]], [Reprv[q_]], scale=-DK)
                P.act(eend_[q_][:], cum_[q_][:], AF.Exp, [Rcum[q_], Rnb[q_]], [Reend[q_]], bias=nb_[q_][:, 0:1], scale=DK)
                P.act(nb_[q_][:, 1:2], cum_[q_][:, 127:128], AF.Exp, [Rcum[q_]], [Rnb[q_]], scale=-DK)
                P.ts('dve', kk_[q_][:], k_s, pp[:, 58 + i:59 + i], None, ALU.mult, None, [Rr, Rpp], [Rkk[q_]])
                P.tt('pool', kk2_[q_][:], kk_[q_][:], kk_[q_][:], ALU.mult, [Rkk[q_]], [Rkk2[q_]])
                P.mm(pb[b][:, 384:512], blkb[:], kk2_[q_][:], True, True, [Rk, Rkk2[q_]], [Rpb[b]])
                P.act(rs_[q_][:], pb[b][:, 384:512], AF.Ln, [Rpb[b], Reps], [Rrs[q_]], bias=epsb[:, 2:3], scale=1.0)
                P.act(rs_[q_][:], rs_[q_][:], AF.Exp, [Rrs[q_]], [Rrs[q_]], scale=-0.5)
                P.tt('dve', kk_[q_][:], kk_[q_][:], rs_[q_][:], ALU.mult, [Rkk[q_], Rrs[q_]], [Rkk[q_]])
                P.tt('pool', ka_[q_][:], kk_[q_][:], aa_[q_][:], ALU.mult, [Rkk[q_], Raa[q_]], [Rka[q_]])
                P.ts('dve', kp_[q_][:], aa_[q_][:], -1.0, pp[:, 66 + i:67 + i], ALU.add, ALU.mult, [Raa[q_], Rpp], [Rkp[q_]])
                P.stt(kp_[q_][:], kp_[q_][:], 1.0, k_s, ALU.add, ALU.mult, [Rkp[q_], Rr], [Rkp[q_]])
                P.stt(AR_[q_][:, 0, :], kk_[q_][:], -1.0, eprv_[q_][:], ALU.mult, ALU.mult, [Rkk[q_], Reprv[q_]], [RAR[q_]])
                P.tt('pool', AR_[q_][:, 1, :], r_s, epos_[q_][:], ALU.mult, [Rr, Repos[q_]], [RAR[q_]])
                P.tt('dve', Bt_[q_][:], ka_[q_][:], eneg_[q_][:], ALU.mult, [Rka[q_], Reneg[q_]], [RBt[q_]])
                P.tt('pool', Kt_[q_][:], kp_[q_][:], eneg_[q_][:], ALU.mult, [Rkp[q_], Reneg[q_]], [RKt[q_]])
                P.tt('dve', BKg_[q_][:, 0, :], ka_[q_][:], eend_[q_][:], ALU.mult, [Rka[q_], Reend[q_]], [RBKg[q_]])
                P.tt('pool', BKg_[q_][:, 1, :], kp_[q_][:], eend_[q_][:], ALU.mult, [Rkp[q_], Reend[q_]], [RBKg[q_]])
                P.cp('pool', BKg_[q_][:, 2, :], v_s, [Rr], [RBKg[q_]])
                P.stt(rkr_[q_][:], r_s, pp[:, 74 + i:75 + i], kp_[q_][:], ALU.mult, ALU.mult, [Rr, Rpp, Rkp[q_]], [Rrkr[q_]])
                yield
                if cfg.get("mix_stop", 99) <= 1:
                    return
                ptb = pb[0].bitcast(BF16)
                for j in range(3):
                    P.tr(ptb[:, j * 128:(j + 1) * 128], BKg_[q_][:, j, :], identb[:], [RBKg[q_], Rk], [Rpb[0]], sig=(j == 2))
                P.cp('act', TM3_[q_][:].rearrange("p a b -> p (a b)"), ptb[:, 0:384], [Rpb[0]], [RTM3[q_]])
                for g in range(2):
                    gs = slice(g * 64, (g + 1) * 64)
                    ba = 4 + g
                    bd = 6 + g
                    ARv = AR_[q_][gs, :, :].rearrange("p a b -> p (a b)")
                    P.mm(pb[ba][:, 0:256], Bt_[q_][gs, :], ARv, True, True, [RBt[q_], RAR[q_]], [Rpb[ba]], sig=False)
                    P.mm(pb[ba][:, 256:512], Kt_[q_][gs, :], ARv, True, True, [RKt[q_], RAR[q_]], [Rpb[ba]])
                    P.mm(pb[bd][:, 384:512], AR_[q_][gs, 0, :], Bt_[q_][gs, :], True, True, [RBt[q_], RAR[q_]], [Rpb[bd]])
                    P.tt('dve', LT_[g][:].rearrange("p a b -> p (a b)"), pb[ba][:, 0:256], mask2.rearrange("p a b -> p (a b)"), ALU.mult,
                         [Rpb[ba], Rcst], [RLT[g]])
                    P.tt('dve', KT2_[g][:].rearrange("p a b -> p (a b)"), pb[ba][:, 256:512], mask2.rearrange("p a b -> p (a b)"), ALU.mult,
                         [Rpb[ba], Rcst], [RKT2[g]])
                    P.tt('dve', PP_[g][0][:, 0, :], pb[bd][:, 384:512], mask_ls, ALU.mult, [Rpb[bd], Rcst], [RPP[g][0]])
                    P.cp('pool', PP_[g][0][:, 1, :], LT_[g][:, 0, :], [RLT[g]], [RPP[g][0]])
                    P.tt('pool', MT_[g][0][:], LT_[g][:, 0, :], identb[:], ALU.add, [RLT[g], Rk], [RMT[g][0]])
                yield
                if cfg.get("mix_stop", 99) <= 2:
                    return
                bl = 3
                bm = 4

                def att_qk():
                    for g in range(2):
                        for sh in range(2):
                            slot = (n - 1) % 3 if sh == 0 else n % 3
                            c0 = (g * 2 + sh) * 128
                            P.mm(pb[bl][:, c0:c0 + 128], kT[slot][:, :], qT[par][:, i, g, :], True, True, [RkT[slot], RqT[par][i]], [Rpb[bl]],
                                 sig=(g == 1 and sh == 1))
                    P.tt('dve', pb[bl][:], pb[bl][:], biasT[:, i, :], ALU.add, [Rpb[bl], Rbias], [Rpb[bl]])

                def att_exp():
                    if n == 1:
                        lbv = pb[bl][:].rearrange("p (g s q) -> p g s q", g=2, s=2)
                        PTv = PT[:].rearrange("p (g s q) -> p g s q", g=2, s=2)
                        P.act(PTv[:, :, 0, :], lbv[:, :, 0, :], AF.Exp, [Rpb[bl], Rpp], [RPT], bias=pp[:, 106:107], scale=1.0)
                        P.act(PTv[:, :, 1, :], lbv[:, :, 1, :], AF.Exp, [Rpb[bl]], [RPT])
                    else:
                        P.act(PT[:], pb[bl][:], AF.Exp, [Rpb[bl]], [RPT])

                def att_pv():
                    for g in range(2):
                        gs = slice(g * 64, (g + 1) * 64)
                        for sh in range(2):
                            slot = (n - 1) % 3 if sh == 0 else n % 3
                            c0 = (g * 2 + sh) * 128
                            P.mm(pb[bm][gs, 0:128], vat[slot][:, gs], PT[:, c0:c0 + 128], sh == 0, sh == 1, [Rvat[slot], RPT], [Rpb[bm]], sig=False)
                        for sh in range(2):
                            c0 = (g * 2 + sh) * 128
                            P.mm(pb[bm][gs, 128:256], onesb[:, 0:64], PT[:, c0:c0 + 128], sh == 0, sh == 1, [Rk, RPT], [Rpb[bm]],
                                 sig=(g == 1 and sh == 1))
                    P.ts('dve', rden[:], pb[bm][:, 128:256], sinkexp[:, i:i + 1], None, ALU.add, None, [Rpb[bm], Rk], [Rrden])

                def att_norm():
                    P.op('dve', lambda e: e.reciprocal(out=rden[:], in_=rden[:]), [Rrden], [Rrden])
                    P.tt('dve', att[:], pb[bm][:, 0:128], rden[:], ALU.mult, [Rpb[bm], Rrden], [Ratt])
                    P.tt('pool', att[:], att[:], gat[par][:, i, :], ALU.mult, [Ratt, Rga[par][i]], [Ratt])
                att_pieces = {1: att_qk, 2: att_exp, 3: att_pv, 4: att_norm} if n >= 1 else {}
                for lev in range(1, 7):
                    src_ = (lev - 1) % 2
                    dst = lev % 2
                    for g in range(2):
                        bd = 6 + g
                        Pm = PP_[g][src_][:, 0, :]; PTm = PP_[g][src_][:, 1, :]
                        last = (lev == 6)
                        P.mm(pb[bd][:, 0:128], PTm, Pm, True, True, [RPP[g][src_]], [Rpb[bd]], sig=last)
                        if not last:
                            P.mm(pb[bd][:, 128:256], Pm, PTm, True, True, [RPP[g][src_]], [Rpb[bd]])
                            P.cp('act', PP_[g][dst][:].rearrange("p a b -> p (a b)"), pb[bd][:, 0:256], [Rpb[bd]], [RPP[g][dst]])
                        else:
                            P.cp('act', PP_[g][dst][:, 0, :], pb[bd][:, 0:128], [Rpb[bd]], [RPP[g][dst]])
                        P.mm(pb[bd][:, 256:384], PP_[g][dst][:, 0, :], MT_[g][src_][:], True, True, [RPP[g][dst], RMT[g][src_]], [Rpb[bd]])
                        P.tt('dve', MT_[g][dst][:], pb[bd][:, 256:384], MT_[g][src_][:], ALU.add, [Rpb[bd], RMT[g][src_]], [RMT[g][dst]])
                    if lev in att_pieces:
                        att_pieces[lev]()
                    yield
                if cfg.get("mix_stop", 99) <= 3:
                    return
                MTf = [MT_[g][0] for g in range(2)]
                RMTf = [RMT[g][0] for g in range(2)]
                bs_ = 5
                for g in range(2):
                    gs = slice(g * 64, (g + 1) * 64)
                    Vt_g = TM3_[q_][:, 2, gs]
                    P.mm(pb[bs_][:, g * 64:(g + 1) * 64], AR_[q_][:, 0, :], Hb[:, i, g, :], True, False, [RAR[q_], RHb[i]], [Rpb[bs_]], sig=False)
                    P.mm(pb[bs_][:, g * 64:(g + 1) * 64], KT2_[g][:, 0, :], Vt_g, False, True, [RKT2[g], RTM3[q_]], [Rpb[bs_]])
                    P.cp('act', Wb_[g][:], pb[bs_][:, g * 64:(g + 1) * 64], [Rpb[bs_]], [RWb[g]])
                for g in range(2):
                    P.mm(pb[bs_][:, 128 + g * 64:128 + (g + 1) * 64], MTf[g][:], Wb_[g][:], True, True, [RMTf[g], RWb[g]], [Rpb[bs_]])
                    P.cp('dve', Ub_[g][:], pb[bs_][:, 128 + g * 64:128 + (g + 1) * 64], [Rpb[bs_]], [RUb[g]])
                for g in range(2):
                    gs = slice(g * 64, (g + 1) * 64)
                    Vt_g = TM3_[q_][:, 2, gs]
                    oc = slice(256 + g * 64, 256 + (g + 1) * 64)
                    P.mm(pb[bs_][:, oc], AR_[q_][:, 1, :], Hb[:, i, g, :], True, False, [RAR[q_], RHb[i]], [Rpb[bs_]], sig=False)
                    P.mm(pb[bs_][:, oc], LT_[g][:, 1, :], Ub_[g][:], False, False, [RLT[g], RUb[g]], [Rpb[bs_]], sig=False)
                    P.mm(pb[bs_][:, oc], KT2_[g][:, 1, :], Vt_g, False, True, [RKT2[g], RTM3[q_]], [Rpb[bs_]], sig=False)
                    P.mm(pb[bs_][gs, 384:448], TM3_[q_][:, 0, gs], Ub_[g][:], True, False, [RTM3[q_], RUb[g]], [Rpb[bs_]], sig=False)
                    P.mm(pb[bs_][gs, 384:448], TM3_[q_][:, 1, gs], Vt_g, False, True, [RTM3[q_]], [Rpb[bs_]], sig=(g == 1))
                P.stt(Hst[:, i, :], Hst[:, i, :], nb_[q_][:, 1:2], pb[bs_][:, 384:448], ALU.mult, ALU.add, [RH[i], Rnb[q_], Rpb[bs_]], [RH[i]])
                P.cp('act', Hb[0:64, i, 0, :], Hst[0:64, i, :], [RH[i]], [RHb[i]])
                P.cp('act', Hb[64:128, i, 1, :], Hst[64:128, i, :], [RH[i]], [RHb[i]])
                for g in range(2):
                    oc = slice(256 + g * 64, 256 + (g + 1) * 64)
                    P.op('dve', lambda e, g=g, oc=oc: e.bn_stats(out=gst[:, g, :], in_=pb[bs_][:, oc]), [Rpb[bs_]], [Rgst])
                for g in range(2):
                    P.op('dve', lambda e, g=g: e.bn_aggr(out=gmv[:, g, :], in_=gst[:, g, :]), [Rgst], [Rgmv])
                P.act(grs[:], gmv[:, :, 1], AF.Sqrt, [Rgmv, Reps], [Rgrs], bias=epsb[:, 1:2], scale=1.0)
                P.op('dve', lambda e: e.reciprocal(out=grs[:], in_=grs[:]), [Rgrs], [Rgrs])
                for g in range(2):
                    oc = slice(256 + g * 64, 256 + (g + 1) * 64)
                    P.ts('dve', ynb[:, g * 64:(g + 1) * 64], pb[bs_][:, oc], gmv[:, g, 0:1], grs[:, g:g + 1], ALU.subtract, ALU.mult,
                         [Rpb[bs_], Rgmv, Rgrs], [Rynb])
                yield
                if cfg.get("mix_stop", 99) <= 4:
                    return
                bm = 4
                pmb = pb[bm].bitcast(BF16)
                P.tr(pmb[:, 768:896], ynb[:], identb[:], [Rynb, Rk], [Rpb[bm]])
                P.mm(pb[bm][:, 256:384], blkb[:], rkr_[q_][:], True, True, [Rk, Rrkr[q_]], [Rpb[bm]])
                P.ts('dve', yy_[q_][:], pmb[:, 768:896], pp[:, 82 + i:83 + i], pp[:, 90 + i:91 + i], ALU.mult, ALU.add, [Rpb[bm], Rpp], [Ryy[q_]])
                P.tt('dve', y2_[q_][:], pb[bm][:, 256:384], v_s, ALU.mult, [Rpb[bm], Rr], [Ry2[q_]])
                if cfg.get('dbgbs') and i == 7:
                    P.cp('dve', kk_[0][:], pb[bm][:, 256:384], [Rpb[bm]], [Rkk[0]])
                P.tt('pool', yy_[q_][:], yy_[q_][:], y2_[q_][:], ALU.add, [Ryy[q_], Ry2[q_]], [Ryy[q_]])
                P.tt('pool', yy_[q_][:], yy_[q_][:], gg_[q_][:], ALU.mult, [Ryy[q_], Rgg[q_]], [Ryy[q_]])
                if n == 0:
                    yield ('done', i)
                    continue
                P.tt('pool', yy_[q_][:], yy_[q_][:], grw[par][:, i, :], ALU.mult, [Ryy[q_], Rgr[par][i]], [Ryy[q_]])
                P.tt('pool', merged[:, i, :], att[:], yy_[q_][:], ALU.add, [Ratt, Ryy[q_]], [Rmg[i]])
                yield ('done', i)
                if cfg.get("mix_stop", 99) <= 6:
                    return
            if n == 0:
                return
            for hh in range(2):
                b = 1 + hh
                for i in range(8):
                    P.mm(pb[b][:], merged[:, i, :], w_out[:, i, hh * 512:(hh + 1) * 512], i == 0, i == 7, [Rmg[i], Rwo], [Rpb[b]], sig=(i == 7))
                P.stt(pre[:, hh * 512:(hh + 1) * 512], xn_dummy(hres[par], hh), 1.0, pb[b][:], ALU.mult, ALU.add, [Rhres[par], Rpb[b]], [Rpre])
            yield
            layer_norm_stats(pre, Rpre, LN_EPS)
            P.ts('dve', pre[:], pre[:], mv[:, 0:1], rstd[:, 0:1], ALU.subtract, ALU.mult, [Rpre, Rmv, Rrstd], [Rpre])
            P.dma('sp', h1_d[(n - 1) * 128:n * 128, :], pre[:], reads=[Rpre], writes=[Rscr])
            yield

        def xn_dummy(t, hh):
            return t[:, hh * 512:(hh + 1) * 512]

        Rscr = Res("scratch")

        def ffn(n):
            par = n % 2
            P.dma('sp', h1t[par][:], h1_d[(n - 1) * 128:n * 128, :], reads=[Rscr], writes=[Rh1t[par]])
            P.tt('pool', h1t[par][:], h1t[par][:], lnbc[:, 0, :], ALU.mult, [Rh1t[par], Rlnbc], [Rh1t[par]])
            P.tt('pool', h1t[par][:], h1t[par][:], lnbc[:, 1, :], ALU.add, [Rh1t[par], Rlnbc], [Rh1t[par]])
            P.cp('pool', xnb[:], h1t[par][:], [Rh1t[par]], [Rxnb])
            ptb = pb[0].bitcast(BF16)
            for kc in range(8):
                P.tr(ptb[:, kc * 128:(kc + 1) * 128], xnb[:, kc * 128:(kc + 1) * 128], identb[:], [Rxnb, Rk], [Rpb[0]], sig=(kc == 7))
            P.cp('act', hT[par][:].rearrange("p a b -> p (a b)"), ptb[:, 0:1024], [Rpb[0]], [RhT[par]])
            yield
            for cg in range(8):
                b = 3 + (cg % 4)
                for s_ in range(4):
                    c = cg * 4 + s_
                    for kc in range(8):
                        P.mm(pb[b][:, s_ * 128:(s_ + 1) * 128], w_ff1[:, kc, c * 128:(c + 1) * 128], hT[par][:, kc, :], kc == 0, kc == 7,
                             [Rwk[kc], RhT[par]], [Rpb[b]], sig=(kc == 7 and s_ == 3))
                u = cg % 2
                P.act(urelu[u][:], pb[b][:], AF.Relu, [Rpb[b]], [Rurelu[u]])
                P.tt('pool' if cg % 2 else 'dve', uT[:, cg * 4:(cg + 1) * 4, :].rearrange("p a b -> p (a b)"), urelu[u][:], urelu[u][:], ALU.mult,
                     [Rurelu[u]], [RuT[cg]])
                yield
            for hh in range(2):
                b = 1 + hh
                for c in range(32):
                    P.mm(pb[b][:], uT[:, c, :], w_ff2[:, c, hh * 512:(hh + 1) * 512], c == 0, c == 31, [RuT[c // 4], Rf2[c // 2]], [Rpb[b]], sig=(c == 31))
                P.stt(pre[:, hh * 512:(hh + 1) * 512], h1t[par][:, hh * 512:(hh + 1) * 512], ALPHA, pb[b][:], ALU.mult, ALU.add,
                      [Rh1t[par], Rpb[b]], [Rpre])
                yield
            layer_norm_stats(pre, Rpre, LN_EPS)
            P.ts('dve', pre[:], pre[:], mv[:, 0:1], rstd[:, 0:1], ALU.subtract, ALU.mult, [Rpre, Rmv, Rrstd], [Rpre])
            P.tt('pool', hres[par][:], pre[:], lnB[:, 0, :], ALU.mult, [Rpre, RlnB], [Rhres[par]])
            P.tt('pool', hres[par][:], hres[par][:], lnB[:, 1, :], ALU.add, [Rhres[par], RlnB], [Rhres[par]])
            P.dma('sp', out_d[(n - 1) * 128:n * 128, :], hres[par][:], reads=[Rhres[par]], is_output=True)
            yield

        def run2(mx, fr):
            mix_done = -1 if mx is not None else 99
            fr_need = -1
            while mx is not None or fr is not None:
                progressed = False
                if mx is not None:
                    try:
                        r = next(mx)
                        if isinstance(r, tuple) and r[0] == 'done':
                            mix_done = r[1]
                    except StopIteration:
                        mx = None
                        mix_done = 99
                    progressed = True
                if fr is not None and fr_need <= mix_done:
                    try:
                        r = next(fr)
                        if isinstance(r, tuple) and r[0] == 'need':
                            fr_need = r[1]
                    except StopIteration:
                        fr = None
                    progressed = True
                assert progressed

        def run(*gens):
            gens = [g for g in gens if g is not None]
            while gens:
                for g in list(gens):
                    try:
                        next(g)
                    except StopIteration:
                        gens.remove(g)

        run(front(0))
        for n in range(nt):
            if cfg.get('seq'):
                run(mix(n))
                if n + 1 < nt:
                    run(front(n + 1))
                continue
            run2(mix(n) if cfg.get('mix', True) else None, front(n + 1) if n + 1 < nt else None)
        dumps = {'hT': (hT[0], [128, 8, 128], BF16, RhT[0]), 'qT': (qT[0], [128, 8, 2, 128], BF16, RqT[0][7]),
                 'rkv': (rkv, [128, 8, 3, 128], F32, Rrkv[7]), 'kT0': (kT[0], [128, 128], BF16, RkT[0]), 'kT1': (kT[1], [128, 128], BF16, RkT[1]),
                 'vat0': (vat[0], [128, 128], BF16, Rvat[0]), 'lor0': (lor[0], [128, 3, 128], BF16, Rlor[0]),
                 'Hst': (Hst, [128, 8, 64], F32, RH[7]), 'Hb': (Hb, [128, 8, 2, 64], BF16, RHb[7]), 'merged': (merged, [128, 8, 128], BF16, Rmg[7]),
                 'pre': (pre, [128, D], F32, Rpre), 'hres1': (hres[1], [128, D], F32, Rhres[1]), 'hres0': (hres[0], [128, D], F32, Rhres[0]),
                 'yy': (yy_[0], [128, 128], F32, Ryy[0]), 'MT0': (MT_[0][0], [128, 128], BF16, RMT[0][0]), 'LT0': (LT_[0], [128, 2, 128], BF16, RLT[0]),
                 'sig': (sig_[0], [128, 128], F32, Rsig[0]), 'kk': (kk_[0], [128, 128], F32, Rkk[0]), 'att': (att, [128, 128], F32, Ratt),
                 'AR': (AR_[0], [128, 2, 128], BF16, RAR[0]), 'TM3': (TM3_[0], [128, 3, 128], BF16, RTM3[0]), 'PT': (PT, [128, 512], BF16, RPT),
                 'ynb': (ynb, [128, 128], BF16, Rynb), 'gg': (gg_[0], [128, 128], F32, Rgg[0]), 'gmv': (gmv, [128, 2, 2], F32, Rgmv), 'grs': (grs, [128, 2], F32, Rgrs), 'rkr': (rkr_[0], [128, 128], BF16, Rrkr[0]), 'kp': (kp_[0], [128, 128], F32, Rkp[0]), 'ga': (gat[0], [128, 8, 128], BF16, Rga[0][7]), 'gr': (grw[0], [128, 8, 128], BF16, Rgr[0][7])}
        print("nops", P.nops)
        P.maxops = None
        for name in cfg.get('dump', []):
            t_, shp, dty, R_ = dumps[name]
            dd = nc.dram_tensor("dbg_" + name, shp, dty, kind="ExternalOutput").ap()
            P.dma('sp', dd, t_[:], reads=[R_], is_output=True)
        if not cfg.get('phase2', True):
            P.finish()
            P.emit()
            return nc
        P.barrier()
        RlnB = Res("lnB")
        P.dma('sp', lnbc[:], lnbc_d[:, 2:4, :], writes=[Rlnbc])
        P.dma('sp', lnB[:], lnbc_d[:, 4:6, :], writes=[RlnB])
        for k in range(8):
            Rwk[k] = Res(f"wk2_{k}")
        wff1_v = wff1_d.rearrange("(k p) c -> p k c", p=128)
        for kc in range(8):
            for c0 in range(0, 4096, 2048):
                P.dma('pool', w_ff1[:, kc, c0:c0 + 2048], wff1_v[:, kc, c0:c0 + 2048], writes=[Rwk[kc]])
        wff2_v = wff2_d.rearrange("(c p) n -> p c n", p=128)
        for c0 in range(0, 32, 2):
            P.dma('pool', w_ff2[:, c0:c0 + 2, :], wff2_v[:, c0:c0 + 2, :], writes=[Rf2[c0 // 2]])
        prev = None
        for n in range(1, nt):
            run(ffn(n))
        P.finish()
        P.emit()
    return nc


def _paired_perm():
    idx = np.empty(1024, np.int64)
    for i in range(8):
        for g in range(2):
            idx[i * 128 + g * 64:i * 128 + (g + 1) * 64] = g * 512 + i * 64 + np.arange(64)
    return idx


def _t5_bucket(dist):
    d = np.maximum(dist, 1).astype(np.float32)
    large = 16 + (np.log(d / np.float32(16)) / np.float32(math.log(128 / 16)) * np.float32(16)).astype(np.int32)
    large = np.minimum(large, 31)
    return np.where(dist < 16, dist, large)


def _prep_shared(inp):
    f32 = np.float32
    PPm = _paired_perm()
    w_in = np.asarray(inp['w_in'], f32)[0]
    cols = [np.arange(1024, 1152), np.arange(1152, 1280), np.arange(4352, 4480), np.arange(4480, 4608)]
    for i in range(8):
        pi = PPm[i * 128:(i + 1) * 128]
        for B in (0, 4608, 5632, 1280, 2304, 3328):
            cols.append(B + pi)
    cols = np.concatenate(cols)
    assert cols.shape[0] == NCOL
    wcat = np.ascontiguousarray(w_in[:, cols])
    wout = np.ascontiguousarray(np.asarray(inp['w_out'], f32)[0][PPm, :])
    wlora = np.ascontiguousarray(np.concatenate([np.asarray(inp['decay_w2'], f32)[0][:, PPm],
                                                 np.asarray(inp['iclr_a2'], f32)[0][:, PPm]], 0))
    wg2 = np.ascontiguousarray(np.asarray(inp['gate_w2'], f32)[0][:, PPm])
    rows = [inp['ln0_g'], inp['ln0_b'], inp['ln1_g'][0], inp['ln1_b'][0], inp['ln2_g'][0], inp['ln2_b'][0]]
    lnbc = np.ascontiguousarray(np.broadcast_to(np.stack([np.asarray(r, f32) for r in rows], 0)[None], (128, 6, 1024)))
    pp = np.zeros((128, NP), f32)
    pp[:, 0:8] = np.asarray(inp['ln0_g'], f32).reshape(8, 128).T
    pp[:, 8:16] = np.asarray(inp['ln0_b'], f32).reshape(8, 128).T
    mu = np.asarray(inp['shift_mu'], f32)[0]
    pair = lambda v: np.asarray(v, f32).reshape(-1)[PPm].reshape(8, 128).T
    for j in range(3):
        pp[:, 16 + j:40:3] = pair(mu[j * 1024:(j + 1) * 1024])
    pp[:, 40] = mu[3072:3200]
    pp[:, 41] = mu[3200:3328]
    pp[:, 42:50] = pair(inp['decay_w0'][0])
    pp[:, 50:58] = pair(inp['iclr_a0'][0])
    pp[:, 58:66] = pair(inp['k_k'][0])
    pp[:, 66:74] = pair(inp['k_a'][0])
    pp[:, 74:82] = pair(np.asarray(inp['r_k'])[0].reshape(-1))
    pp[:, 82:90] = pair(inp['lnx_g'][0])
    pp[:, 90:98] = pair(inp['lnx_b'][0])
    sinks = np.asarray(inp['attn_sinks'], f32)[0]
    for i in range(8):
        pp[0:64, 98 + i] = sinks[i]
        pp[64:128, 98 + i] = sinks[8 + i]
    pp[0:112, 106] = NEG
    rb = np.asarray(inp['rel_bias'], f32)
    s = np.arange(128)[:, None]; q = np.arange(128)[None, :]
    biasT = np.empty((128, 8, 2, 2, 128), f32)
    for sh in range(2):
        dist = q + 128 - (sh * 128 + s)
        inw = (dist >= 0) & (dist < 128)
        bk = _t5_bucket(np.maximum(dist, 0))
        for i in range(8):
            for g in range(2):
                biasT[:, i, g, sh, :] = np.where(inw, rb[bk, g * 8 + i], f32(NEG))
    biasT = np.ascontiguousarray(biasT.reshape(128, 8, 512))
    cst = np.zeros((128, 6, 128), f32)
    r_ = np.arange(128)[:, None]; c_ = np.arange(128)[None, :]
    cst[:, 0] = (r_ == c_); cst[:, 1] = (c_ > r_); cst[:, 2] = (c_ >= r_); cst[:, 3] = (r_ > c_)
    cst[:, 4] = ((r_ // 64) == (c_ // 64)); cst[:, 5] = 1.0
    return dict(meta=np.ascontiguousarray(np.asarray(inp['meta_tokens'], f32)), wcat=wcat, wout=wout, wlora=wlora, wg2=wg2,
                wff1=np.ascontiguousarray(np.asarray(inp['w_ff1'], f32)[0]), wff2=np.ascontiguousarray(np.asarray(inp['w_ff2'], f32)[0]),
                lnbc=lnbc, pp=pp, biasT=biasT, cst=cst)


_NC_CACHE = {}


def kernel(**inputs):
    x = np.asarray(inputs['x'], np.float32)
    shared = _prep_shared(inputs)
    if 'nc' not in _NC_CACHE:
        _NC_CACHE['nc'] = build_nc()
    nc = _NC_CACHE['nc']
    in_maps = [dict(shared, x=np.ascontiguousarray(x[b])) for b in range(8)]
    res = run_bass_kernel_spmd(nc, in_maps, core_ids=list(range(8)))
    return np.stack([np.asarray(res.results[b]["out"], np.float32) for b in range(8)], 0)
```

```python
import math
from contextlib import ExitStack

import numpy as np
import concourse.bass as bass
import concourse.mybir as mybir
from concourse.bass_utils import run_bass_kernel_spmd

F32 = mybir.dt.float32
BF16 = mybir.dt.bfloat16
AF = mybir.ActivationFunctionType
ALU = mybir.AluOpType

ENGS = ['pe', 'act', 'dve', 'pool', 'sp']
BLOCKNAME = {'pe': 'tensor', 'act': 'scalar', 'dve': 'vector', 'pool': 'gpsimd', 'sp': 'sync'}
EPOCH = 16000
NDMA = 24

NT = 33
D = 1024
NCOL = 6656
ALPHA = 2.0 ** 0.25
LN_EPS = 1e-5
GN_EPS = 1e-5 * 64
DK = 0.6065306597126334
NEG = -30000.0
NP = 107


class Res:
    __slots__ = ('name', 'lw', 'rd')

    def __init__(self, name):
        self.name = name
        self.lw = None
        self.rd = []


class Prog:
    def __init__(self, nc, stack):
        self.nc = nc
        self.stack = stack
        self.q = {e: [] for e in ENGS}
        self.cnt = {e: 0 for e in ENGS}
        self.waited = {e: {} for e in ENGS}
        self.dma_i = 0
        self.dsem = [stack.enter_context(nc.semaphore(f"dsem{i}")) for i in range(NDMA)]
        self.esem = {e: [] for e in ENGS}
        self.out_toks = []

    def _sem(self, e, epoch):
        while len(self.esem[e]) <= epoch:
            self.esem[e].append(self.stack.enter_context(self.nc.semaphore(f"es_{e}_{len(self.esem[e])}")))
        return self.esem[e][epoch]

    def _need(self, eng, waits, tok, war=False):
        if tok is None:
            return
        if tok[0] == 'dma':
            key = ('dma', tok[1]); val = tok[2]
        else:
            peng, idx = tok
            if peng == eng:
                if war or eng == 'pe':
                    return
                if idx <= self.cnt[eng] - 2:
                    return
            key = peng; val = idx
        if self.waited[eng].get(key, 0) >= val:
            return
        if waits.get(key, 0) < val:
            waits[key] = val

    def _deps(self, eng, reads, writes, waits):
        for r in reads:
            self._need(eng, waits, r.lw)
        for w in writes:
            self._need(eng, waits, w.lw)
            for t in w.rd:
                self._need(eng, waits, t, war=True)
        for k, v in waits.items():
            self.waited[eng][k] = v

    def _mark(self, tok, reads, writes):
        for r in reads:
            r.rd.append(tok)
        for w in writes:
            w.lw = tok
            w.rd = []

    maxops = None
    allsig = False
    nops = 0

    def op(self, eng, fn, reads=(), writes=(), sig=True):
        self.nops += 1
        if self.maxops is not None and self.nops > self.maxops:
            return None
        if self.allsig:
            sig = True
        waits = {}
        self._deps(eng, reads, writes, waits)
        idx = self.cnt[eng] + 1
        if sig:
            self.cnt[eng] = idx
        tok = (eng, idx)
        self.q[eng].append((fn, list(waits.items()), sig, None))
        self._mark(tok, reads, writes)
        return tok

    def dma(self, eng, out_ap, in_ap, reads=(), writes=(), is_output=False, **kw):
        i = self.dma_i
        self.dma_i += 1
        s = i % NDMA
        val = 16 * (i // NDMA + 1)
        waits = {}
        if val > 16:
            self._need(eng, waits, ('dma', s, val - 16))
        self._deps(eng, reads, writes, waits)
        tok = ('dma', s, val)
        fn = lambda e: e.dma_start(out=out_ap, in_=in_ap, **kw)
        self.q[eng].append((fn, list(waits.items()), False, s))
        self._mark(tok, reads, writes)
        if is_output:
            self.out_toks.append(tok)
        return tok

    def finish(self):
        waits = {}
        for t in self.out_toks:
            self._need('sp', waits, t)
        self.q['sp'].append((None, list(waits.items()), False, None))

    def barrier(self):
        snap = dict(self.cnt)
        for e in ENGS:
            waits = {}
            for o in ENGS:
                if o != e and snap[o] > 0:
                    self._need(e, waits, (o, snap[o]))
            for k, v in waits.items():
                self.waited[e][k] = v
            self.q[e].append((None, list(waits.items()), False, None))

    def emit(self):
        nc = self.nc
        for e in ENGS:
            n = self.cnt[e]
            if n:
                self._sem(e, (n - 1) // EPOCH)
        with nc.Block() as block:
            for e in ENGS:
                if self.q[e]:
                    self._emit_engine(block, e)

    def _emit_engine(self, block, e):
        q = self.q[e]
        prog = self

        def body(eng):
            cnt = 0
            for fn, waits, sig, dsem in q:
                for key, val in waits:
                    if isinstance(key, tuple):
                        eng.wait_ge(prog.dsem[key[1]], val)
                    else:
                        eng.wait_ge(prog._sem(key, (val - 1) // EPOCH), (val - 1) % EPOCH + 1)
                if fn is None:
                    continue
                ins = fn(eng)
                if dsem is not None:
                    ins.then_inc(prog.dsem[dsem], 16)
                elif sig:
                    cnt += 1
                    ins.then_inc(prog._sem(e, (cnt - 1) // EPOCH), 1)
        getattr(block, BLOCKNAME[e])(body)

    def mm(self, out, lhsT, rhs, start, stop, reads, writes, sig=True):
        return self.op('pe', lambda e: e.matmul(out, lhsT=lhsT, rhs=rhs, start=start, stop=stop), reads, writes, sig)

    def tr(self, out, in_, ident, reads, writes, sig=True):
        return self.op('pe', lambda e: e.transpose(out, in_, ident), reads, writes, sig)

    def act(self, out, in_, func, reads, writes, bias=None, scale=None):
        kw = {}
        if bias is not None:
            kw['bias'] = bias
        if scale is not None:
            kw['scale'] = scale
        return self.op('act', lambda e: e.activation(out=out, in_=in_, func=func, **kw), reads, writes)

    def tt(self, eng, out, in0, in1, op, reads, writes):
        return self.op(eng, lambda e: e.tensor_tensor(out=out, in0=in0, in1=in1, op=op), reads, writes)

    def ts(self, eng, out, in0, s1, s2, op0, op1, reads, writes):
        if op1 is None:
            return self.op(eng, lambda e: e.tensor_scalar(out=out, in0=in0, scalar1=s1, scalar2=None, op0=op0), reads, writes)
        return self.op(eng, lambda e: e.tensor_scalar(out=out, in0=in0, scalar1=s1, scalar2=s2, op0=op0, op1=op1), reads, writes)

    def stt(self, out, in0, scalar, in1, op0, op1, reads, writes):
        return self.op('dve', lambda e: e.scalar_tensor_tensor(out=out, in0=in0, scalar=scalar, in1=in1, op0=op0, op1=op1), reads, writes)

    def cp(self, eng, out, in_, reads, writes):
        if eng == 'act':
            return self.op('act', lambda e: e.copy(out=out, in_=in_), reads, writes)
        return self.op(eng, lambda e: e.tensor_copy(out=out, in_=in_), reads, writes)

    def memset(self, eng, ap, val, writes):
        return self.op(eng, lambda e: e.memset(ap, val), (), writes)


def build_nc(cfg=None):
    cfg = cfg or {}
    nt = cfg.get('nt', NT)
    nc = bass.Bass("TRN2", target_bir_lowering=False, dynamic_dma_scratch_size=4096)
    dt = nc.dram_tensor
    x_d = dt("x", [4096, D], F32, kind="ExternalInput").ap()
    meta_d = dt("meta", [16, D], F32, kind="ExternalInput").ap()
    wcat_d = dt("wcat", [D, NCOL], F32, kind="ExternalInput").ap()
    wout_d = dt("wout", [D, D], F32, kind="ExternalInput").ap()
    wlora_d = dt("wlora", [128, D], F32, kind="ExternalInput").ap()
    wg2_d = dt("wg2", [128, D], F32, kind="ExternalInput").ap()
    wff1_d = dt("wff1", [D, 4096], F32, kind="ExternalInput").ap()
    wff2_d = dt("wff2", [4096, D], F32, kind="ExternalInput").ap()
    lnbc_d = dt("lnbc", [128, 6, D], F32, kind="ExternalInput").ap()
    pp_d = dt("pp", [128, NP], F32, kind="ExternalInput").ap()
    bias_d = dt("biasT", [128, 8, 512], F32, kind="ExternalInput").ap()
    cst_d = dt("cst", [128, 6, 128], F32, kind="ExternalInput").ap()
    out_d = dt("out", [4096, D], F32, kind="ExternalOutput").ap()
    h1_d = dt("h1s", [4096, D], F32, kind="Internal").ap()

    with ExitStack() as st:
        P = Prog(nc, st)
        P.maxops = cfg.get('maxops')
        P.allsig = cfg.get('allsig', False)

        ARENA = 222000
        arena = st.enter_context(nc.sbuf_tensor("arena", [128, ARENA // 2], BF16))
        bump = [0]

        class _T:
            def __init__(self, ap):
                self.ap = ap

            def __getitem__(self, k):
                return self.ap[k]

        def SB(name, shape, dtype):
            n = 1
            for s_ in shape[1:]:
                n *= s_
            nbytes = n * (4 if dtype == F32 else 2)
            nbytes = (nbytes + 31) // 32 * 32
            o = bump[0]
            bump[0] += nbytes
            assert bump[0] <= ARENA, (name, bump[0])
            v = arena[:, o // 2:(o + nbytes) // 2]
            if dtype == F32:
                v = v.bitcast(F32)
            v = v[0:shape[0], 0:n]
            if len(shape) == 3:
                v = v.rearrange("p (a b) -> p a b", a=shape[1])
            elif len(shape) == 4:
                v = v.rearrange("p (a b c) -> p a b c", a=shape[1], b=shape[2])
            return _T(v)

        def PS(name):
            return st.enter_context(nc.psum_tensor(name, [128, 512], F32))

        def same2(t):
            return [t, t]
        wbig = SB("wbig", [128, 65536], BF16)
        Rwk = [Res(f"wk{k}") for k in range(8)]; Rwo = Res("wo"); Rf2 = [Res(f"f2_{k}") for k in range(16)]
        w_in = wbig[:, 0:8 * NCOL].rearrange("p (k c) -> p k c", k=8)
        w_out = wbig[:, 8 * NCOL:8 * NCOL + 8192].rearrange("p (k c) -> p k c", k=8)
        wlora = _T(wbig[:, 61440:62464])
        wg2 = _T(wbig[:, 62464:63488])
        w_ff1 = wbig[:, 0:32768].rearrange("p (k c) -> p k c", k=8)
        w_ff2 = wbig[:, 32768:65536].rearrange("p (k c) -> p k c", k=32)
        Rwl = Res("wl")
        lnbc = SB("lnA", [128, 2, D], F32); Rlnbc = Res("lnbc")
        pp = SB("pp_sb", [128, NP], F32); Rpp = Res("pp")
        cst = SB("cst_sb", [128, 6, 128], F32); Rcst = Res("cst")
        identb = SB("identb", [128, 128], BF16)
        onesb = SB("onesb", [128, 128], BF16)
        blkb = SB("blkb", [128, 128], BF16)
        sinkexp = SB("sinkexp", [128, 8], F32)
        epsb = SB("epsb", [128, 4], F32); Reps = Res("eps")
        ident_f = cst[:, 0, :]
        mask2 = cst[:, 1:3, :]
        mask_ls = cst[:, 3, :]
        ones_f = cst[:, 5, :]
        stt6 = SB("stt6", [128, 2, 6], F32); Rst = Res("st")
        mv = SB("mv", [128, 2], F32); Rmv = Res("mv")
        rstd = SB("rstd", [128, 1], F32); Rrstd = Res("rstd")
        xt = same2(SB("xt", [128, D], F32)); Rxt = same2(Res("xt"))
        pre = xt[0]; Rpre = Rxt[0]
        xnb = SB("xnb", [128, D], BF16); Rxnb = Res("xnb")
        hres = [SB(f"hres{j}", [128, D], F32) for j in range(2)]; Rhres = [Res(f"hres{j}") for j in range(2)]
        hT = same2(SB("hT", [128, 8, 128], BF16)); RhT = same2(Res("hT"))
        mark = bump[0]
        biasT = SB("bias_sb", [128, 8, 512], BF16); Rbias = Res("bias")
        kT = [SB(f"kT{j}", [128, 128], BF16) for j in range(3)]; RkT = [Res(f"kT{j}") for j in range(3)]
        vat = [SB(f"vat{j}", [128, 128], BF16) for j in range(3)]; Rvat = [Res(f"vat{j}") for j in range(3)]
        qT = same2(SB("qT", [128, 8, 2, 128], BF16)); RqT = same2([Res(f"qT_{i}") for i in range(8)])
        gat = same2(SB("ga", [128, 8, 128], BF16)); Rga = same2([Res(f"ga_{i}") for i in range(8)])
        grw = same2(SB("gr", [128, 8, 128], BF16)); Rgr = same2([Res(f"gr_{i}") for i in range(8)])
        rkv = SB("rkv", [128, 8, 3, 128], F32); Rrkv = [Res(f"rkv{i}") for i in range(8)]
        car = SB("car", [128, 2, 26], F32); Rcar = Res("car")
        zt = [SB(f"zt{j}", [128, 3, 129], F32) for j in range(2)]; Rzt = [Res(f"zt{j}") for j in range(2)]
        zd = [SB(f"zd{j}", [128, 3, 128], F32) for j in range(1)]; Rzd = [Res(f"zd{j}") for j in range(1)]
        zAG = SB("zAG", [128, 2, 128], F32); RzAG = Res("zAG")
        lor = [SB(f"lor{j}", [128, 3, 128], BF16) for j in range(2)]; Rlor = [Res(f"lor{j}") for j in range(2)]
        merged = SB("merged", [128, 8, 128], BF16); Rmg = [Res(f"mg{i}") for i in range(8)]

        def dbl(name, shape, dtype):
            return same2(SB(name, shape, dtype)), same2(Res(name))
        sig_, Rsig = dbl("sig", [128, 128], F32)
        aa_, Raa = dbl("aa", [128, 128], F32)
        gg_, Rgg = dbl("gg", [128, 128], F32)
        cum_, Rcum = dbl("cum", [128, 128], F32)
        epos_, Repos = dbl("epos", [128, 128], F32)
        eneg_, Reneg = dbl("eneg", [128, 128], F32)
        eprv_, Reprv = dbl("eprv", [128, 128], F32)
        eend_, Reend = dbl("eend", [128, 128], F32)
        nb_, Rnb = dbl("nb", [128, 2], F32)
        kk_, Rkk = dbl("kk", [128, 128], F32)
        kk2_, Rkk2 = dbl("kk2", [128, 128], BF16)
        rs_, Rrs = dbl("rs", [128, 128], F32)
        tmp_, Rtmp = rs_, Rrs
        ka_, Rka = dbl("ka", [128, 128], F32)
        kp_, Rkp = dbl("kp", [128, 128], F32)
        AR_, RAR = dbl("AR", [128, 2, 128], BF16)
        Bt_, RBt = dbl("Bt", [128, 128], BF16)
        Kt_, RKt = dbl("Kt", [128, 128], BF16)
        BKg_, RBKg = dbl("BKg", [128, 3, 128], BF16)
        TM3_, RTM3 = dbl("TM3", [128, 3, 128], BF16)
        rkr_, Rrkr = dbl("rkr", [128, 128], BF16)
        LT_ = [SB(f"LT{g}", [128, 2, 128], BF16) for g in range(2)]; RLT = [Res(f"LT{g}") for g in range(2)]
        KT2_ = [SB(f"KT2{g}", [128, 2, 128], BF16) for g in range(2)]; RKT2 = [Res(f"KT2{g}") for g in range(2)]
        PP_ = [[SB(f"PP{g}_{j}", [128, 2, 128], BF16) for j in range(2)] for g in range(2)]
        RPP = [[Res(f"PP{g}_{j}") for j in range(2)] for g in range(2)]
        MT_ = [[SB(f"MT{g}_{j}", [128, 128], BF16) for j in range(2)] for g in range(2)]
        RMT = [[Res(f"MT{g}_{j}") for j in range(2)] for g in range(2)]
        Wb_ = [SB(f"Wb{g}", [128, 64], BF16) for g in range(2)]; RWb = [Res(f"Wb{g}") for g in range(2)]
        Ub_ = [SB(f"Ub{g}", [128, 64], BF16) for g in range(2)]; RUb = [Res(f"Ub{g}") for g in range(2)]
        Hst = SB("Hst", [128, 8, 64], F32); Hb = SB("Hb", [128, 8, 2, 64], BF16)
        RH = [Res(f"H{i}") for i in range(8)]; RHb = [Res(f"Hb{i}") for i in range(8)]
        att = SB("att", [128, 128], F32); Ratt = Res("att")
        y2_, Ry2 = same2(rs_[0]), same2(Rrs[0])
        gst = SB("gst", [128, 2, 6], F32); Rgst = Res("gst")
        gmv = SB("gmv", [128, 2, 2], F32); Rgmv = Res("gmv")
        grs = SB("grs", [128, 2], F32); Rgrs = Res("grs")
        ynb = SB("ynb", [128, 128], BF16); Rynb = Res("ynb")
        yy_, Ryy = dbl("yy", [128, 128], F32)
        PT = SB("PTs", [128, 512], BF16); RPT = Res("PT")
        rden = rs_[0]; Rrden = Rrs[0]
        p1_end = bump[0]
        bump[0] = mark
        lnB = SB("lnB", [128, 2, D], F32)
        h1t2 = [SB(f"h1t2_{j}", [128, D], F32) for j in range(2)]; Rh1t2 = [Res(f"h1t2_{j}") for j in range(2)]
        xnb2 = [SB(f"xnb2_{j}", [128, D], BF16) for j in range(2)]; Rxnb2 = [Res(f"xnb2_{j}") for j in range(2)]
        hT2 = [SB(f"hT2_{j}", [128, 8, 128], BF16) for j in range(2)]; RhT2 = [Res(f"hT2_{j}") for j in range(2)]
        uT = SB("uT", [128, 32, 128], BF16); RuT = [Res(f"uT{j}") for j in range(8)]
        urelu = [SB(f"urelu{j}", [128, 512], F32) for j in range(2)]; Rurelu = [Res(f"urelu{j}") for j in range(2)]
        print("SBUF bytes: shared", mark, "phase1 end", p1_end, "phase2 end", bump[0])

        pb = [PS(f"pb{j}") for j in range(8)]
        Rpb = [Res(f"pb{j}") for j in range(8)]
        pb_bf = [p.bitcast(BF16) if hasattr(p, 'bitcast') else None for p in pb]

        P.dma('sp', pp[:], pp_d, writes=[Rpp])
        P.dma('sp', cst[:], cst_d, writes=[Rcst])
        P.dma('sp', lnbc[:], lnbc_d[:, 0:2, :], writes=[Rlnbc])
        for i_ in range(8):
            P.dma('pool', biasT[:, i_, :], bias_d[:, i_, :], writes=[Rbias])
        P.dma('pool', wlora[:], wlora_d, writes=[Rwl])
        P.dma('pool', wg2[:], wg2_d, writes=[Rwl])
        wcat_v = wcat_d.rearrange("(k p) c -> p k c", p=128)
        for kc in range(8):
            for c0 in range(0, NCOL, 2048):
                c1 = min(NCOL, c0 + 2048)
                P.dma('pool', w_in[:, kc, c0:c1], wcat_v[:, kc, c0:c1], writes=[Rwk[kc]])
        wout_v = wout_d.rearrange("(k p) c -> p k c", p=128)
        for kc in range(8):
            P.dma('pool', w_out[:, kc, :], wout_v[:, kc, :], writes=[Rwo])
        Rk = Res("consts")
        P.cp('dve', identb[:], ident_f, [Rcst], [Rk])
        P.cp('dve', onesb[:], ones_f, [Rcst], [Rk])
        P.cp('dve', blkb[:], cst[:, 4, :], [Rcst], [Rk])
        P.act(sinkexp[:], pp[:, 98:106], AF.Exp, [Rpp], [Rk])
        P.ts('pool', lnbc[:, 0, :], lnbc[:, 0, :], ALPHA, None, ALU.mult, None, [Rlnbc], [Rlnbc])
        P.ts('pool', lnbc[:, 1, :], lnbc[:, 1, :], ALPHA, None, ALU.mult, None, [Rlnbc], [Rlnbc])
        P.memset('pool', car[:], 0.0, [Rcar])
        P.memset('pool', xt[0][:], 0.0, [Rxt[0]])
        P.memset('dve', Hst[:], 0.0, RH)
        P.memset('dve', Hb[:], 0.0, RHb)
        P.memset('pool', qT[0][:], 0.0, RqT[0])
        P.memset('pool', lor[0][:], 0.0, [Rlor[0]])
        P.memset('pool', lor[1][:], 0.0, [Rlor[1]])

        zt_i = [0]
        zd_i = [0]

        def layer_norm_stats(src, Rsrc, eps):
            P.op('dve', lambda e: e.bn_stats(out=stt6[:, 0, :], in_=src[:, 0:512]), [Rsrc], [Rst])
            P.op('dve', lambda e: e.bn_stats(out=stt6[:, 1, :], in_=src[:, 512:1024]), [Rsrc], [Rst])
            P.op('dve', lambda e: e.bn_aggr(out=mv[:], in_=stt6[:].rearrange("p a b -> p (a b)")), [Rst], [Rmv])
            P.act(rstd[:], mv[:, 1:2], AF.Sqrt, [Rmv], [Rrstd], bias=eps_ap(eps), scale=1.0)
            P.op('dve', lambda e: e.reciprocal(out=rstd[:], in_=rstd[:]), [Rrstd], [Rrstd])

        P.memset('pool', epsb[:, 0:1], LN_EPS, [Reps])
        P.memset('pool', epsb[:, 1:2], GN_EPS, [Reps])
        P.memset('pool', epsb[:, 2:3], 1e-16, [Reps])

        def eps_ap(eps):
            return epsb[:, 0:1] if eps == LN_EPS else epsb[:, 1:2]

        def token_shift(psrc, Rpsrc, c0, nb, j0, mu_ap, dst, Rdst, n):
            k = zt_i[0] % 2; zt_i[0] += 1
            kd = 0
            z = zt[k]
            pc, pn_ = (n - 1) % 2, n % 2
            P.cp('act', z[:, 0:nb, 1:129], psrc[:, c0:c0 + nb * 128].rearrange("p (a b) -> p a b", a=nb), [Rpsrc], [Rzt[k]])
            P.cp('pool', z[:, 0:nb, 0], car[:, pc, j0:j0 + nb], [Rcar], [Rzt[k]])
            P.cp('pool', car[:, pn_, j0:j0 + nb], z[:, 0:nb, 128], [Rzt[k]], [Rcar])
            d = zd[kd]
            P.tt('dve', d[:, 0:nb, :], z[:, 0:nb, 0:128], z[:, 0:nb, 1:129], ALU.subtract, [Rzt[k]], [Rzd[kd]])
            P.tt('dve', d[:, 0:nb, :], d[:, 0:nb, :], mu_ap, ALU.mult, [Rzd[kd], Rpp], [Rzd[kd]])
            P.tt('pool', dst, d[:, 0:nb, :], z[:, 0:nb, 1:129], ALU.add, [Rzd[kd], Rzt[k]], [Rdst])

        pj_i = [0]

        def proj_bank():
            b = 1 + (pj_i[0] % 2); pj_i[0] += 1
            return b

        def front(n):
            par = n % 2
            if n >= 1:
                P.dma('sp', xt[par][:], x_d[(n - 1) * 128:n * 128, :], writes=[Rxt[par]])
            else:
                P.dma('sp', xt[0][112:128, :], meta_d, writes=[Rxt[0]])
            layer_norm_stats(xt[par], Rxt[par], LN_EPS)
            P.ts('dve', hres[par][:], xt[par][:], mv[:, 0:1], rstd[:, 0:1], ALU.subtract, ALU.mult, [Rxt[par], Rmv, Rrstd], [Rhres[par]])
            P.cp('pool', xnb[:], hres[par][:], [Rhres[par]], [Rxnb])
            if n >= 1:
                P.tt('pool', hres[par][:], hres[par][:], lnbc[:, 0, :], ALU.mult, [Rhres[par], Rlnbc], [Rhres[par]])
                P.tt('pool', hres[par][:], hres[par][:], lnbc[:, 1, :], ALU.add, [Rhres[par], Rlnbc], [Rhres[par]])
            yield
            ptb = pb[0].bitcast(BF16)
            for kc in range(8):
                P.tr(ptb[:, kc * 128:(kc + 1) * 128], xnb[:, kc * 128:(kc + 1) * 128], identb[:], [Rxnb, Rk], [Rpb[0]], sig=(kc == 7))
            for kc in range(8):
                P.act(hT[par][:, kc, :], ptb[:, kc * 128:(kc + 1) * 128], AF.Identity, [Rpb[0], Rpp], [RhT[par]],
                      bias=pp[:, 8 + kc:9 + kc], scale=pp[:, kc:kc + 1])
            if n == 0:
                P.memset('pool', hT[0][:, :, 0:112], 0.0, [RhT[0]])
            yield

            def fm_group(cols, b):
                for s_, cb in enumerate(cols):
                    for kc in range(8):
                        P.mm(pb[b][:, s_ * 128:(s_ + 1) * 128], w_in[:, kc, cb:cb + 128], hT[par][:, kc, :],
                             kc == 0, kc == 7, [Rwk[kc], RhT[par]], [Rpb[b]], sig=(kc == 7))
                    if s_ < len(cols) - 1:
                        yield

            b = proj_bank()
            for s_, cb in enumerate((0, 256, 384)):
                for kc in range(8):
                    P.mm(pb[b][:, s_ * 128:(s_ + 1) * 128], w_in[:, kc, cb:cb + 128], hT[par][:, kc, :],
                         kc == 0, kc == 7, [Rwk[kc], RhT[par]], [Rpb[b]], sig=False)
            for kc in range(8):
                P.mm(pb[b][:, 384:512], hT[par][:, kc, :], w_in[:, kc, 128:256], kc == 0, kc == 7,
                     [Rwk[kc], RhT[par]], [Rpb[b]], sig=(kc == 7))
            P.cp('act', kT[n % 3][:], pb[b][:, 0:128], [Rpb[b]], [RkT[n % 3]])
            P.cp('act', vat[n % 3][:], pb[b][:, 384:512], [Rpb[b]], [Rvat[n % 3]])
            token_shift(pb[b], Rpb[b], 128, 2, 24, pp[:, 40:42].unsqueeze(2).to_broadcast([128, 2, 128]),
                        zAG[:], RzAG, n)
            P.act(lor[par][0:64, 0, :], zAG[0:64, 0, :], AF.Tanh, [RzAG], [Rlor[par]])
            P.cp('dve', lor[par][64:128, 1, :], zAG[64:128, 0, :], [RzAG], [Rlor[par]])
            P.act(lor[par][:, 2, :], zAG[:, 1, :], AF.Sigmoid, [RzAG], [Rlor[par]])
            yield
            for i in range(8):
                yield ('need', i)
                base = 512 + i * 768
                b = proj_bank()
                yield from fm_group((base, base + 128, base + 256), b)
                P.act(qT[par][0:64, i, 0, :], pb[b][0:64, 0:128], AF.Identity, [Rpb[b]], [RqT[par][i]], scale=0.125)
                P.act(qT[par][64:128, i, 1, :], pb[b][64:128, 0:128], AF.Identity, [Rpb[b]], [RqT[par][i]], scale=0.125)
                P.act(gat[par][:, i, :], pb[b][:, 128:256], AF.Sigmoid, [Rpb[b]], [Rga[par][i]])
                P.act(grw[par][:, i, :], pb[b][:, 256:384], AF.Sigmoid, [Rpb[b]], [Rgr[par][i]])
                yield
                b = proj_bank()
                yield from fm_group((base + 384, base + 512, base + 640), b)
                token_shift(pb[b], Rpb[b], 0, 3, i * 3,
                            pp[:, 16 + i * 3:19 + i * 3].unsqueeze(2).to_broadcast([128, 3, 128]),
                            rkv[:, i, :, :], Rrkv[i], n)
                yield

        def mix(n):
            par = n % 2
            ppar = (n - 1) % 2
            for i in range(8):
                q_ = i % 2
                r_s = rkv[:, i, 0, :]; k_s = rkv[:, i, 1, :]; v_s = rkv[:, i, 2, :]
                Rr = Rrkv[i]
                b = 3
                P.mm(pb[b][:, 0:128], wlora[:, i * 128:(i + 1) * 128], lor[par][:, 0, :], True, True, [Rwl, Rlor[par]], [Rpb[b]], sig=False)
                P.mm(pb[b][:, 128:256], wlora[:, i * 128:(i + 1) * 128], lor[par][:, 1, :], True, True, [Rwl, Rlor[par]], [Rpb[b]], sig=False)
                P.mm(pb[b][:, 256:384], wg2[:, i * 128:(i + 1) * 128], lor[par][:, 2, :], True, True, [Rwl, Rlor[par]], [Rpb[b]])
                P.act(sig_[q_][:], pb[b][:, 0:128], AF.Sigmoid, [Rpb[b], Rpp], [Rsig[q_]], bias=pp[:, 42 + i:43 + i], scale=1.0)
                P.act(aa_[q_][:], pb[b][:, 128:256], AF.Sigmoid, [Rpb[b], Rpp], [Raa[q_]], bias=pp[:, 50 + i:51 + i], scale=1.0)
                P.cp('act', gg_[q_][:], pb[b][:, 256:384], [Rpb[b]], [Rgg[q_]])
                P.op('dve', lambda e, o=cum_[q_], s=sig_[q_]: e.tensor_tensor_scan(out=o[:], data0=ones_f, data1=s[:], initial=0.0, op0=ALU.mult, op1=ALU.add),
                     [Rsig[q_], Rcst], [Rcum[q_]])
                P.tt('pool', tmp_[q_][:], cum_[q_][:], sig_[q_][:], ALU.subtract, [Rcum[q_], Rsig[q_]], [Rtmp[q_]])
                P.ts('dve', nb_[q_][:, 0:1], cum_[q_][:, 127:128], -DK, None, ALU.mult, None, [Rcum[q_]], [Rnb[q_]])
                P.act(epos_[q_][:], cum_[q_][:], AF.Exp, [Rcum[q_]], [Repos[q_]], scale=-DK)
                P.act(eneg_[q_][:], cum_[q_][:], AF.Exp, [Rcum[q_]], [Reneg[q_]], scale=DK)
                P.act(eprv_[q_][:], tmp_[q_][:], AF.Exp, [Rtmp[q_]], [Reprv[q_]], scale=-DK)
                P.act(eend_[q_][:], cum_[q_][:], AF.Exp, [Rcum[q_], Rnb[q_]], [Reend[q_]], bias=nb_[q_][:, 0:1], scale=DK)
                P.act(nb_[q_][:, 1:2], cum_[q_][:, 127:128], AF.Exp, [Rcum[q_]], [Rnb[q_]], scale=-DK)
                P.ts('dve', kk_[q_][:], k_s, pp[:, 58 + i:59 + i], None, ALU.mult, None, [Rr, Rpp], [Rkk[q_]])
                P.tt('pool', kk2_[q_][:], kk_[q_][:], kk_[q_][:], ALU.mult, [Rkk[q_]], [Rkk2[q_]])
                P.mm(pb[b][:, 384:512], blkb[:], kk2_[q_][:], True, True, [Rk, Rkk2[q_]], [Rpb[b]])
                P.act(rs_[q_][:], pb[b][:, 384:512], AF.Ln, [Rpb[b], Reps], [Rrs[q_]], bias=epsb[:, 2:3], scale=1.0)
                P.act(rs_[q_][:], rs_[q_][:], AF.Exp, [Rrs[q_]], [Rrs[q_]], scale=-0.5)
                P.tt('dve', kk_[q_][:], kk_[q_][:], rs_[q_][:], ALU.mult, [Rkk[q_], Rrs[q_]], [Rkk[q_]])
                P.tt('pool', ka_[q_][:], kk_[q_][:], aa_[q_][:], ALU.mult, [Rkk[q_], Raa[q_]], [Rka[q_]])
                P.ts('dve', kp_[q_][:], aa_[q_][:], -1.0, pp[:, 66 + i:67 + i], ALU.add, ALU.mult, [Raa[q_], Rpp], [Rkp[q_]])
                P.stt(kp_[q_][:], kp_[q_][:], 1.0, k_s, ALU.add, ALU.mult, [Rkp[q_], Rr], [Rkp[q_]])
                P.stt(AR_[q_][:, 0, :], kk_[q_][:], -1.0, eprv_[q_][:], ALU.mult, ALU.mult, [Rkk[q_], Reprv[q_]], [RAR[q_]])
                P.tt('pool', AR_[q_][:, 1, :], r_s, epos_[q_][:], ALU.mult, [Rr, Repos[q_]], [RAR[q_]])
                P.tt('dve', Bt_[q_][:], ka_[q_][:], eneg_[q_][:], ALU.mult, [Rka[q_], Reneg[q_]], [RBt[q_]])
                P.tt('pool', Kt_[q_][:], kp_[q_][:], eneg_[q_][:], ALU.mult, [Rkp[q_], Reneg[q_]], [RKt[q_]])
                P.tt('dve', BKg_[q_][:, 0, :], ka_[q_][:], eend_[q_][:], ALU.mult, [Rka[q_], Reend[q_]], [RBKg[q_]])
                P.tt('pool', BKg_[q_][:, 1, :], kp_[q_][:], eend_[q_][:], ALU.mult, [Rkp[q_], Reend[q_]], [RBKg[q_]])
                P.cp('pool', BKg_[q_][:, 2, :], v_s, [Rr], [RBKg[q_]])
                P.stt(rkr_[q_][:], r_s, pp[:, 74 + i:75 + i], kp_[q_][:], ALU.mult, ALU.mult, [Rr, Rpp, Rkp[q_]], [Rrkr[q_]])
                yield
                if cfg.get("mix_stop", 99) <= 1:
                    return
                ptb = pb[0].bitcast(BF16)
                for j in range(3):
                    P.tr(ptb[:, j * 128:(j + 1) * 128], BKg_[q_][:, j, :], identb[:], [RBKg[q_], Rk], [Rpb[0]], sig=(j == 2))
                P.cp('act', TM3_[q_][:].rearrange("p a b -> p (a b)"), ptb[:, 0:384], [Rpb[0]], [RTM3[q_]])
                for g in range(2):
                    gs = slice(g * 64, (g + 1) * 64)
                    ba = 4 + g
                    bd = 6 + g
                    ARv = AR_[q_][gs, :, :].rearrange("p a b -> p (a b)")
                    P.mm(pb[ba][:, 0:256], Bt_[q_][gs, :], ARv, True, True, [RBt[q_], RAR[q_]], [Rpb[ba]], sig=False)
                    P.mm(pb[ba][:, 256:512], Kt_[q_][gs, :], ARv, True, True, [RKt[q_], RAR[q_]], [Rpb[ba]])
                    P.mm(pb[bd][:, 384:512], AR_[q_][gs, 0, :], Bt_[q_][gs, :], True, True, [RBt[q_], RAR[q_]], [Rpb[bd]])
                    P.tt('dve', LT_[g][:].rearrange("p a b -> p (a b)"), pb[ba][:, 0:256], mask2.rearrange("p a b -> p (a b)"), ALU.mult,
                         [Rpb[ba], Rcst], [RLT[g]])
                    P.tt('dve', KT2_[g][:].rearrange("p a b -> p (a b)"), pb[ba][:, 256:512], mask2.rearrange("p a b -> p (a b)"), ALU.mult,
                         [Rpb[ba], Rcst], [RKT2[g]])
                    P.tt('dve', PP_[g][0][:, 0, :], pb[bd][:, 384:512], mask_ls, ALU.mult, [Rpb[bd], Rcst], [RPP[g][0]])
                    P.cp('pool', PP_[g][0][:, 1, :], LT_[g][:, 0, :], [RLT[g]], [RPP[g][0]])
                    P.tt('pool', MT_[g][0][:], LT_[g][:, 0, :], identb[:], ALU.add, [RLT[g], Rk], [RMT[g][0]])
                yield
                if cfg.get("mix_stop", 99) <= 2:
                    return
                bl = 3
                bm = 4

                def att_qk():
                    for g in range(2):
                        for sh in range(2):
                            slot = (n - 1) % 3 if sh == 0 else n % 3
                            c0 = (g * 2 + sh) * 128
                            P.mm(pb[bl][:, c0:c0 + 128], kT[slot][:, :], qT[par][:, i, g, :], True, True, [RkT[slot], RqT[par][i]], [Rpb[bl]],
                                 sig=(g == 1 and sh == 1))
                    P.tt('dve', pb[bl][:], pb[bl][:], biasT[:, i, :], ALU.add, [Rpb[bl], Rbias], [Rpb[bl]])

                def att_exp():
                    if n == 1:
                        lbv = pb[bl][:].rearrange("p (g s q) -> p g s q", g=2, s=2)
                        PTv = PT[:].rearrange("p (g s q) -> p g s q", g=2, s=2)
                        P.act(PTv[:, :, 0, :], lbv[:, :, 0, :], AF.Exp, [Rpb[bl], Rpp], [RPT], bias=pp[:, 106:107], scale=1.0)
                        P.act(PTv[:, :, 1, :], lbv[:, :, 1, :], AF.Exp, [Rpb[bl]], [RPT])
                    else:
                        P.act(PT[:], pb[bl][:], AF.Exp, [Rpb[bl]], [RPT])

                def att_pv():
                    for g in range(2):
                        gs = slice(g * 64, (g + 1) * 64)
                        for sh in range(2):
                            slot = (n - 1) % 3 if sh == 0 else n % 3
                            c0 = (g * 2 + sh) * 128
                            P.mm(pb[bm][gs, 0:128], vat[slot][:, gs], PT[:, c0:c0 + 128], sh == 0, sh == 1, [Rvat[slot], RPT], [Rpb[bm]], sig=False)
                        for sh in range(2):
                            c0 = (g * 2 + sh) * 128
                            P.mm(pb[bm][gs, 128:256], onesb[:, 0:64], PT[:, c0:c0 + 128], sh == 0, sh == 1, [Rk, RPT], [Rpb[bm]],
                                 sig=(g == 1 and sh == 1))
                    P.ts('dve', rden[:], pb[bm][:, 128:256], sinkexp[:, i:i + 1], None, ALU.add, None, [Rpb[bm], Rk], [Rrden])

                def att_norm():
                    P.op('dve', lambda e: e.reciprocal(out=rden[:], in_=rden[:]), [Rrden], [Rrden])
                    P.tt('dve', att[:], pb[bm][:, 0:128], rden[:], ALU.mult, [Rpb[bm], Rrden], [Ratt])
                    P.tt('pool', att[:], att[:], gat[par][:, i, :], ALU.mult, [Ratt, Rga[par][i]], [Ratt])
                att_pieces = {1: att_qk, 2: att_exp, 3: att_pv, 4: att_norm} if n >= 1 else {}
                for lev in range(1, 7):
                    src_ = (lev - 1) % 2
                    dst = lev % 2
                    for g in range(2):
                        bd = 6 + g
                        Pm = PP_[g][src_][:, 0, :]; PTm = PP_[g][src_][:, 1, :]
                        last = (lev == 6)
                        P.mm(pb[bd][:, 0:128], PTm, Pm, True, True, [RPP[g][src_]], [Rpb[bd]], sig=last)
                        if not last:
                            P.mm(pb[bd][:, 128:256], Pm, PTm, True, True, [RPP[g][src_]], [Rpb[bd]])
                            P.cp('act', PP_[g][dst][:].rearrange("p a b -> p (a b)"), pb[bd][:, 0:256], [Rpb[bd]], [RPP[g][dst]])
                        else:
                            P.cp('act', PP_[g][dst][:, 0, :], pb[bd][:, 0:128], [Rpb[bd]], [RPP[g][dst]])
                        P.mm(pb[bd][:, 256:384], PP_[g][dst][:, 0, :], MT_[g][src_][:], True, True, [RPP[g][dst], RMT[g][src_]], [Rpb[bd]])
                        P.tt('dve', MT_[g][dst][:], pb[bd][:, 256:384], MT_[g][src_][:], ALU.add, [Rpb[bd], RMT[g][src_]], [RMT[g][dst]])
                    if lev in att_pieces:
                        att_pieces[lev]()
                    yield
                if cfg.get("mix_stop", 99) <= 3:
                    return
                MTf = [MT_[g][0] for g in range(2)]
                RMTf = [RMT[g][0] for g in range(2)]
                bs_ = 5
                for g in range(2):
                    gs = slice(g * 64, (g + 1) * 64)
                    Vt_g = TM3_[q_][:, 2, gs]
                    P.mm(pb[bs_][:, g * 64:(g + 1) * 64], AR_[q_][:, 0, :], Hb[:, i, g, :], True, False, [RAR[q_], RHb[i]], [Rpb[bs_]], sig=False)
                    P.mm(pb[bs_][:, g * 64:(g + 1) * 64], KT2_[g][:, 0, :], Vt_g, False, True, [RKT2[g], RTM3[q_]], [Rpb[bs_]])
                    P.cp('act', Wb_[g][:], pb[bs_][:, g * 64:(g + 1) * 64], [Rpb[bs_]], [RWb[g]])
                for g in range(2):
                    P.mm(pb[bs_][:, 128 + g * 64:128 + (g + 1) * 64], MTf[g][:], Wb_[g][:], True, True, [RMTf[g], RWb[g]], [Rpb[bs_]])
                    P.cp('dve', Ub_[g][:], pb[bs_][:, 128 + g * 64:128 + (g + 1) * 64], [Rpb[bs_]], [RUb[g]])
                for g in range(2):
                    gs = slice(g * 64, (g + 1) * 64)
                    Vt_g = TM3_[q_][:, 2, gs]
                    oc = slice(256 + g * 64, 256 + (g + 1) * 64)
                    P.mm(pb[bs_][:, oc], AR_[q_][:, 1, :], Hb[:, i, g, :], True, False, [RAR[q_], RHb[i]], [Rpb[bs_]], sig=False)
                    P.mm(pb[bs_][:, oc], LT_[g][:, 1, :], Ub_[g][:], False, False, [RLT[g], RUb[g]], [Rpb[bs_]], sig=False)
                    P.mm(pb[bs_][:, oc], KT2_[g][:, 1, :], Vt_g, False, True, [RKT2[g], RTM3[q_]], [Rpb[bs_]], sig=False)
                    P.mm(pb[bs_][gs, 384:448], TM3_[q_][:, 0, gs], Ub_[g][:], True, False, [RTM3[q_], RUb[g]], [Rpb[bs_]], sig=False)
                    P.mm(pb[bs_][gs, 384:448], TM3_[q_][:, 1, gs], Vt_g, False, True, [RTM3[q_]], [Rpb[bs_]], sig=(g == 1))
                P.stt(Hst[:, i, :], Hst[:, i, :], nb_[q_][:, 1:2], pb[bs_][:, 384:448], ALU.mult, ALU.add, [RH[i], Rnb[q_], Rpb[bs_]], [RH[i]])
                P.cp('act', Hb[0:64, i, 0, :], Hst[0:64, i, :], [RH[i]], [RHb[i]])
                P.cp('act', Hb[64:128, i, 1, :], Hst[64:128, i, :], [RH[i]], [RHb[i]])
                for g in range(2):
                    oc = slice(256 + g * 64, 256 + (g + 1) * 64)
                    P.op('dve', lambda e, g=g, oc=oc: e.bn_stats(out=gst[:, g, :], in_=pb[bs_][:, oc]), [Rpb[bs_]], [Rgst])
                for g in range(2):
                    P.op('dve', lambda e, g=g: e.bn_aggr(out=gmv[:, g, :], in_=gst[:, g, :]), [Rgst], [Rgmv])
                P.act(grs[:], gmv[:, :, 1], AF.Sqrt, [Rgmv, Reps], [Rgrs], bias=epsb[:, 1:2], scale=1.0)
                P.op('dve', lambda e: e.reciprocal(out=grs[:], in_=grs[:]), [Rgrs], [Rgrs])
                for g in range(2):
                    oc = slice(256 + g * 64, 256 + (g + 1) * 64)
                    P.ts('dve', ynb[:, g * 64:(g + 1) * 64], pb[bs_][:, oc], gmv[:, g, 0:1], grs[:, g:g + 1], ALU.subtract, ALU.mult,
                         [Rpb[bs_], Rgmv, Rgrs], [Rynb])
                yield
                if cfg.get("mix_stop", 99) <= 4:
                    return
                bm = 4
                pmb = pb[bm].bitcast(BF16)
                P.tr(pmb[:, 768:896], ynb[:], identb[:], [Rynb, Rk], [Rpb[bm]])
                P.mm(pb[bm][:, 256:384], blkb[:], rkr_[q_][:], True, True, [Rk, Rrkr[q_]], [Rpb[bm]])
                P.ts('dve', yy_[q_][:], pmb[:, 768:896], pp[:, 82 + i:83 + i], pp[:, 90 + i:91 + i], ALU.mult, ALU.add, [Rpb[bm], Rpp], [Ryy[q_]])
                P.tt('dve', y2_[q_][:], pb[bm][:, 256:384], v_s, ALU.mult, [Rpb[bm], Rr], [Ry2[q_]])
                if cfg.get('dbgbs') and i == 7:
                    P.cp('dve', kk_[0][:], pb[bm][:, 256:384], [Rpb[bm]], [Rkk[0]])
                P.tt('pool', yy_[q_][:], yy_[q_][:], y2_[q_][:], ALU.add, [Ryy[q_], Ry2[q_]], [Ryy[q_]])
                P.tt('pool', yy_[q_][:], yy_[q_][:], gg_[q_][:], ALU.mult, [Ryy[q_], Rgg[q_]], [Ryy[q_]])
                if n == 0:
                    yield ('done', i)
                    continue
                P.tt('pool', yy_[q_][:], yy_[q_][:], grw[par][:, i, :], ALU.mult, [Ryy[q_], Rgr[par][i]], [Ryy[q_]])
                P.tt('pool', merged[:, i, :], att[:], yy_[q_][:], ALU.add, [Ratt, Ryy[q_]], [Rmg[i]])
                yield ('done', i)
                if cfg.get("mix_stop", 99) <= 6:
                    return
            if n == 0:
                return
            for hh in range(2):
                b = 6 + hh
                for i in range(8):
                    P.mm(pb[b][:], merged[:, i, :], w_out[:, i, hh * 512:(hh + 1) * 512], i == 0, i == 7, [Rmg[i], Rwo], [Rpb[b]], sig=(i == 7))
                P.stt(pre[:, hh * 512:(hh + 1) * 512], xn_dummy(hres[par], hh), 1.0, pb[b][:], ALU.mult, ALU.add, [Rhres[par], Rpb[b]], [Rpre])
            yield
            layer_norm_stats(pre, Rpre, LN_EPS)
            P.ts('dve', pre[:], pre[:], mv[:, 0:1], rstd[:, 0:1], ALU.subtract, ALU.mult, [Rpre, Rmv, Rrstd], [Rpre])
            P.dma('sp', h1_d[(n - 1) * 128:n * 128, :], pre[:], reads=[Rpre], writes=[Rscr])
            yield

        def xn_dummy(t, hh):
            return t[:, hh * 512:(hh + 1) * 512]

        Rscr = Res("scratch")

        def ffn_prep(n):
            par = n % 2
            P.dma('sp', h1t2[par][:], h1_d[(n - 1) * 128:n * 128, :], reads=[Rscr], writes=[Rh1t2[par]])
            P.tt('pool', h1t2[par][:], h1t2[par][:], lnbc[:, 0, :], ALU.mult, [Rh1t2[par], Rlnbc], [Rh1t2[par]])
            P.tt('pool', h1t2[par][:], h1t2[par][:], lnbc[:, 1, :], ALU.add, [Rh1t2[par], Rlnbc], [Rh1t2[par]])
            P.cp('dve', xnb2[par][:], h1t2[par][:], [Rh1t2[par]], [Rxnb2[par]])
            ptb = pb[0].bitcast(BF16)
            for kc in range(8):
                P.tr(ptb[:, kc * 128:(kc + 1) * 128], xnb2[par][:, kc * 128:(kc + 1) * 128], identb[:], [Rxnb2[par], Rk], [Rpb[0]], sig=(kc == 7))
            P.cp('act', hT2[par][:].rearrange("p a b -> p (a b)"), ptb[:, 0:1024], [Rpb[0]], [RhT2[par]])

        def ffn_ff1(n):
            par = n % 2
            for cg in range(8):
                b = 3 + (cg % 4)
                for s_ in range(4):
                    c = cg * 4 + s_
                    for kc in range(8):
                        P.mm(pb[b][:, s_ * 128:(s_ + 1) * 128], w_ff1[:, kc, c * 128:(c + 1) * 128], hT2[par][:, kc, :], kc == 0, kc == 7,
                             [Rwk[kc], RhT2[par]], [Rpb[b]], sig=(kc == 7 and s_ == 3))
                u = cg % 2
                P.act(urelu[u][:], pb[b][:], AF.Relu, [Rpb[b]], [Rurelu[u]])
                P.tt('pool' if cg % 2 else 'dve', uT[:, cg * 4:(cg + 1) * 4, :].rearrange("p a b -> p (a b)"), urelu[u][:], urelu[u][:], ALU.mult,
                     [Rurelu[u]], [RuT[cg]])

        def ffn_ff2(n):
            par = n % 2
            for hh in range(2):
                b = 1 + hh
                for c in range(32):
                    P.mm(pb[b][:], uT[:, c, :], w_ff2[:, c, hh * 512:(hh + 1) * 512], c == 0, c == 31, [RuT[c // 4], Rf2[c // 2]], [Rpb[b]], sig=(c == 31))
                P.stt(pre[:, hh * 512:(hh + 1) * 512], h1t2[par][:, hh * 512:(hh + 1) * 512], ALPHA, pb[b][:], ALU.mult, ALU.add,
                      [Rh1t2[par], Rpb[b]], [Rpre])
            layer_norm_stats(pre, Rpre, LN_EPS)
            P.ts('dve', pre[:], pre[:], mv[:, 0:1], rstd[:, 0:1], ALU.subtract, ALU.mult, [Rpre, Rmv, Rrstd], [Rpre])
            P.tt('pool', hres[par][:], pre[:], lnB[:, 0, :], ALU.mult, [Rpre, RlnB], [Rhres[par]])
            P.tt('pool', hres[par][:], hres[par][:], lnB[:, 1, :], ALU.add, [Rhres[par], RlnB], [Rhres[par]])
            P.dma('sp', out_d[(n - 1) * 128:n * 128, :], hres[par][:], reads=[Rhres[par]], is_output=True)

        def run2(mx, fr):
            mix_done = -1 if mx is not None else 99
            fr_need = -1
            while mx is not None or fr is not None:
                progressed = False
                if mx is not None:
                    try:
                        r = next(mx)
                        if isinstance(r, tuple) and r[0] == 'done':
                            mix_done = r[1]
                    except StopIteration:
                        mx = None
                        mix_done = 99
                    progressed = True
                if fr is not None and fr_need <= mix_done:
                    try:
                        r = next(fr)
                        if isinstance(r, tuple) and r[0] == 'need':
                            fr_need = r[1]
                    except StopIteration:
                        fr = None
                    progressed = True
                assert progressed

        def run(*gens):
            gens = [g for g in gens if g is not None]
            while gens:
                for g in list(gens):
                    try:
                        next(g)
                    except StopIteration:
                        gens.remove(g)

        run(front(0))
        for n in range(nt):
            if cfg.get('seq'):
                run(mix(n))
                if n + 1 < nt:
                    run(front(n + 1))
                continue
            run2(mix(n) if cfg.get('mix', True) else None, front(n + 1) if n + 1 < nt else None)
        dumps = {'hT': (hT[0], [128, 8, 128], BF16, RhT[0]), 'qT': (qT[0], [128, 8, 2, 128], BF16, RqT[0][7]),
                 'rkv': (rkv, [128, 8, 3, 128], F32, Rrkv[7]), 'kT0': (kT[0], [128, 128], BF16, RkT[0]), 'kT1': (kT[1], [128, 128], BF16, RkT[1]),
                 'vat0': (vat[0], [128, 128], BF16, Rvat[0]), 'lor0': (lor[0], [128, 3, 128], BF16, Rlor[0]),
                 'Hst': (Hst, [128, 8, 64], F32, RH[7]), 'Hb': (Hb, [128, 8, 2, 64], BF16, RHb[7]), 'merged': (merged, [128, 8, 128], BF16, Rmg[7]),
                 'pre': (pre, [128, D], F32, Rpre), 'hres1': (hres[1], [128, D], F32, Rhres[1]), 'hres0': (hres[0], [128, D], F32, Rhres[0]),
                 'yy': (yy_[0], [128, 128], F32, Ryy[0]), 'MT0': (MT_[0][0], [128, 128], BF16, RMT[0][0]), 'LT0': (LT_[0], [128, 2, 128], BF16, RLT[0]),
                 'sig': (sig_[0], [128, 128], F32, Rsig[0]), 'kk': (kk_[0], [128, 128], F32, Rkk[0]), 'att': (att, [128, 128], F32, Ratt),
                 'AR': (AR_[0], [128, 2, 128], BF16, RAR[0]), 'TM3': (TM3_[0], [128, 3, 128], BF16, RTM3[0]), 'PT': (PT, [128, 512], BF16, RPT),
                 'ynb': (ynb, [128, 128], BF16, Rynb), 'gg': (gg_[0], [128, 128], F32, Rgg[0]), 'gmv': (gmv, [128, 2, 2], F32, Rgmv), 'grs': (grs, [128, 2], F32, Rgrs), 'rkr': (rkr_[0], [128, 128], BF16, Rrkr[0]), 'kp': (kp_[0], [128, 128], F32, Rkp[0]), 'ga': (gat[0], [128, 8, 128], BF16, Rga[0][7]), 'gr': (grw[0], [128, 8, 128], BF16, Rgr[0][7])}
        print("nops", P.nops)
        P.maxops = None
        for name in cfg.get('dump', []):
            t_, shp, dty, R_ = dumps[name]
            dd = nc.dram_tensor("dbg_" + name, shp, dty, kind="ExternalOutput").ap()
            P.dma('sp', dd, t_[:], reads=[R_], is_output=True)
        if not cfg.get('phase2', True):
            P.finish()
            P.emit()
            return nc
        P.barrier()
        RlnB = Res("lnB")
        P.dma('sp', lnbc[:], lnbc_d[:, 2:4, :], writes=[Rlnbc])
        P.dma('sp', lnB[:], lnbc_d[:, 4:6, :], writes=[RlnB])
        for k in range(8):
            Rwk[k] = Res(f"wk2_{k}")
        wff1_v = wff1_d.rearrange("(k p) c -> p k c", p=128)
        for kc in range(8):
            for c0 in range(0, 4096, 2048):
                P.dma('pool', w_ff1[:, kc, c0:c0 + 2048], wff1_v[:, kc, c0:c0 + 2048], writes=[Rwk[kc]])
        wff2_v = wff2_d.rearrange("(c p) n -> p c n", p=128)
        for c0 in range(0, 32, 2):
            P.dma('pool', w_ff2[:, c0:c0 + 2, :], wff2_v[:, c0:c0 + 2, :], writes=[Rf2[c0 // 2]])
        prev = None
        if nt > 1:
            ffn_prep(1)
        for n in range(1, nt):
            ffn_ff1(n)
            if n + 1 < nt:
                ffn_prep(n + 1)
            ffn_ff2(n)
        P.finish()
        P.emit()
    return nc


def _paired_perm():
    idx = np.empty(1024, np.int64)
    for i in range(8):
        for g in range(2):
            idx[i * 128 + g * 64:i * 128 + (g + 1) * 64] = g * 512 + i * 64 + np.arange(64)
    return idx


def _t5_bucket(dist):
    d = np.maximum(dist, 1).astype(np.float32)
    large = 16 + (np.log(d / np.float32(16)) / np.float32(math.log(128 / 16)) * np.float32(16)).astype(np.int32)
    large = np.minimum(large, 31)
    return np.where(dist < 16, dist, large)


def _prep_shared(inp):
    f32 = np.float32
    PPm = _paired_perm()
    w_in = np.asarray(inp['w_in'], f32)[0]
    cols = [np.arange(1024, 1152), np.arange(1152, 1280), np.arange(4352, 4480), np.arange(4480, 4608)]
    for i in range(8):
        pi = PPm[i * 128:(i + 1) * 128]
        for B in (0, 4608, 5632, 1280, 2304, 3328):
            cols.append(B + pi)
    cols = np.concatenate(cols)
    assert cols.shape[0] == NCOL
    wcat = np.ascontiguousarray(w_in[:, cols])
    wout = np.ascontiguousarray(np.asarray(inp['w_out'], f32)[0][PPm, :])
    wlora = np.ascontiguousarray(np.concatenate([np.asarray(inp['decay_w2'], f32)[0][:, PPm],
                                                 np.asarray(inp['iclr_a2'], f32)[0][:, PPm]], 0))
    wg2 = np.ascontiguousarray(np.asarray(inp['gate_w2'], f32)[0][:, PPm])
    rows = [inp['ln0_g'], inp['ln0_b'], inp['ln1_g'][0], inp['ln1_b'][0], inp['ln2_g'][0], inp['ln2_b'][0]]
    lnbc = np.ascontiguousarray(np.broadcast_to(np.stack([np.asarray(r, f32) for r in rows], 0)[None], (128, 6, 1024)))
    pp = np.zeros((128, NP), f32)
    pp[:, 0:8] = np.asarray(inp['ln0_g'], f32).reshape(8, 128).T
    pp[:, 8:16] = np.asarray(inp['ln0_b'], f32).reshape(8, 128).T
    mu = np.asarray(inp['shift_mu'], f32)[0]
    pair = lambda v: np.asarray(v, f32).reshape(-1)[PPm].reshape(8, 128).T
    for j in range(3):
        pp[:, 16 + j:40:3] = pair(mu[j * 1024:(j + 1) * 1024])
    pp[:, 40] = mu[3072:3200]
    pp[:, 41] = mu[3200:3328]
    pp[:, 42:50] = pair(inp['decay_w0'][0])
    pp[:, 50:58] = pair(inp['iclr_a0'][0])
    pp[:, 58:66] = pair(inp['k_k'][0])
    pp[:, 66:74] = pair(inp['k_a'][0])
    pp[:, 74:82] = pair(np.asarray(inp['r_k'])[0].reshape(-1))
    pp[:, 82:90] = pair(inp['lnx_g'][0])
    pp[:, 90:98] = pair(inp['lnx_b'][0])
    sinks = np.asarray(inp['attn_sinks'], f32)[0]
    for i in range(8):
        pp[0:64, 98 + i] = sinks[i]
        pp[64:128, 98 + i] = sinks[8 + i]
    pp[0:112, 106] = NEG
    rb = np.asarray(inp['rel_bias'], f32)
    s = np.arange(128)[:, None]; q = np.arange(128)[None, :]
    biasT = np.empty((128, 8, 2, 2, 128), f32)
    for sh in range(2):
        dist = q + 128 - (sh * 128 + s)
        inw = (dist >= 0) & (dist < 128)
        bk = _t5_bucket(np.maximum(dist, 0))
        for i in range(8):
            for g in range(2):
                biasT[:, i, g, sh, :] = np.where(inw, rb[bk, g * 8 + i], f32(NEG))
    biasT = np.ascontiguousarray(biasT.reshape(128, 8, 512))
    cst = np.zeros((128, 6, 128), f32)
    r_ = np.arange(128)[:, None]; c_ = np.arange(128)[None, :]
    cst[:, 0] = (r_ == c_); cst[:, 1] = (c_ > r_); cst[:, 2] = (c_ >= r_); cst[:, 3] = (r_ > c_)
    cst[:, 4] = ((r_ // 64) == (c_ // 64)); cst[:, 5] = 1.0
    return dict(meta=np.ascontiguousarray(np.asarray(inp['meta_tokens'], f32)), wcat=wcat, wout=wout, wlora=wlora, wg2=wg2,
                wff1=np.ascontiguousarray(np.asarray(inp['w_ff1'], f32)[0]), wff2=np.ascontiguousarray(np.asarray(inp['w_ff2'], f32)[0]),
                lnbc=lnbc, pp=pp, biasT=biasT, cst=cst)


_NC_CACHE = {}


def kernel(**inputs):
    x = np.asarray(inputs['x'], np.float32)
    shared = _prep_shared(inputs)
    if 'nc' not in _NC_CACHE:
        _NC_CACHE['nc'] = build_nc()
    nc = _NC_CACHE['nc']
    in_maps = [dict(shared, x=np.ascontiguousarray(x[b])) for b in range(8)]
    res = run_bass_kernel_spmd(nc, in_maps, core_ids=list(range(8)))
    return np.stack([np.asarray(res.results[b]["out"], np.float32) for b in range(8)], 0)
```

```python
import math
from contextlib import ExitStack

import numpy as np
import concourse.bass as bass
import concourse.mybir as mybir
from concourse.bass_utils import run_bass_kernel_spmd

F32 = mybir.dt.float32
BF16 = mybir.dt.bfloat16
AF = mybir.ActivationFunctionType
ALU = mybir.AluOpType

ENGS = ['pe', 'act', 'dve', 'pool', 'sp']
BLOCKNAME = {'pe': 'tensor', 'act': 'scalar', 'dve': 'vector', 'pool': 'gpsimd', 'sp': 'sync'}
EPOCH = 16000
NDMA = 24

NT = 33
D = 1024
NCOL = 6656
ALPHA = 2.0 ** 0.25
LN_EPS = 1e-5
GN_EPS = 1e-5 * 64
DK = 0.6065306597126334
NEG = -30000.0
NP = 107


class Res:
    __slots__ = ('name', 'lw', 'rd')

    def __init__(self, name):
        self.name = name
        self.lw = None
        self.rd = []


class Prog:
    def __init__(self, nc, stack):
        self.nc = nc
        self.stack = stack
        self.q = {e: [] for e in ENGS}
        self.cnt = {e: 0 for e in ENGS}
        self.waited = {e: {} for e in ENGS}
        self.dma_i = 0
        self.dsem = [stack.enter_context(nc.semaphore(f"dsem{i}")) for i in range(NDMA)]
        self.esem = {e: [] for e in ENGS}
        self.out_toks = []

    def _sem(self, e, epoch):
        while len(self.esem[e]) <= epoch:
            self.esem[e].append(self.stack.enter_context(self.nc.semaphore(f"es_{e}_{len(self.esem[e])}")))
        return self.esem[e][epoch]

    def _need(self, eng, waits, tok, war=False):
        if tok is None:
            return
        if tok[0] == 'dma':
            key = ('dma', tok[1]); val = tok[2]
        else:
            peng, idx = tok
            if peng == eng:
                if war or eng == 'pe':
                    return
                if idx <= self.cnt[eng] - (1 if eng == 'dve' else 2):
                    return
            key = peng; val = idx
        if self.waited[eng].get(key, 0) >= val:
            return
        if waits.get(key, 0) < val:
            waits[key] = val

    def _deps(self, eng, reads, writes, waits):
        for r in reads:
            self._need(eng, waits, r.lw)
        for w in writes:
            self._need(eng, waits, w.lw)
            for t in w.rd:
                self._need(eng, waits, t, war=True)
        for k, v in waits.items():
            self.waited[eng][k] = v

    def _mark(self, tok, reads, writes):
        for r in reads:
            r.rd.append(tok)
        for w in writes:
            w.lw = tok
            w.rd = []

    maxops = None
    allsig = False
    nops = 0

    def op(self, eng, fn, reads=(), writes=(), sig=True):
        self.nops += 1
        if self.maxops is not None and self.nops > self.maxops:
            return None
        if self.allsig:
            sig = True
        waits = {}
        self._deps(eng, reads, writes, waits)
        idx = self.cnt[eng] + 1
        if sig:
            self.cnt[eng] = idx
        tok = (eng, idx)
        self.q[eng].append((fn, list(waits.items()), sig, None))
        self._mark(tok, reads, writes)
        return tok

    def dma(self, eng, out_ap, in_ap, reads=(), writes=(), is_output=False, **kw):
        i = self.dma_i
        self.dma_i += 1
        s = i % NDMA
        val = 16 * (i // NDMA + 1)
        waits = {}
        if val > 16:
            self._need(eng, waits, ('dma', s, val - 16))
        self._deps(eng, reads, writes, waits)
        tok = ('dma', s, val)
        fn = lambda e: e.dma_start(out=out_ap, in_=in_ap, **kw)
        self.q[eng].append((fn, list(waits.items()), False, s))
        self._mark(tok, reads, writes)
        if is_output:
            self.out_toks.append(tok)
        return tok

    def finish(self):
        waits = {}
        for t in self.out_toks:
            self._need('sp', waits, t)
        self.q['sp'].append((None, list(waits.items()), False, None))

    def barrier(self):
        snap = dict(self.cnt)
        for e in ENGS:
            waits = {}
            for o in ENGS:
                if o != e and snap[o] > 0:
                    self._need(e, waits, (o, snap[o]))
            for k, v in waits.items():
                self.waited[e][k] = v
            self.q[e].append((None, list(waits.items()), False, None))

    def emit(self):
        nc = self.nc
        for e in ENGS:
            n = self.cnt[e]
            if n:
                self._sem(e, (n - 1) // EPOCH)
        with nc.Block() as block:
            for e in ENGS:
                if self.q[e]:
                    self._emit_engine(block, e)

    def _emit_engine(self, block, e):
        q = self.q[e]
        prog = self

        def body(eng):
            cnt = 0
            for fn, waits, sig, dsem in q:
                for key, val in waits:
                    if isinstance(key, tuple):
                        eng.wait_ge(prog.dsem[key[1]], val)
                    else:
                        eng.wait_ge(prog._sem(key, (val - 1) // EPOCH), (val - 1) % EPOCH + 1)
                if fn is None:
                    continue
                ins = fn(eng)
                if dsem is not None:
                    ins.then_inc(prog.dsem[dsem], 16)
                elif sig:
                    cnt += 1
                    ins.then_inc(prog._sem(e, (cnt - 1) // EPOCH), 1)
        getattr(block, BLOCKNAME[e])(body)

    def mm(self, out, lhsT, rhs, start, stop, reads, writes, sig=True):
        return self.op('pe', lambda e: e.matmul(out, lhsT=lhsT, rhs=rhs, start=start, stop=stop), reads, writes, sig)

    def tr(self, out, in_, ident, reads, writes, sig=True):
        return self.op('pe', lambda e: e.transpose(out, in_, ident), reads, writes, sig)

    def act(self, out, in_, func, reads, writes, bias=None, scale=None):
        kw = {}
        if bias is not None:
            kw['bias'] = bias
        if scale is not None:
            kw['scale'] = scale
        return self.op('act', lambda e: e.activation(out=out, in_=in_, func=func, **kw), reads, writes)

    def tt(self, eng, out, in0, in1, op, reads, writes):
        return self.op(eng, lambda e: e.tensor_tensor(out=out, in0=in0, in1=in1, op=op), reads, writes)

    def ts(self, eng, out, in0, s1, s2, op0, op1, reads, writes):
        if op1 is None:
            return self.op(eng, lambda e: e.tensor_scalar(out=out, in0=in0, scalar1=s1, scalar2=None, op0=op0), reads, writes)
        return self.op(eng, lambda e: e.tensor_scalar(out=out, in0=in0, scalar1=s1, scalar2=s2, op0=op0, op1=op1), reads, writes)

    def stt(self, out, in0, scalar, in1, op0, op1, reads, writes):
        return self.op('dve', lambda e: e.scalar_tensor_tensor(out=out, in0=in0, scalar=scalar, in1=in1, op0=op0, op1=op1), reads, writes)

    def cp(self, eng, out, in_, reads, writes):
        if eng == 'act':
            return self.op('act', lambda e: e.copy(out=out, in_=in_), reads, writes)
        return self.op(eng, lambda e: e.tensor_copy(out=out, in_=in_), reads, writes)

    def memset(self, eng, ap, val, writes):
        return self.op(eng, lambda e: e.memset(ap, val), (), writes)


def build_nc(cfg=None):
    cfg = cfg or {}
    nt = cfg.get('nt', NT)
    nc = bass.Bass("TRN2", target_bir_lowering=False, dynamic_dma_scratch_size=4096)
    dt = nc.dram_tensor
    x_d = dt("x", [4096, D], F32, kind="ExternalInput").ap()
    meta_d = dt("meta", [16, D], F32, kind="ExternalInput").ap()
    wcat_d = dt("wcat", [D, NCOL], F32, kind="ExternalInput").ap()
    wout_d = dt("wout", [D, D], F32, kind="ExternalInput").ap()
    wlora_d = dt("wlora", [128, D], F32, kind="ExternalInput").ap()
    wg2_d = dt("wg2", [128, D], F32, kind="ExternalInput").ap()
    wff1_d = dt("wff1", [D, 4096], F32, kind="ExternalInput").ap()
    wff2_d = dt("wff2", [4096, D], F32, kind="ExternalInput").ap()
    lnbc_d = dt("lnbc", [128, 6, D], F32, kind="ExternalInput").ap()
    pp_d = dt("pp", [128, NP], F32, kind="ExternalInput").ap()
    bias_d = dt("biasT", [128, 8, 512], F32, kind="ExternalInput").ap()
    cst_d = dt("cst", [128, 6, 128], F32, kind="ExternalInput").ap()
    out_d = dt("out", [4096, D], F32, kind="ExternalOutput").ap()
    h1_d = dt("h1s", [4096, D], F32, kind="Internal").ap()

    with ExitStack() as st:
        P = Prog(nc, st)
        P.maxops = cfg.get('maxops')
        P.allsig = cfg.get('allsig', False)

        ARENA = 222000
        arena = st.enter_context(nc.sbuf_tensor("arena", [128, ARENA // 2], BF16))
        bump = [0]

        class _T:
            def __init__(self, ap):
                self.ap = ap

            def __getitem__(self, k):
                return self.ap[k]

        def SB(name, shape, dtype):
            n = 1
            for s_ in shape[1:]:
                n *= s_
            nbytes = n * (4 if dtype == F32 else 2)
            nbytes = (nbytes + 31) // 32 * 32
            o = bump[0]
            bump[0] += nbytes
            assert bump[0] <= ARENA, (name, bump[0])
            v = arena[:, o // 2:(o + nbytes) // 2]
            if dtype == F32:
                v = v.bitcast(F32)
            v = v[0:shape[0], 0:n]
            if len(shape) == 3:
                v = v.rearrange("p (a b) -> p a b", a=shape[1])
            elif len(shape) == 4:
                v = v.rearrange("p (a b c) -> p a b c", a=shape[1], b=shape[2])
            return _T(v)

        def PS(name):
            return st.enter_context(nc.psum_tensor(name, [128, 512], F32))

        def same2(t):
            return [t, t]
        wbig = SB("wbig", [128, 65536], BF16)
        Rwk = [Res(f"wk{k}") for k in range(8)]; Rwo = Res("wo"); Rf2 = [Res(f"f2_{k}") for k in range(16)]
        w_in = wbig[:, 0:8 * NCOL].rearrange("p (k c) -> p k c", k=8)
        w_out = wbig[:, 8 * NCOL:8 * NCOL + 8192].rearrange("p (k c) -> p k c", k=8)
        wlora = _T(wbig[:, 61440:62464])
        wg2 = _T(wbig[:, 62464:63488])
        w_ff1 = wbig[:, 0:32768].rearrange("p (k c) -> p k c", k=8)
        w_ff2 = wbig[:, 32768:65536].rearrange("p (k c) -> p k c", k=32)
        Rwl = Res("wl")
        lnbc = SB("lnA", [128, 2, D], F32); Rlnbc = Res("lnbc")
        pp = SB("pp_sb", [128, NP], F32); Rpp = Res("pp")
        cst = SB("cst_sb", [128, 6, 128], F32); Rcst = Res("cst")
        identb = SB("identb", [128, 128], BF16)
        onesb = SB("onesb", [128, 128], BF16)
        blkb = SB("blkb", [128, 128], BF16)
        sinkexp = SB("sinkexp", [128, 8], F32)
        epsb = SB("epsb", [128, 4], F32); Reps = Res("eps")
        ident_f = cst[:, 0, :]
        mask2 = cst[:, 1:3, :]
        mask_ls = cst[:, 3, :]
        ones_f = cst[:, 5, :]
        stt6 = SB("stt6", [128, 2, 6], F32); Rst = Res("st")
        mv = SB("mv", [128, 2], F32); Rmv = Res("mv")
        rstd = SB("rstd", [128, 1], F32); Rrstd = Res("rstd")
        xt = same2(SB("xt", [128, D], F32)); Rxt = same2(Res("xt"))
        pre = xt[0]; Rpre = Rxt[0]
        xnb = SB("xnb", [128, D], BF16); Rxnb = Res("xnb")
        hres = [SB(f"hres{j}", [128, D], F32) for j in range(2)]; Rhres = [Res(f"hres{j}") for j in range(2)]
        hT = same2(SB("hT", [128, 8, 128], BF16)); RhT = same2(Res("hT"))
        mark = bump[0]
        biasT = SB("bias_sb", [128, 8, 512], BF16); Rbias = Res("bias")
        kT = [SB(f"kT{j}", [128, 128], BF16) for j in range(3)]; RkT = [Res(f"kT{j}") for j in range(3)]
        vat = [SB(f"vat{j}", [128, 128], BF16) for j in range(3)]; Rvat = [Res(f"vat{j}") for j in range(3)]
        qT = same2(SB("qT", [128, 8, 2, 128], BF16)); RqT = same2([Res(f"qT_{i}") for i in range(8)])
        gat = same2(SB("ga", [128, 8, 128], BF16)); Rga = same2([Res(f"ga_{i}") for i in range(8)])
        grw = same2(SB("gr", [128, 8, 128], BF16)); Rgr = same2([Res(f"gr_{i}") for i in range(8)])
        rkv = SB("rkv", [128, 8, 3, 128], F32); Rrkv = [Res(f"rkv{i}") for i in range(8)]
        car = SB("car", [128, 2, 26], F32); Rcar = Res("car")
        zt = [SB(f"zt{j}", [128, 3, 129], F32) for j in range(2)]; Rzt = [Res(f"zt{j}") for j in range(2)]
        zd = [SB(f"zd{j}", [128, 3, 128], F32) for j in range(1)]; Rzd = [Res(f"zd{j}") for j in range(1)]
        zAG = SB("zAG", [128, 2, 128], F32); RzAG = Res("zAG")
        lor = [SB(f"lor{j}", [128, 3, 128], BF16) for j in range(2)]; Rlor = [Res(f"lor{j}") for j in range(2)]
        merged = SB("merged", [128, 8, 128], BF16); Rmg = [Res(f"mg{i}") for i in range(8)]

        def dbl(name, shape, dtype):
            return same2(SB(name, shape, dtype)), same2(Res(name))
        sig_, Rsig = dbl("sig", [128, 128], F32)
        aa_, Raa = dbl("aa", [128, 128], F32)
        gg_, Rgg = dbl("gg", [128, 128], F32)
        cum_, Rcum = dbl("cum", [128, 128], F32)
        epos_, Repos = dbl("epos", [128, 128], F32)
        eneg_, Reneg = dbl("eneg", [128, 128], F32)
        eprv_, Reprv = dbl("eprv", [128, 128], F32)
        eend_, Reend = dbl("eend", [128, 128], F32)
        nb_, Rnb = dbl("nb", [128, 2], F32)
        kk_, Rkk = dbl("kk", [128, 128], F32)
        kk2_, Rkk2 = dbl("kk2", [128, 128], BF16)
        rs_, Rrs = dbl("rs", [128, 128], F32)
        tmp_, Rtmp = rs_, Rrs
        ka_, Rka = dbl("ka", [128, 128], F32)
        kp_, Rkp = dbl("kp", [128, 128], F32)
        AR_, RAR = dbl("AR", [128, 2, 128], BF16)
        Bt_, RBt = dbl("Bt", [128, 128], BF16)
        Kt_, RKt = dbl("Kt", [128, 128], BF16)
        BKg_, RBKg = dbl("BKg", [128, 3, 128], BF16)
        TM3_, RTM3 = dbl("TM3", [128, 3, 128], BF16)
        rkr_, Rrkr = dbl("rkr", [128, 128], BF16)
        LT_ = [SB(f"LT{g}", [128, 2, 128], BF16) for g in range(2)]; RLT = [Res(f"LT{g}") for g in range(2)]
        KT2_ = [SB(f"KT2{g}", [128, 2, 128], BF16) for g in range(2)]; RKT2 = [Res(f"KT2{g}") for g in range(2)]
        PP_ = [[SB(f"PP{g}_{j}", [128, 2, 128], BF16) for j in range(2)] for g in range(2)]
        RPP = [[Res(f"PP{g}_{j}") for j in range(2)] for g in range(2)]
        MT_ = [[SB(f"MT{g}_{j}", [128, 128], BF16) for j in range(2)] for g in range(2)]
        RMT = [[Res(f"MT{g}_{j}") for j in range(2)] for g in range(2)]
        Wb_ = [SB(f"Wb{g}", [128, 64], BF16) for g in range(2)]; RWb = [Res(f"Wb{g}") for g in range(2)]
        Ub_ = [SB(f"Ub{g}", [128, 64], BF16) for g in range(2)]; RUb = [Res(f"Ub{g}") for g in range(2)]
        Hst = SB("Hst", [128, 8, 64], F32); Hb = SB("Hb", [128, 8, 2, 64], BF16)
        RH = [Res(f"H{i}") for i in range(8)]; RHb = [Res(f"Hb{i}") for i in range(8)]
        att = SB("att", [128, 128], F32); Ratt = Res("att")
        y2_, Ry2 = same2(rs_[0]), same2(Rrs[0])
        gst = SB("gst", [128, 2, 6], F32); Rgst = Res("gst")
        gmv = SB("gmv", [128, 2, 2], F32); Rgmv = Res("gmv")
        grs = SB("grs", [128, 2], F32); Rgrs = Res("grs")
        ynb = SB("ynb", [128, 128], BF16); Rynb = Res("ynb")
        yy_, Ryy = dbl("yy", [128, 128], F32)
        PT = SB("PTs", [128, 512], BF16); RPT = Res("PT")
        rden = rs_[0]; Rrden = Rrs[0]
        p1_end = bump[0]
        bump[0] = mark
        lnB = SB("lnB", [128, 2, D], F32)
        h1t2 = [SB(f"h1t2_{j}", [128, D], F32) for j in range(2)]; Rh1t2 = [Res(f"h1t2_{j}") for j in range(2)]
        xnb2 = [SB(f"xnb2_{j}", [128, D], BF16) for j in range(2)]; Rxnb2 = [Res(f"xnb2_{j}") for j in range(2)]
        hT2 = [SB(f"hT2_{j}", [128, 8, 128], BF16) for j in range(2)]; RhT2 = [Res(f"hT2_{j}") for j in range(2)]
        uT = SB("uT", [128, 32, 128], BF16); RuT = [Res(f"uT{j}") for j in range(8)]
        urelu = [SB(f"urelu{j}", [128, 512], F32) for j in range(2)]; Rurelu = [Res(f"urelu{j}") for j in range(2)]
        print("SBUF bytes: shared", mark, "phase1 end", p1_end, "phase2 end", bump[0])

        pb = [PS(f"pb{j}") for j in range(8)]
        Rpb = [Res(f"pb{j}") for j in range(8)]
        pb_bf = [p.bitcast(BF16) if hasattr(p, 'bitcast') else None for p in pb]

        P.dma('sp', pp[:], pp_d, writes=[Rpp])
        P.dma('sp', cst[:], cst_d, writes=[Rcst])
        P.dma('sp', lnbc[:], lnbc_d[:, 0:2, :], writes=[Rlnbc])
        for i_ in range(8):
            P.dma('pool', biasT[:, i_, :], bias_d[:, i_, :], writes=[Rbias])
        P.dma('pool', wlora[:], wlora_d, writes=[Rwl])
        P.dma('pool', wg2[:], wg2_d, writes=[Rwl])
        wcat_v = wcat_d.rearrange("(k p) c -> p k c", p=128)
        for kc in range(8):
            for c0 in range(0, NCOL, 2048):
                c1 = min(NCOL, c0 + 2048)
                P.dma('pool', w_in[:, kc, c0:c1], wcat_v[:, kc, c0:c1], writes=[Rwk[kc]])
        wout_v = wout_d.rearrange("(k p) c -> p k c", p=128)
        for kc in range(8):
            P.dma('pool', w_out[:, kc, :], wout_v[:, kc, :], writes=[Rwo])
        Rk = Res("consts")
        P.cp('dve', identb[:], ident_f, [Rcst], [Rk])
        P.cp('dve', onesb[:], ones_f, [Rcst], [Rk])
        P.cp('dve', blkb[:], cst[:, 4, :], [Rcst], [Rk])
        P.act(sinkexp[:], pp[:, 98:106], AF.Exp, [Rpp], [Rk])
        P.ts('pool', lnbc[:, 0, :], lnbc[:, 0, :], ALPHA, None, ALU.mult, None, [Rlnbc], [Rlnbc])
        P.ts('pool', lnbc[:, 1, :], lnbc[:, 1, :], ALPHA, None, ALU.mult, None, [Rlnbc], [Rlnbc])
        P.memset('pool', car[:], 0.0, [Rcar])
        P.memset('pool', xt[0][:], 0.0, [Rxt[0]])
        P.memset('dve', Hst[:], 0.0, RH)
        P.memset('dve', Hb[:], 0.0, RHb)
        P.memset('pool', qT[0][:], 0.0, RqT[0])
        P.memset('pool', lor[0][:], 0.0, [Rlor[0]])
        P.memset('pool', lor[1][:], 0.0, [Rlor[1]])

        zt_i = [0]
        zd_i = [0]

        def layer_norm_stats(src, Rsrc, eps):
            P.op('dve', lambda e: e.bn_stats(out=stt6[:, 0, :], in_=src[:, 0:512]), [Rsrc], [Rst])
            P.op('dve', lambda e: e.bn_stats(out=stt6[:, 1, :], in_=src[:, 512:1024]), [Rsrc], [Rst])
            P.op('dve', lambda e: e.bn_aggr(out=mv[:], in_=stt6[:].rearrange("p a b -> p (a b)")), [Rst], [Rmv])
            P.act(rstd[:], mv[:, 1:2], AF.Sqrt, [Rmv], [Rrstd], bias=eps_ap(eps), scale=1.0)
            P.op('dve', lambda e: e.reciprocal(out=rstd[:], in_=rstd[:]), [Rrstd], [Rrstd])

        P.memset('pool', epsb[:, 0:1], LN_EPS, [Reps])
        P.memset('pool', epsb[:, 1:2], GN_EPS, [Reps])
        P.memset('pool', epsb[:, 2:3], 1e-16, [Reps])

        def eps_ap(eps):
            return epsb[:, 0:1] if eps == LN_EPS else epsb[:, 1:2]

        def token_shift(psrc, Rpsrc, c0, nb, j0, mu_ap, dst, Rdst, n):
            k = zt_i[0] % 2; zt_i[0] += 1
            kd = 0
            z = zt[k]
            pc, pn_ = (n - 1) % 2, n % 2
            P.cp('act', z[:, 0:nb, 1:129], psrc[:, c0:c0 + nb * 128].rearrange("p (a b) -> p a b", a=nb), [Rpsrc], [Rzt[k]])
            P.cp('pool', z[:, 0:nb, 0], car[:, pc, j0:j0 + nb], [Rcar], [Rzt[k]])
            P.cp('pool', car[:, pn_, j0:j0 + nb], z[:, 0:nb, 128], [Rzt[k]], [Rcar])
            d = zd[kd]
            P.tt('dve', d[:, 0:nb, :], z[:, 0:nb, 0:128], z[:, 0:nb, 1:129], ALU.subtract, [Rzt[k]], [Rzd[kd]])
            P.tt('dve', d[:, 0:nb, :], d[:, 0:nb, :], mu_ap, ALU.mult, [Rzd[kd], Rpp], [Rzd[kd]])
            P.tt('pool', dst, d[:, 0:nb, :], z[:, 0:nb, 1:129], ALU.add, [Rzd[kd], Rzt[k]], [Rdst])

        pj_i = [0]

        def proj_bank():
            b = 1 + (pj_i[0] % 2); pj_i[0] += 1
            return b

        def front(n):
            par = n % 2
            if n >= 1:
                P.dma('sp', xt[par][:], x_d[(n - 1) * 128:n * 128, :], writes=[Rxt[par]])
            else:
                P.dma('sp', xt[0][112:128, :], meta_d, writes=[Rxt[0]])
            layer_norm_stats(xt[par], Rxt[par], LN_EPS)
            P.ts('dve', hres[par][:], xt[par][:], mv[:, 0:1], rstd[:, 0:1], ALU.subtract, ALU.mult, [Rxt[par], Rmv, Rrstd], [Rhres[par]])
            P.cp('pool', xnb[:], hres[par][:], [Rhres[par]], [Rxnb])
            if n >= 1:
                P.tt('pool', hres[par][:], hres[par][:], lnbc[:, 0, :], ALU.mult, [Rhres[par], Rlnbc], [Rhres[par]])
                P.tt('pool', hres[par][:], hres[par][:], lnbc[:, 1, :], ALU.add, [Rhres[par], Rlnbc], [Rhres[par]])
            yield
            ptb = pb[0].bitcast(BF16)
            for kc in range(8):
                P.tr(ptb[:, kc * 128:(kc + 1) * 128], xnb[:, kc * 128:(kc + 1) * 128], identb[:], [Rxnb, Rk], [Rpb[0]], sig=(kc == 7))
            for kc in range(8):
                P.act(hT[par][:, kc, :], ptb[:, kc * 128:(kc + 1) * 128], AF.Identity, [Rpb[0], Rpp], [RhT[par]],
                      bias=pp[:, 8 + kc:9 + kc], scale=pp[:, kc:kc + 1])
            if n == 0:
                P.memset('pool', hT[0][:, :, 0:112], 0.0, [RhT[0]])
            yield

            def fm_group(cols, b):
                for s_, cb in enumerate(cols):
                    for kc in range(8):
                        P.mm(pb[b][:, s_ * 128:(s_ + 1) * 128], w_in[:, kc, cb:cb + 128], hT[par][:, kc, :],
                             kc == 0, kc == 7, [Rwk[kc], RhT[par]], [Rpb[b]], sig=(kc == 7))
                    if s_ < len(cols) - 1:
                        yield

            b = proj_bank()
            for s_, cb in enumerate((0, 256, 384)):
                for kc in range(8):
                    P.mm(pb[b][:, s_ * 128:(s_ + 1) * 128], w_in[:, kc, cb:cb + 128], hT[par][:, kc, :],
                         kc == 0, kc == 7, [Rwk[kc], RhT[par]], [Rpb[b]], sig=False)
            for kc in range(8):
                P.mm(pb[b][:, 384:512], hT[par][:, kc, :], w_in[:, kc, 128:256], kc == 0, kc == 7,
                     [Rwk[kc], RhT[par]], [Rpb[b]], sig=(kc == 7))
            P.cp('act', kT[n % 3][:], pb[b][:, 0:128], [Rpb[b]], [RkT[n % 3]])
            P.cp('act', vat[n % 3][:], pb[b][:, 384:512], [Rpb[b]], [Rvat[n % 3]])
            token_shift(pb[b], Rpb[b], 128, 2, 24, pp[:, 40:42].unsqueeze(2).to_broadcast([128, 2, 128]),
                        zAG[:], RzAG, n)
            P.act(lor[par][0:64, 0, :], zAG[0:64, 0, :], AF.Tanh, [RzAG], [Rlor[par]])
            P.cp('dve', lor[par][64:128, 1, :], zAG[64:128, 0, :], [RzAG], [Rlor[par]])
            P.act(lor[par][:, 2, :], zAG[:, 1, :], AF.Sigmoid, [RzAG], [Rlor[par]])
            yield
            for i in range(8):
                yield ('need', i)
                base = 512 + i * 768
                b = proj_bank()
                yield from fm_group((base, base + 128, base + 256), b)
                P.act(qT[par][0:64, i, 0, :], pb[b][0:64, 0:128], AF.Identity, [Rpb[b]], [RqT[par][i]], scale=0.125)
                P.act(qT[par][64:128, i, 1, :], pb[b][64:128, 0:128], AF.Identity, [Rpb[b]], [RqT[par][i]], scale=0.125)
                P.act(gat[par][:, i, :], pb[b][:, 128:256], AF.Sigmoid, [Rpb[b]], [Rga[par][i]])
                P.act(grw[par][:, i, :], pb[b][:, 256:384], AF.Sigmoid, [Rpb[b]], [Rgr[par][i]])
                yield
                b = proj_bank()
                yield from fm_group((base + 384, base + 512, base + 640), b)
                token_shift(pb[b], Rpb[b], 0, 3, i * 3,
                            pp[:, 16 + i * 3:19 + i * 3].unsqueeze(2).to_broadcast([128, 3, 128]),
                            rkv[:, i, :, :], Rrkv[i], n)
                yield

        def mix(n):
            par = n % 2
            ppar = (n - 1) % 2
            for i in range(8):
                q_ = i % 2
                r_s = rkv[:, i, 0, :]; k_s = rkv[:, i, 1, :]; v_s = rkv[:, i, 2, :]
                Rr = Rrkv[i]
                b = 3
                P.mm(pb[b][:, 0:128], wlora[:, i * 128:(i + 1) * 128], lor[par][:, 0, :], True, True, [Rwl, Rlor[par]], [Rpb[b]], sig=False)
                P.mm(pb[b][:, 128:256], wlora[:, i * 128:(i + 1) * 128], lor[par][:, 1, :], True, True, [Rwl, Rlor[par]], [Rpb[b]], sig=False)
                P.mm(pb[b][:, 256:384], wg2[:, i * 128:(i + 1) * 128], lor[par][:, 2, :], True, True, [Rwl, Rlor[par]], [Rpb[b]])
                P.act(sig_[q_][:], pb[b][:, 0:128], AF.Sigmoid, [Rpb[b], Rpp], [Rsig[q_]], bias=pp[:, 42 + i:43 + i], scale=1.0)
                P.act(aa_[q_][:], pb[b][:, 128:256], AF.Sigmoid, [Rpb[b], Rpp], [Raa[q_]], bias=pp[:, 50 + i:51 + i], scale=1.0)
                P.cp('act', gg_[q_][:], pb[b][:, 256:384], [Rpb[b]], [Rgg[q_]])
                P.op('dve', lambda e, o=cum_[q_], s=sig_[q_]: e.tensor_tensor_scan(out=o[:], data0=ones_f, data1=s[:], initial=0.0, op0=ALU.mult, op1=ALU.add),
                     [Rsig[q_], Rcst], [Rcum[q_]])
                P.tt('pool', tmp_[q_][:], cum_[q_][:], sig_[q_][:], ALU.subtract, [Rcum[q_], Rsig[q_]], [Rtmp[q_]])
                P.ts('dve', nb_[q_][:, 0:1], cum_[q_][:, 127:128], -DK, None, ALU.mult, None, [Rcum[q_]], [Rnb[q_]])
                P.act(epos_[q_][:], cum_[q_][:], AF.Exp, [Rcum[q_]], [Repos[q_]], scale=-DK)
                P.act(eneg_[q_][:], cum_[q_][:], AF.Exp, [Rcum[q_]], [Reneg[q_]], scale=DK)
                P.act(eprv_[q_][:], tmp_[q_][:], AF.Exp, [Rtmp[q_]], [Reprv[q_]], scale=-DK)
                P.act(eend_[q_][:], cum_[q_][:], AF.Exp, [Rcum[q_], Rnb[q_]], [Reend[q_]], bias=nb_[q_][:, 0:1], scale=DK)
                P.act(nb_[q_][:, 1:2], cum_[q_][:, 127:128], AF.Exp, [Rcum[q_]], [Rnb[q_]], scale=-DK)
                P.ts('dve', kk_[q_][:], k_s, pp[:, 58 + i:59 + i], None, ALU.mult, None, [Rr, Rpp], [Rkk[q_]])
                P.tt('pool', kk2_[q_][:], kk_[q_][:], kk_[q_][:], ALU.mult, [Rkk[q_]], [Rkk2[q_]])
                P.mm(pb[b][:, 384:512], blkb[:], kk2_[q_][:], True, True, [Rk, Rkk2[q_]], [Rpb[b]])
                P.act(rs_[q_][:], pb[b][:, 384:512], AF.Ln, [Rpb[b], Reps], [Rrs[q_]], bias=epsb[:, 2:3], scale=1.0)
                P.act(rs_[q_][:], rs_[q_][:], AF.Exp, [Rrs[q_]], [Rrs[q_]], scale=-0.5)
                P.tt('dve', kk_[q_][:], kk_[q_][:], rs_[q_][:], ALU.mult, [Rkk[q_], Rrs[q_]], [Rkk[q_]])
                P.tt('pool', ka_[q_][:], kk_[q_][:], aa_[q_][:], ALU.mult, [Rkk[q_], Raa[q_]], [Rka[q_]])
                P.ts('dve', kp_[q_][:], aa_[q_][:], -1.0, pp[:, 66 + i:67 + i], ALU.add, ALU.mult, [Raa[q_], Rpp], [Rkp[q_]])
                P.stt(kp_[q_][:], kp_[q_][:], 1.0, k_s, ALU.add, ALU.mult, [Rkp[q_], Rr], [Rkp[q_]])
                P.stt(AR_[q_][:, 0, :], kk_[q_][:], -1.0, eprv_[q_][:], ALU.mult, ALU.mult, [Rkk[q_], Reprv[q_]], [RAR[q_]])
                P.tt('pool', AR_[q_][:, 1, :], r_s, epos_[q_][:], ALU.mult, [Rr, Repos[q_]], [RAR[q_]])
                P.tt('dve', Bt_[q_][:], ka_[q_][:], eneg_[q_][:], ALU.mult, [Rka[q_], Reneg[q_]], [RBt[q_]])
                P.tt('pool', Kt_[q_][:], kp_[q_][:], eneg_[q_][:], ALU.mult, [Rkp[q_], Reneg[q_]], [RKt[q_]])
                P.tt('dve', BKg_[q_][:, 0, :], ka_[q_][:], eend_[q_][:], ALU.mult, [Rka[q_], Reend[q_]], [RBKg[q_]])
                P.tt('pool', BKg_[q_][:, 1, :], kp_[q_][:], eend_[q_][:], ALU.mult, [Rkp[q_], Reend[q_]], [RBKg[q_]])
                P.cp('pool', BKg_[q_][:, 2, :], v_s, [Rr], [RBKg[q_]])
                P.stt(rkr_[q_][:], r_s, pp[:, 74 + i:75 + i], kp_[q_][:], ALU.mult, ALU.mult, [Rr, Rpp, Rkp[q_]], [Rrkr[q_]])
                yield
                if cfg.get("mix_stop", 99) <= 1:
                    return
                ptb = pb[0].bitcast(BF16)
                for j in range(3):
                    P.tr(ptb[:, j * 128:(j + 1) * 128], BKg_[q_][:, j, :], identb[:], [RBKg[q_], Rk], [Rpb[0]], sig=(j == 2))
                P.cp('act', TM3_[q_][:].rearrange("p a b -> p (a b)"), ptb[:, 0:384], [Rpb[0]], [RTM3[q_]])
                for g in range(2):
                    gs = slice(g * 64, (g + 1) * 64)
                    ba = 4 + g
                    bd = 6 + g
                    ARv = AR_[q_][gs, :, :].rearrange("p a b -> p (a b)")
                    P.mm(pb[ba][:, 0:256], Bt_[q_][gs, :], ARv, True, True, [RBt[q_], RAR[q_]], [Rpb[ba]], sig=False)
                    P.mm(pb[ba][:, 256:512], Kt_[q_][gs, :], ARv, True, True, [RKt[q_], RAR[q_]], [Rpb[ba]])
                    P.mm(pb[bd][:, 384:512], AR_[q_][gs, 0, :], Bt_[q_][gs, :], True, True, [RBt[q_], RAR[q_]], [Rpb[bd]])
                    P.tt('dve', LT_[g][:].rearrange("p a b -> p (a b)"), pb[ba][:, 0:256], mask2.rearrange("p a b -> p (a b)"), ALU.mult,
                         [Rpb[ba], Rcst], [RLT[g]])
                    P.tt('dve', KT2_[g][:].rearrange("p a b -> p (a b)"), pb[ba][:, 256:512], mask2.rearrange("p a b -> p (a b)"), ALU.mult,
                         [Rpb[ba], Rcst], [RKT2[g]])
                    P.tt('dve', PP_[g][0][:, 0, :], pb[bd][:, 384:512], mask_ls, ALU.mult, [Rpb[bd], Rcst], [RPP[g][0]])
                    P.cp('pool', PP_[g][0][:, 1, :], LT_[g][:, 0, :], [RLT[g]], [RPP[g][0]])
                    P.tt('pool', MT_[g][0][:], LT_[g][:, 0, :], identb[:], ALU.add, [RLT[g], Rk], [RMT[g][0]])
                yield
                if cfg.get("mix_stop", 99) <= 2:
                    return
                bl = 3
                bm = 4

                def att_qk():
                    for g in range(2):
                        for sh in range(2):
                            slot = (n - 1) % 3 if sh == 0 else n % 3
                            c0 = (g * 2 + sh) * 128
                            P.mm(pb[bl][:, c0:c0 + 128], kT[slot][:, :], qT[par][:, i, g, :], True, True, [RkT[slot], RqT[par][i]], [Rpb[bl]],
                                 sig=(g == 1 and sh == 1))
                    P.tt('dve', pb[bl][:], pb[bl][:], biasT[:, i, :], ALU.add, [Rpb[bl], Rbias], [Rpb[bl]])

                def att_exp():
                    if n == 1:
                        lbv = pb[bl][:].rearrange("p (g s q) -> p g s q", g=2, s=2)
                        PTv = PT[:].rearrange("p (g s q) -> p g s q", g=2, s=2)
                        P.act(PTv[:, :, 0, :], lbv[:, :, 0, :], AF.Exp, [Rpb[bl], Rpp], [RPT], bias=pp[:, 106:107], scale=1.0)
                        P.act(PTv[:, :, 1, :], lbv[:, :, 1, :], AF.Exp, [Rpb[bl]], [RPT])
                    else:
                        P.act(PT[:], pb[bl][:], AF.Exp, [Rpb[bl]], [RPT])

                def att_pv():
                    for g in range(2):
                        gs = slice(g * 64, (g + 1) * 64)
                        for sh in range(2):
                            slot = (n - 1) % 3 if sh == 0 else n % 3
                            c0 = (g * 2 + sh) * 128
                            P.mm(pb[bm][gs, 0:128], vat[slot][:, gs], PT[:, c0:c0 + 128], sh == 0, sh == 1, [Rvat[slot], RPT], [Rpb[bm]], sig=False)
                        for sh in range(2):
                            c0 = (g * 2 + sh) * 128
                            P.mm(pb[bm][gs, 128:256], onesb[:, 0:64], PT[:, c0:c0 + 128], sh == 0, sh == 1, [Rk, RPT], [Rpb[bm]],
                                 sig=(g == 1 and sh == 1))
                    P.ts('dve', rden[:], pb[bm][:, 128:256], sinkexp[:, i:i + 1], None, ALU.add, None, [Rpb[bm], Rk], [Rrden])

                def att_norm():
                    P.op('dve', lambda e: e.reciprocal(out=rden[:], in_=rden[:]), [Rrden], [Rrden])
                    P.tt('dve', att[:], pb[bm][:, 0:128], rden[:], ALU.mult, [Rpb[bm], Rrden], [Ratt])
                    P.tt('pool', att[:], att[:], gat[par][:, i, :], ALU.mult, [Ratt, Rga[par][i]], [Ratt])
                att_pieces = {1: att_qk, 2: att_exp, 3: att_pv, 4: att_norm} if n >= 1 else {}
                for lev in range(1, 7):
                    src_ = (lev - 1) % 2
                    dst = lev % 2
                    for g in range(2):
                        bd = 6 + g
                        Pm = PP_[g][src_][:, 0, :]; PTm = PP_[g][src_][:, 1, :]
                        last = (lev == 6)
                        P.mm(pb[bd][:, 0:128], PTm, Pm, True, True, [RPP[g][src_]], [Rpb[bd]], sig=last)
                        if not last:
                            P.mm(pb[bd][:, 128:256], Pm, PTm, True, True, [RPP[g][src_]], [Rpb[bd]])
                            P.cp('act', PP_[g][dst][:].rearrange("p a b -> p (a b)"), pb[bd][:, 0:256], [Rpb[bd]], [RPP[g][dst]])
                        else:
                            P.cp('act', PP_[g][dst][:, 0, :], pb[bd][:, 0:128], [Rpb[bd]], [RPP[g][dst]])
                        P.mm(pb[bd][:, 256:384], PP_[g][dst][:, 0, :], MT_[g][src_][:], True, True, [RPP[g][dst], RMT[g][src_]], [Rpb[bd]])
                        P.tt('dve', MT_[g][dst][:], pb[bd][:, 256:384], MT_[g][src_][:], ALU.add, [Rpb[bd], RMT[g][src_]], [RMT[g][dst]])
                    if lev in att_pieces:
                        att_pieces[lev]()
                    yield
                if cfg.get("mix_stop", 99) <= 3:
                    return
                MTf = [MT_[g][0] for g in range(2)]
                RMTf = [RMT[g][0] for g in range(2)]
                bs_ = 5
                for g in range(2):
                    gs = slice(g * 64, (g + 1) * 64)
                    Vt_g = TM3_[q_][:, 2, gs]
                    P.mm(pb[bs_][:, g * 64:(g + 1) * 64], AR_[q_][:, 0, :], Hb[:, i, g, :], True, False, [RAR[q_], RHb[i]], [Rpb[bs_]], sig=False)
                    P.mm(pb[bs_][:, g * 64:(g + 1) * 64], KT2_[g][:, 0, :], Vt_g, False, True, [RKT2[g], RTM3[q_]], [Rpb[bs_]])
                    P.cp('act', Wb_[g][:], pb[bs_][:, g * 64:(g + 1) * 64], [Rpb[bs_]], [RWb[g]])
                for g in range(2):
                    P.mm(pb[bs_][:, 128 + g * 64:128 + (g + 1) * 64], MTf[g][:], Wb_[g][:], True, True, [RMTf[g], RWb[g]], [Rpb[bs_]])
                    P.cp('dve', Ub_[g][:], pb[bs_][:, 128 + g * 64:128 + (g + 1) * 64], [Rpb[bs_]], [RUb[g]])
                for g in range(2):
                    gs = slice(g * 64, (g + 1) * 64)
                    Vt_g = TM3_[q_][:, 2, gs]
                    oc = slice(256 + g * 64, 256 + (g + 1) * 64)
                    P.mm(pb[bs_][:, oc], AR_[q_][:, 1, :], Hb[:, i, g, :], True, False, [RAR[q_], RHb[i]], [Rpb[bs_]], sig=False)
                    P.mm(pb[bs_][:, oc], LT_[g][:, 1, :], Ub_[g][:], False, False, [RLT[g], RUb[g]], [Rpb[bs_]], sig=False)
                    P.mm(pb[bs_][:, oc], KT2_[g][:, 1, :], Vt_g, False, True, [RKT2[g], RTM3[q_]], [Rpb[bs_]], sig=False)
                    P.mm(pb[bs_][gs, 384:448], TM3_[q_][:, 0, gs], Ub_[g][:], True, False, [RTM3[q_], RUb[g]], [Rpb[bs_]], sig=False)
                    P.mm(pb[bs_][gs, 384:448], TM3_[q_][:, 1, gs], Vt_g, False, True, [RTM3[q_]], [Rpb[bs_]], sig=(g == 1))
                P.stt(Hst[:, i, :], Hst[:, i, :], nb_[q_][:, 1:2], pb[bs_][:, 384:448], ALU.mult, ALU.add, [RH[i], Rnb[q_], Rpb[bs_]], [RH[i]])
                P.cp('act', Hb[0:64, i, 0, :], Hst[0:64, i, :], [RH[i]], [RHb[i]])
                P.cp('act', Hb[64:128, i, 1, :], Hst[64:128, i, :], [RH[i]], [RHb[i]])
                for g in range(2):
                    oc = slice(256 + g * 64, 256 + (g + 1) * 64)
                    P.op('dve', lambda e, g=g, oc=oc: e.bn_stats(out=gst[:, g, :], in_=pb[bs_][:, oc]), [Rpb[bs_]], [Rgst])
                for g in range(2):
                    P.op('dve', lambda e, g=g: e.bn_aggr(out=gmv[:, g, :], in_=gst[:, g, :]), [Rgst], [Rgmv])
                P.act(grs[:], gmv[:, :, 1], AF.Sqrt, [Rgmv, Reps], [Rgrs], bias=epsb[:, 1:2], scale=1.0)
                P.op('dve', lambda e: e.reciprocal(out=grs[:], in_=grs[:]), [Rgrs], [Rgrs])
                for g in range(2):
                    oc = slice(256 + g * 64, 256 + (g + 1) * 64)
                    P.ts('dve', ynb[:, g * 64:(g + 1) * 64], pb[bs_][:, oc], gmv[:, g, 0:1], grs[:, g:g + 1], ALU.subtract, ALU.mult,
                         [Rpb[bs_], Rgmv, Rgrs], [Rynb])
                yield
                if cfg.get("mix_stop", 99) <= 4:
                    return
                bm = 4
                pmb = pb[bm].bitcast(BF16)
                P.tr(pmb[:, 768:896], ynb[:], identb[:], [Rynb, Rk], [Rpb[bm]])
                P.mm(pb[bm][:, 256:384], blkb[:], rkr_[q_][:], True, True, [Rk, Rrkr[q_]], [Rpb[bm]])
                P.ts('dve', yy_[q_][:], pmb[:, 768:896], pp[:, 82 + i:83 + i], pp[:, 90 + i:91 + i], ALU.mult, ALU.add, [Rpb[bm], Rpp], [Ryy[q_]])
                P.tt('dve', y2_[q_][:], pb[bm][:, 256:384], v_s, ALU.mult, [Rpb[bm], Rr], [Ry2[q_]])
                if cfg.get('dbgbs') and i == 7:
                    P.cp('dve', kk_[0][:], pb[bm][:, 256:384], [Rpb[bm]], [Rkk[0]])
                P.tt('pool', yy_[q_][:], yy_[q_][:], y2_[q_][:], ALU.add, [Ryy[q_], Ry2[q_]], [Ryy[q_]])
                P.tt('pool', yy_[q_][:], yy_[q_][:], gg_[q_][:], ALU.mult, [Ryy[q_], Rgg[q_]], [Ryy[q_]])
                if n == 0:
                    yield ('done', i)
                    continue
                P.tt('pool', yy_[q_][:], yy_[q_][:], grw[par][:, i, :], ALU.mult, [Ryy[q_], Rgr[par][i]], [Ryy[q_]])
                P.tt('pool', merged[:, i, :], att[:], yy_[q_][:], ALU.add, [Ratt, Ryy[q_]], [Rmg[i]])
                yield ('done', i)
                if cfg.get("mix_stop", 99) <= 6:
                    return
            if n == 0:
                return
            for hh in range(2):
                b = 6 + hh
                for i in range(8):
                    P.mm(pb[b][:], merged[:, i, :], w_out[:, i, hh * 512:(hh + 1) * 512], i == 0, i == 7, [Rmg[i], Rwo], [Rpb[b]], sig=(i == 7))
                P.stt(pre[:, hh * 512:(hh + 1) * 512], xn_dummy(hres[par], hh), 1.0, pb[b][:], ALU.mult, ALU.add, [Rhres[par], Rpb[b]], [Rpre])
            yield
            layer_norm_stats(pre, Rpre, LN_EPS)
            P.ts('dve', pre[:], pre[:], mv[:, 0:1], rstd[:, 0:1], ALU.subtract, ALU.mult, [Rpre, Rmv, Rrstd], [Rpre])
            P.dma('sp', h1_d[(n - 1) * 128:n * 128, :], pre[:], reads=[Rpre], writes=[Rscr])
            yield

        def xn_dummy(t, hh):
            return t[:, hh * 512:(hh + 1) * 512]

        Rscr = Res("scratch")

        def ffn_prep(n):
            par = n % 2
            P.dma('sp', h1t2[par][:], h1_d[(n - 1) * 128:n * 128, :], reads=[Rscr], writes=[Rh1t2[par]])
            P.tt('pool', h1t2[par][:], h1t2[par][:], lnbc[:, 0, :], ALU.mult, [Rh1t2[par], Rlnbc], [Rh1t2[par]])
            P.tt('pool', h1t2[par][:], h1t2[par][:], lnbc[:, 1, :], ALU.add, [Rh1t2[par], Rlnbc], [Rh1t2[par]])
            P.cp('dve', xnb2[par][:], h1t2[par][:], [Rh1t2[par]], [Rxnb2[par]])
            ptb = pb[0].bitcast(BF16)
            for kc in range(8):
                P.tr(ptb[:, kc * 128:(kc + 1) * 128], xnb2[par][:, kc * 128:(kc + 1) * 128], identb[:], [Rxnb2[par], Rk], [Rpb[0]], sig=(kc == 7))
            P.cp('act', hT2[par][:].rearrange("p a b -> p (a b)"), ptb[:, 0:1024], [Rpb[0]], [RhT2[par]])

        def ffn_ff1(n):
            par = n % 2
            for cg in range(8):
                b = 3 + (cg % 4)
                for s_ in range(4):
                    c = cg * 4 + s_
                    for kc in range(8):
                        P.mm(pb[b][:, s_ * 128:(s_ + 1) * 128], w_ff1[:, kc, c * 128:(c + 1) * 128], hT2[par][:, kc, :], kc == 0, kc == 7,
                             [Rwk[kc], RhT2[par]], [Rpb[b]], sig=(kc == 7 and s_ == 3))
                u = cg % 2
                P.act(urelu[u][:], pb[b][:], AF.Relu, [Rpb[b]], [Rurelu[u]])
                P.tt('pool' if cg % 2 else 'dve', uT[:, cg * 4:(cg + 1) * 4, :].rearrange("p a b -> p (a b)"), urelu[u][:], urelu[u][:], ALU.mult,
                     [Rurelu[u]], [RuT[cg]])

        def ffn_ff2(n):
            par = n % 2
            for hh in range(2):
                b = 1 + hh
                for c in range(32):
                    P.mm(pb[b][:], uT[:, c, :], w_ff2[:, c, hh * 512:(hh + 1) * 512], c == 0, c == 31, [RuT[c // 4], Rf2[c // 2]], [Rpb[b]], sig=(c == 31))
                P.stt(pre[:, hh * 512:(hh + 1) * 512], h1t2[par][:, hh * 512:(hh + 1) * 512], ALPHA, pb[b][:], ALU.mult, ALU.add,
                      [Rh1t2[par], Rpb[b]], [Rpre])
            layer_norm_stats(pre, Rpre, LN_EPS)
            P.ts('dve', pre[:], pre[:], mv[:, 0:1], rstd[:, 0:1], ALU.subtract, ALU.mult, [Rpre, Rmv, Rrstd], [Rpre])
            P.tt('pool', hres[par][:], pre[:], lnB[:, 0, :], ALU.mult, [Rpre, RlnB], [Rhres[par]])
            P.tt('pool', hres[par][:], hres[par][:], lnB[:, 1, :], ALU.add, [Rhres[par], RlnB], [Rhres[par]])
            P.dma('sp', out_d[(n - 1) * 128:n * 128, :], hres[par][:], reads=[Rhres[par]], is_output=True)

        def run2(mx, fr):
            mix_done = -1 if mx is not None else 99
            fr_need = -1
            while mx is not None or fr is not None:
                progressed = False
                if mx is not None:
                    try:
                        r = next(mx)
                        if isinstance(r, tuple) and r[0] == 'done':
                            mix_done = r[1]
                    except StopIteration:
                        mx = None
                        mix_done = 99
                    progressed = True
                if fr is not None and fr_need <= mix_done:
                    try:
                        r = next(fr)
                        if isinstance(r, tuple) and r[0] == 'need':
                            fr_need = r[1]
                    except StopIteration:
                        fr = None
                    progressed = True
                assert progressed

        def run(*gens):
            gens = [g for g in gens if g is not None]
            while gens:
                for g in list(gens):
                    try:
                        next(g)
                    except StopIteration:
                        gens.remove(g)

        run(front(0))
        for n in range(nt):
            if cfg.get('seq'):
                run(mix(n))
                if n + 1 < nt:
                    run(front(n + 1))
                continue
            run2(mix(n) if cfg.get('mix', True) else None, front(n + 1) if n + 1 < nt else None)
        dumps = {'hT': (hT[0], [128, 8, 128], BF16, RhT[0]), 'qT': (qT[0], [128, 8, 2, 128], BF16, RqT[0][7]),
                 'rkv': (rkv, [128, 8, 3, 128], F32, Rrkv[7]), 'kT0': (kT[0], [128, 128], BF16, RkT[0]), 'kT1': (kT[1], [128, 128], BF16, RkT[1]),
                 'vat0': (vat[0], [128, 128], BF16, Rvat[0]), 'lor0': (lor[0], [128, 3, 128], BF16, Rlor[0]),
                 'Hst': (Hst, [128, 8, 64], F32, RH[7]), 'Hb': (Hb, [128, 8, 2, 64], BF16, RHb[7]), 'merged': (merged, [128, 8, 128], BF16, Rmg[7]),
                 'pre': (pre, [128, D], F32, Rpre), 'hres1': (hres[1], [128, D], F32, Rhres[1]), 'hres0': (hres[0], [128, D], F32, Rhres[0]),
                 'yy': (yy_[0], [128, 128], F32, Ryy[0]), 'MT0': (MT_[0][0], [128, 128], BF16, RMT[0][0]), 'LT0': (LT_[0], [128, 2, 128], BF16, RLT[0]),
                 'sig': (sig_[0], [128, 128], F32, Rsig[0]), 'kk': (kk_[0], [128, 128], F32, Rkk[0]), 'att': (att, [128, 128], F32, Ratt),
                 'AR': (AR_[0], [128, 2, 128], BF16, RAR[0]), 'TM3': (TM3_[0], [128, 3, 128], BF16, RTM3[0]), 'PT': (PT, [128, 512], BF16, RPT),
                 'ynb': (ynb, [128, 128], BF16, Rynb), 'gg': (gg_[0], [128, 128], F32, Rgg[0]), 'gmv': (gmv, [128, 2, 2], F32, Rgmv), 'grs': (grs, [128, 2], F32, Rgrs), 'rkr': (rkr_[0], [128, 128], BF16, Rrkr[0]), 'kp': (kp_[0], [128, 128], F32, Rkp[0]), 'ga': (gat[0], [128, 8, 128], BF16, Rga[0][7]), 'gr': (grw[0], [128, 8, 128], BF16, Rgr[0][7])}
        print("nops", P.nops)
        P.maxops = None
        for name in cfg.get('dump', []):
            t_, shp, dty, R_ = dumps[name]
            dd = nc.dram_tensor("dbg_" + name, shp, dty, kind="ExternalOutput").ap()
            P.dma('sp', dd, t_[:], reads=[R_], is_output=True)
        if not cfg.get('phase2', True):
            P.finish()
            P.emit()
            return nc
        P.barrier()
        RlnB = Res("lnB")
        P.dma('sp', lnbc[:], lnbc_d[:, 2:4, :], writes=[Rlnbc])
        P.dma('sp', lnB[:], lnbc_d[:, 4:6, :], writes=[RlnB])
        for k in range(8):
            Rwk[k] = Res(f"wk2_{k}")
        wff1_v = wff1_d.rearrange("(k p) c -> p k c", p=128)
        for kc in range(8):
            for c0 in range(0, 4096, 2048):
                P.dma('pool', w_ff1[:, kc, c0:c0 + 2048], wff1_v[:, kc, c0:c0 + 2048], writes=[Rwk[kc]])
        wff2_v = wff2_d.rearrange("(c p) n -> p c n", p=128)
        for c0 in range(0, 32, 2):
            P.dma('pool', w_ff2[:, c0:c0 + 2, :], wff2_v[:, c0:c0 + 2, :], writes=[Rf2[c0 // 2]])
        prev = None
        if nt > 1:
            ffn_prep(1)
        for n in range(1, nt):
            ffn_ff1(n)
            if n + 1 < nt:
                ffn_prep(n + 1)
            ffn_ff2(n)
        P.finish()
        P.emit()
    return nc


def _paired_perm():
    idx = np.empty(1024, np.int64)
    for i in range(8):
        for g in range(2):
            idx[i * 128 + g * 64:i * 128 + (g + 1) * 64] = g * 512 + i * 64 + np.arange(64)
    return idx


def _t5_bucket(dist):
    d = np.maximum(dist, 1).astype(np.float32)
    large = 16 + (np.log(d / np.float32(16)) / np.float32(math.log(128 / 16)) * np.float32(16)).astype(np.int32)
    large = np.minimum(large, 31)
    return np.where(dist < 16, dist, large)


def _prep_shared(inp):
    f32 = np.float32
    PPm = _paired_perm()
    w_in = np.asarray(inp['w_in'], f32)[0]
    cols = [np.arange(1024, 1152), np.arange(1152, 1280), np.arange(4352, 4480), np.arange(4480, 4608)]
    for i in range(8):
        pi = PPm[i * 128:(i + 1) * 128]
        for B in (0, 4608, 5632, 1280, 2304, 3328):
            cols.append(B + pi)
    cols = np.concatenate(cols)
    assert cols.shape[0] == NCOL
    wcat = np.ascontiguousarray(w_in[:, cols])
    wout = np.ascontiguousarray(np.asarray(inp['w_out'], f32)[0][PPm, :])
    wlora = np.ascontiguousarray(np.concatenate([np.asarray(inp['decay_w2'], f32)[0][:, PPm],
                                                 np.asarray(inp['iclr_a2'], f32)[0][:, PPm]], 0))
    wg2 = np.ascontiguousarray(np.asarray(inp['gate_w2'], f32)[0][:, PPm])
    rows = [inp['ln0_g'], inp['ln0_b'], inp['ln1_g'][0], inp['ln1_b'][0], inp['ln2_g'][0], inp['ln2_b'][0]]
    lnbc = np.ascontiguousarray(np.broadcast_to(np.stack([np.asarray(r, f32) for r in rows], 0)[None], (128, 6, 1024)))
    pp = np.zeros((128, NP), f32)
    pp[:, 0:8] = np.asarray(inp['ln0_g'], f32).reshape(8, 128).T
    pp[:, 8:16] = np.asarray(inp['ln0_b'], f32).reshape(8, 128).T
    mu = np.asarray(inp['shift_mu'], f32)[0]
    pair = lambda v: np.asarray(v, f32).reshape(-1)[PPm].reshape(8, 128).T
    for j in range(3):
        pp[:, 16 + j:40:3] = pair(mu[j * 1024:(j + 1) * 1024])
    pp[:, 40] = mu[3072:3200]
    pp[:, 41] = mu[3200:3328]
    pp[:, 42:50] = pair(inp['decay_w0'][0])
    pp[:, 50:58] = pair(inp['iclr_a0'][0])
    pp[:, 58:66] = pair(inp['k_k'][0])
    pp[:, 66:74] = pair(inp['k_a'][0])
    pp[:, 74:82] = pair(np.asarray(inp['r_k'])[0].reshape(-1))
    pp[:, 82:90] = pair(inp['lnx_g'][0])
    pp[:, 90:98] = pair(inp['lnx_b'][0])
    sinks = np.asarray(inp['attn_sinks'], f32)[0]
    for i in range(8):
        pp[0:64, 98 + i] = sinks[i]
        pp[64:128, 98 + i] = sinks[8 + i]
    pp[0:112, 106] = NEG
    rb = np.asarray(inp['rel_bias'], f32)
    s = np.arange(128)[:, None]; q = np.arange(128)[None, :]
    biasT = np.empty((128, 8, 2, 2, 128), f32)
    for sh in range(2):
        dist = q + 128 - (sh * 128 + s)
        inw = (dist >= 0) & (dist < 128)
        bk = _t5_bucket(np.maximum(dist, 0))
        for i in range(8):
            for g in range(2):
                biasT[:, i, g, sh, :] = np.where(inw, rb[bk, g * 8 + i], f32(NEG))
    biasT = np.ascontiguousarray(biasT.reshape(128, 8, 512))
    cst = np.zeros((128, 6, 128), f32)
    r_ = np.arange(128)[:, None]; c_ = np.arange(128)[None, :]
    cst[:, 0] = (r_ == c_); cst[:, 1] = (c_ > r_); cst[:, 2] = (c_ >= r_); cst[:, 3] = (r_ > c_)
    cst[:, 4] = ((r_ // 64) == (c_ // 64)); cst[:, 5] = 1.0
    return dict(meta=np.ascontiguousarray(np.asarray(inp['meta_tokens'], f32)), wcat=wcat, wout=wout, wlora=wlora, wg2=wg2,
                wff1=np.ascontiguousarray(np.asarray(inp['w_ff1'], f32)[0]), wff2=np.ascontiguousarray(np.asarray(inp['w_ff2'], f32)[0]),
                lnbc=lnbc, pp=pp, biasT=biasT, cst=cst)


_NC_CACHE = {}


def kernel(**inputs):
    x = np.asarray(inputs['x'], np.float32)
    shared = _prep_shared(inputs)
    if 'nc' not in _NC_CACHE:
        _NC_CACHE['nc'] = build_nc()
    nc = _NC_CACHE['nc']
    in_maps = [dict(shared, x=np.ascontiguousarray(x[b])) for b in range(8)]
    res = run_bass_kernel_spmd(nc, in_maps, core_ids=list(range(8)))
    return np.stack([np.asarray(res.results[b]["out"], np.float32) for b in range(8)], 0)
```
